# Optimizing a Trainium2 kernel written in Bass

```python
import numpy as np
import jax, jax.numpy as jnp
from jax import lax

D_MODEL = 2048
BATCH = 2
SEQ = 4096
DEPTH = 4

D_MIX = D_MODEL
GLA_HEADS = 4
GLA_DK = 64
GLA_DV = 128
GLA_GATE_RANK = 16
GLA_GATE_NORMALIZER = 16.0
HGRN_HEADS = 4
HGRN_DK = 128
HGRN_DV = 128
MLA_HEADS = 8
MLA_Q_RANK = 512
MLA_KV_RANK = 512
MLA_NOPE = 128
MLA_ROPE = 64
MLA_DV = 128
ROPE_THETA = 10000.0
D_FF = -(-8 * D_MODEL // (3 * 256)) * 256
CHUNK = 64
Q_BLOCK = 128
EPS = 1e-6

IN_SIZES = (
    GLA_HEADS * GLA_DK,
    GLA_HEADS * GLA_DK,
    GLA_HEADS * GLA_DV,
    GLA_GATE_RANK,
    GLA_HEADS * GLA_DV,
    HGRN_HEADS * HGRN_DK,
    HGRN_HEADS * HGRN_DK,
    HGRN_HEADS * HGRN_DV,
    HGRN_HEADS * HGRN_DV,
    MLA_Q_RANK,
    MLA_KV_RANK,
    MLA_ROPE,
)
D_IN = sum(IN_SIZES)

kernel_name = "hybrid_gla_hgrn2_mla_sandwich_trunk"


def rms_norm(x, gain):
    xf = x.astype(jnp.float32)
    y = xf * lax.rsqrt(jnp.mean(xf * xf, axis=-1, keepdims=True) + EPS)
    return (y * gain.astype(jnp.float32)).astype(x.dtype)


def gated_head_norm(o, gate, gain):
    B, S, H, dv = o.shape
    y = rms_norm(o, gain) * jax.nn.silu(gate.reshape(B, S, H, dv))
    return y.reshape(B, S, H * dv)


def heads(t, n):
    B, S, _ = t.shape
    return t.reshape(B, S, n, -1).transpose(0, 2, 1, 3)


def chunk_gated_linear_attention(q, k, v, log_g, scale):
    B, H, T, dk = q.shape
    dv = v.shape[-1]
    n = T // CHUNK

    def to_chunks(t):
        return t.astype(jnp.float32).reshape(B, H, n, CHUNK, t.shape[-1]).transpose(2, 0, 1, 3, 4)

    qc, kc, vc, gc = to_chunks(q) * scale, to_chunks(k), to_chunks(v), to_chunks(log_g)
    causal = jnp.tril(jnp.ones((CHUNK, CHUNK), dtype=bool))

    def step(S, inp):
        qi, ki, vi, gi = inp
        b = jnp.cumsum(gi, axis=-2)
        diff = b[..., :, None, :] - b[..., None, :, :]
        decay = jnp.exp(jnp.where(causal[:, :, None], diff, -jnp.inf))
        A = jnp.einsum('bhid,bhijd,bhjd->bhij', qi, decay, ki)
        o = A @ vi + jnp.einsum('bhid,bhde->bhie', qi * jnp.exp(b), S)
        b_last = b[..., -1:, :]
        S = jnp.exp(b_last[..., 0, :])[..., None] * S + jnp.einsum(
            'bhjd,bhje->bhde', ki * jnp.exp(b_last - b), vi)
        return S, o

    S0 = jnp.zeros((B, H, dk, dv), jnp.float32)
    _, o = lax.scan(step, S0, (qc, kc, vc, gc))
    return o.transpose(1, 2, 0, 3, 4).reshape(B, H, T, dv)


def apply_rope(x, cos, sin):
    xf = x.astype(jnp.float32)
    x1, x2 = jnp.split(xf, 2, axis=-1)
    y = jnp.concatenate([x1 * cos - x2 * sin, x2 * cos + x1 * sin], axis=-1)
    return y.astype(x.dtype)


def causal_mla_attention(q_nope, q_pe, k_nope, k_pe, v):
    B, H, T, dn = q_nope.shape
    dv = v.shape[-1]
    nb = T // Q_BLOCK
    scale = (MLA_NOPE + MLA_ROPE) ** -0.5
    qn = q_nope.reshape(B, H, nb, Q_BLOCK, dn).transpose(2, 0, 1, 3, 4)
    qp = q_pe.reshape(B, H, nb, Q_BLOCK, MLA_ROPE).transpose(2, 0, 1, 3, 4)
    starts = jnp.arange(nb, dtype=jnp.int32) * Q_BLOCK
    key_idx = jnp.arange(T, dtype=jnp.int32)

    def block(args):
        qn_b, qp_b, start = args
        s = (jnp.einsum('bhqd,bhkd->bhqk', qn_b, k_nope)
             + jnp.einsum('bhqr,bkr->bhqk', qp_b, k_pe)).astype(jnp.float32) * scale
        q_idx = start + jnp.arange(Q_BLOCK, dtype=jnp.int32)
        s = jnp.where(key_idx[None, :] <= q_idx[:, None], s, -jnp.inf)
        p = jax.nn.softmax(s, axis=-1).astype(v.dtype)
        return jnp.einsum('bhqk,bhkd->bhqd', p, v)

    o = lax.map(block, (qn, qp, starts))
    return o.transpose(1, 0, 3, 2, 4).reshape(B, T, H, dv)


def hybrid_mixer(h, cos, sin, lb, w_in, gla_gate_w2, gla_gate_b, gla_out_norm, hgrn_out_norm,
                 mla_q_norm, mla_wq_b, mla_kv_norm, mla_wkv_b, mla_out_norm, w_out):
    B, S, _ = h.shape
    proj = h @ w_in
    split_at = [int(c) for c in np.cumsum(IN_SIZES)[:-1]]
    (gq, gk, gv, g_low, g_out, hq, hf, hi, h_out, qc, kvc, kpe) = jnp.split(proj, split_at, axis=-1)

    log_a = jax.nn.log_sigmoid((g_low @ gla_gate_w2 + gla_gate_b).astype(jnp.float32)) / GLA_GATE_NORMALIZER
    o_gla = chunk_gated_linear_attention(heads(gq, GLA_HEADS), heads(gk, GLA_HEADS), heads(gv, GLA_HEADS),
                                         heads(log_a, GLA_HEADS), GLA_DK ** -0.5)
    y_gla = gated_head_norm(o_gla.transpose(0, 2, 1, 3).astype(h.dtype), g_out, gla_out_norm)

    log_f = jnp.logaddexp(jnp.log(lb), jnp.log1p(-lb) + jax.nn.log_sigmoid(hf.astype(jnp.float32)))
    k_h = 1.0 - jnp.exp(log_f)
    o_hg = chunk_gated_linear_attention(heads(jax.nn.silu(hq), HGRN_HEADS), heads(k_h, HGRN_HEADS),
                                        heads(hi, HGRN_HEADS), heads(log_f, HGRN_HEADS), 1.0)
    y_hg = gated_head_norm(o_hg.transpose(0, 2, 1, 3).astype(h.dtype), h_out, hgrn_out_norm)

    q = (rms_norm(qc, mla_q_norm) @ mla_wq_b).reshape(B, S, MLA_HEADS, MLA_NOPE + MLA_ROPE)
    q_nope, q_pe = q[..., :MLA_NOPE], apply_rope(q[..., MLA_NOPE:], cos[:, :, None, :], sin[:, :, None, :])
    kv = (rms_norm(kvc, mla_kv_norm) @ mla_wkv_b).reshape(B, S, MLA_HEADS, MLA_NOPE + MLA_DV)
    k_nope, v = kv[..., :MLA_NOPE], kv[..., MLA_NOPE:]
    k_pe = apply_rope(kpe, cos, sin)
    o_mla = causal_mla_attention(q_nope.transpose(0, 2, 1, 3), q_pe.transpose(0, 2, 1, 3),
                                 k_nope.transpose(0, 2, 1, 3), k_pe, v.transpose(0, 2, 1, 3))
    y_mla = rms_norm(o_mla.reshape(B, S, MLA_HEADS * MLA_DV), mla_out_norm)

    return jnp.concatenate([y_gla, y_hg, y_mla], axis=-1) @ w_out


def setup_inputs(seed: int = 0) -> dict:
    key = jax.random.key(seed)
    ks = jax.random.split(key, 24)
    L = DEPTH

    def w(k, shape, fan_in):
        return jax.random.normal(k, shape, jnp.float32) * fan_in ** -0.5

    def gain(k, shape):
        return 1.0 + 0.02 * jax.random.normal(k, shape, jnp.float32)

    return {
        "x": jax.random.normal(ks[0], (BATCH, SEQ, D_MODEL), jnp.float32),
        "positions": jnp.broadcast_to(jnp.arange(SEQ, dtype=jnp.int32), (BATCH, SEQ)),
        "attn_pre_norm": gain(ks[1], (L, D_MODEL)),
        "w_in": w(ks[2], (L, D_MODEL, D_IN), D_MODEL),
        "gla_gate_w2": w(ks[3], (L, GLA_GATE_RANK, GLA_HEADS * GLA_DK), GLA_GATE_RANK),
        "gla_gate_b": 0.1 * jax.random.normal(ks[4], (L, GLA_HEADS * GLA_DK), jnp.float32),
        "gla_out_norm": gain(ks[5], (L, GLA_DV)),
        "hgrn_lb_logits": 0.5 * jax.random.normal(ks[6], (L, HGRN_HEADS * HGRN_DK), jnp.float32),
        "hgrn_out_norm": gain(ks[7], (L, HGRN_DV)),
        "mla_q_norm": gain(ks[8], (L, MLA_Q_RANK)),
        "mla_wq_b": w(ks[9], (L, MLA_Q_RANK, MLA_HEADS * (MLA_NOPE + MLA_ROPE)), MLA_Q_RANK),
        "mla_kv_norm": gain(ks[10], (L, MLA_KV_RANK)),
        "mla_wkv_b": w(ks[11], (L, MLA_KV_RANK, MLA_HEADS * (MLA_NOPE + MLA_DV)), MLA_KV_RANK),
        "mla_out_norm": gain(ks[12], (L, MLA_HEADS * MLA_DV)),
        "w_out": w(ks[13], (L, D_MIX, D_MODEL), D_MIX),
        "attn_post_norm": gain(ks[14], (L, D_MODEL)),
        "ffn_pre_norm": gain(ks[15], (L, D_MODEL)),
        "w_gate": w(ks[16], (L, D_MODEL, D_FF), D_MODEL),
        "w_up": w(ks[17], (L, D_MODEL, D_FF), D_MODEL),
        "w_down": w(ks[18], (L, D_FF, D_MODEL), D_FF),
        "ffn_post_norm": gain(ks[19], (L, D_MODEL)),
    }


def reference(x, positions, attn_pre_norm, w_in, gla_gate_w2, gla_gate_b, gla_out_norm,
              hgrn_lb_logits, hgrn_out_norm, mla_q_norm, mla_wq_b, mla_kv_norm, mla_wkv_b,
              mla_out_norm, w_out, attn_post_norm, ffn_pre_norm, w_gate, w_up, w_down, ffn_post_norm):
    inv_freq = ROPE_THETA ** (-jnp.arange(0, MLA_ROPE, 2, dtype=jnp.float32) / MLA_ROPE)
    ang = positions.astype(jnp.float32)[..., None] * inv_freq
    cos, sin = jnp.cos(ang), jnp.sin(ang)
    cs = jnp.cumsum(jax.nn.softmax(hgrn_lb_logits.astype(jnp.float32), axis=0), axis=0)
    lower_bounds = cs - cs[0]

    for l in range(DEPTH):
        h = rms_norm(x, attn_pre_norm[l])
        m = hybrid_mixer(h, cos, sin, lower_bounds[l], w_in[l], gla_gate_w2[l], gla_gate_b[l],
                         gla_out_norm[l], hgrn_out_norm[l], mla_q_norm[l], mla_wq_b[l],
                         mla_kv_norm[l], mla_wkv_b[l], mla_out_norm[l], w_out[l])
        x = x + rms_norm(m, attn_post_norm[l])
        u = rms_norm(x, ffn_pre_norm[l])
        f = (jax.nn.silu(u @ w_gate[l]) * (u @ w_up[l])) @ w_down[l]
        x = x + rms_norm(f, ffn_post_norm[l])
    return x
```

```python
import contextlib
import os
import numpy as np
import concourse.bass as bass
import concourse.mybir as mybir
from concourse.bass_utils import run_bass_kernel_spmd

F32 = mybir.dt.float32
BF16 = mybir.dt.bfloat16
I32 = mybir.dt.int32
AF = mybir.ActivationFunctionType
ALU = mybir.AluOpType

D = 2048
DIN = 4688
DFF = 5632
TT = 512
EPS = 1e-6
O_GQ, O_GK, O_GV, O_GLOW, O_GOUT, O_HQ, O_HF, O_HI, O_HOUT, O_QC, O_KVC, O_KPE = (
    0, 256, 512, 1024, 1040, 1552, 2064, 2576, 3088, 3600, 4112, 4624)
NPL = 88
MAGIC = 12582912.0
NCORES = 8


class Res:
    __slots__ = ("n", "w", "r")

    def __init__(self, n):
        self.n = n
        self.w = {}
        self.r = {}


def _merge(d, k, v):
    if d.get(k, 0) < v:
        d[k] = v


class Sched:
    def __init__(self, nc, es):
        self.nc = nc
        self.E = {"pe": nc.tensor, "act": nc.scalar, "dve": nc.vector, "pool": nc.gpsimd, "sp": nc.sync}
        self.sem = {}
        self.cnt = {}
        for e in ("pe", "act", "dve", "pool"):
            self.sem[e] = es.enter_context(nc.semaphore("s_" + e))
            self.cnt[e] = 0
        self.dslots = {"sp": 12, "pool": 8}
        self.dnext = {"sp": 0, "pool": 0}
        for q, n in self.dslots.items():
            for i in range(n):
                self.sem[(q, i)] = es.enter_context(nc.semaphore("d_%s%d" % (q, i)))
                self.cnt[(q, i)] = 0
        self.waited = {e: {} for e in self.E}
        self.nwaits = 0
        self.nins = 0

    def _deps(self, reads, writes):
        deps = {}
        for r in reads:
            for k, v in r.w.items():
                _merge(deps, k, v)
        for w in writes:
            for k, v in w.w.items():
                _merge(deps, k, v)
            for k, v in w.r.items():
                _merge(deps, k, v)
        return deps

    def _wait(self, e, deps):
        wd = self.waited[e]
        for k, v in deps.items():
            if k == "pe" and e == "pe":
                continue
            if wd.get(k, 0) >= v:
                continue
            self.E[e].wait_ge(self.sem[k], v)
            wd[k] = v
            self.nwaits += 1

    def op(self, e, fn, reads=(), writes=()):
        self._wait(e, self._deps(reads, writes))
        ins = fn(self.E[e])
        self.cnt[e] += 1
        c = self.cnt[e]
        ins.then_inc(self.sem[e], 1)
        self.nins += 1
        for r in reads:
            _merge(r.r, e, c)
        for w in writes:
            w.w = {e: c}
            w.r = {}

    def dma(self, q, out, in_, reads=(), writes=()):
        i = self.dnext[q]
        self.dnext[q] = (i + 1) % self.dslots[q]
        key = (q, i)
        deps = self._deps(reads, writes)
        if self.cnt[key] > 0:
            _merge(deps, key, 16 * self.cnt[key])
        self._wait(q, deps)
        ins = self.E[q].dma_start(out=out, in_=in_)
        self.cnt[key] += 1
        v = 16 * self.cnt[key]
        ins.then_inc(self.sem[key], 16)
        self.nins += 1
        for r in reads:
            _merge(r.r, key, v)
        for w in writes:
            w.w = {key: v}
            w.r = {}

    def fence(self, old, new):
        d = {}
        for o in old:
            for k, v in o.w.items():
                _merge(d, k, v)
            for k, v in o.r.items():
                _merge(d, k, v)
        for n in new:
            n.w = dict(d)
            n.r = {}

    def fence_merge(self, old, new):
        d = {}
        for o in old:
            for k, v in o.w.items():
                _merge(d, k, v)
            for k, v in o.r.items():
                _merge(d, k, v)
        for n in new:
            for k, v in d.items():
                _merge(n.w, k, v)

    def drain(self, e="sp"):
        deps = {}
        for k, c in self.cnt.items():
            if c > 0:
                deps[k] = c * 16 if isinstance(k, tuple) else c
        self._wait(e, deps)


def build_program(T, L, dbg=False):
    NT = T // TT
    nc = bass.Bass("TRN2", target_bir_lowering=False)
    NPV = L * NPL + 2

    def din(name, shape, dt=F32):
        return nc.dram_tensor(name, list(shape), dt, kind="ExternalInput").ap()

    xT_in = din("xT", [D, T])
    pos_in = din("pos", [1, T], I32)
    pvec_in = din("pvec", [128, NPV])
    consts_in = din("consts", [128, 5 * 128])
    w_in = din("w_in", [L, D, DIN])
    w2_in = din("gla_gate_w2", [L, 16, 256])
    wq_in = din("mla_wq_b", [L, 512, 1536])
    wkv_in = din("mla_wkv_b", [L, 512, 2048])
    wo_in = din("w_out", [L, D, D])
    wg_in = din("w_gate", [L, D, DFF])
    wu_in = din("w_up", [L, D, DFF])
    wd_in = din("w_down", [L, DFF, D])
    yT_out = nc.dram_tensor("yT", [D, T], F32, kind="ExternalOutput").ap()

    xmid_d = nc.dram_tensor("xmid_d", [D, T], F32).ap()
    xres_d = nc.dram_tensor("xres_d", [D, T], F32).ap()
    kn_d = nc.dram_tensor("kn_d", [8, 128, T], BF16).ap()
    kpe_d = nc.dram_tensor("kpe_d", [64, T], BF16).ap()
    vd_d = nc.dram_tensor("vd_d", [8, NT, 128, 4, 128], BF16).ap()
    cc_d = nc.dram_tensor("cc_d", [64, T], F32).ap()
    ss_d = nc.dram_tensor("ss_d", [64, T], F32).ap()
    NUNIT = 80
    wcache = nc.dram_tensor("wcache", [NUNIT, 128, 6144], BF16).ap()

    es = contextlib.ExitStack()
    with es:
        S = Sched(nc, es)

        def sb(name, shape, dt):
            return es.enter_context(nc.sbuf_tensor(name, list(shape), dt))

        HY = sb("HY", [128, 16, TT], BF16)
        WB = [sb("WB%d" % i, [128, 6144], BF16) for i in range(4)]
        WSM = sb("WSM", [128, 16, 144], BF16)
        XS = sb("XS", [128, 4, TT], F32)
        SQ = sb("SQ", [128, 3, TT], BF16)
        RS = sb("RS", [128, 3, TT], F32)
        SG = sb("SG", [128, 6, 128], F32)
        SBs = sb("SBs", [128, 6, 128], BF16)
        CS = sb("CS", [128, 2, TT], F32)
        CONF = sb("CONF", [128, 5 * 128], F32)
        IDENT = sb("IDENT", [128, 128], BF16)
        ONESB = sb("ONESB", [128, 128], BF16)
        MASKC = sb("MASKC", [128, 128], BF16)
        PERM = sb("PERM", [128, 64], BF16)
        ONEF = sb("ONEF", [128, TT], F32)
        PV = sb("PV", [128, NPV], F32)
        NEGB = sb("NEGB", [128, L, 2], F32)
        LBt = sb("LBt", [128, 4, L], F32)
        OMLt = sb("OMLt", [128, 4, L], F32)
        SMT = sb("SMT", [128, 4, 8], F32)
        W2 = sb("W2", [16, L, 256], BF16)
        GLOW = sb("GLOW", [16, TT], BF16)
        DEC = sb("DEC", [128, 6, 8], F32)
        WG = sb("WG", [128, 6, 8], F32)
        SI = sb("SI", [128, 6, 8], F32)
        CT = sb("CT", [128, 6, 8], F32)
        RA = sb("RA", [128, 12288], F32)
        RB = sb("RB", [128, 12288], F32)

        def view(reg, off, words, dt, pat=None, **kw):
            a = reg[:, off:off + words]
            if dt == BF16:
                a = a.bitcast(BF16)
            if pat:
                a = a.rearrange(pat, **kw)
            return a

        QF = view(RA, 0, 1536, BF16, "p (a t) -> p a t", a=6)
        KF = view(RA, 1536, 1536, BF16, "p (a t) -> p a t", a=6)
        BFv = view(RA, 3072, 3072, F32, "p (a t) -> p a t", a=6)
        VG = view(RA, 6144, 1024, BF16, "p (a t) -> p a t", a=4)
        VH = view(RA, 7168, 1024, BF16, "p (a t) -> p a t", a=4)
        QN = view(RA, 8192, 1024, BF16, "p (a t) -> p a t", a=4)
        CN = view(RA, 9216, 1024, BF16, "p (a t) -> p a t", a=4)
        GS = view(RA, 10240, 2048, BF16, "p (a t) -> p a t", a=8)
        MT = view(RA, 0, 8192, F32, "p (a t) -> p a t", a=16)
        HID = view(RA, 0, 11264, BF16, "p (a t) -> p a t", a=44)
        QT = view(RB, 0, 1536, BF16, "p (a t) -> p a t", a=6)
        KT = view(RB, 1536, 1536, BF16, "p (a t) -> p a t", a=6)
        KTT = view(RB, 3072, 1536, BF16, "p (a s d) -> p a s d", a=6, s=4)
        ET = view(RB, 4608, 1536, F32, "p (a t) -> p a t", a=3)
        OT = view(RB, 6144, 4096, F32, "p (a t) -> p a t", a=8)
        ATMv = view(RB, 10240, 512, BF16, "p (a t) -> p a t", a=8)
        QNOPE = view(RB, 0, 2048, BF16, "p (a t) -> p a t", a=8)
        QPE = view(RB, 2048, 2048, BF16, "p (a t) -> p a t", a=8)
        KS = view(RB, 4096, 512, BF16, "p (a t) -> p a t", a=2)
        KPS = view(RB, 4608, 512, BF16, "p (a t) -> p a t", a=2)
        VS = view(RB, 5120, 512, BF16, "p (a b e) -> p a b e", a=2, b=4)
        PT = view(RB, 5632, 1024, BF16, "p (a t) -> p a t", a=4)
        KSL = [KS[:, 0, :], KS[:, 1, :]]
        KPSL = [KPS[:, 0, :], KPS[:, 1, :]]
        VSL = [VS[:, 0, :, :], VS[:, 1, :, :]]
        for _k in range(2):
            _b = 6656 + _k * 768
            KSL.append(view(RB, _b, 256, BF16))
            KPSL.append(view(RB, _b + 256, 256, BF16))
            VSL.append(view(RB, _b + 512, 256, BF16, "p (b e) -> p b e", b=4))
        KTO = view(RB, 6656, 512, BF16, "p (a t) -> p a t", a=2)
        VTO = view(RB, 7168, 1024, BF16, "p (a h e) -> p a h e", a=2, h=8)
        OMLA = view(RB, 8192, 4096, F32, "p (a t) -> p a t", a=8)
        FT = view(RB, 0, 8192, F32, "p (a t) -> p a t", a=16)
        XB = view(RB, 0, 8192, F32, "p (a t) -> p a t", a=16)

        PA = [es.enter_context(nc.psum_tensor("PA%d" % i, [128, TT], F32)) for i in range(6)]
        PSTAT = es.enter_context(nc.psum_tensor("PSTAT", [128, TT], F32))
        PTR = es.enter_context(nc.psum_tensor("PTR", [128, 8, 128], BF16))

        def R(n):
            return Res(n)

        def RL(n, k):
            return [Res("%s%d" % (n, i)) for i in range(k)]

        r_PA = RL("PA", 6)
        r_PSTAT = R("PSTAT")
        r_PTR = RL("PTR", 8)
        r_WB = RL("WB", 4)
        r_WSM = R("WSM")
        r_XS = RL("XS", 4)
        r_SQ = RL("SQ", 3)
        r_RS = RL("RS", 3)
        r_SG = RL("SG", 6)
        r_SBs = RL("SBs", 6)
        r_CS = R("CS")
        r_const = R("const")
        r_PV = R("PV")
        r_GLOW = R("GLOW")
        r_CH = R("CH")
        r_h = RL("h", 16)
        r_y = RL("y", 16)
        r_u = RL("u", 16)
        r_QF, r_KF, r_BF = RL("QF", 6), RL("KF", 6), RL("BF", 6)
        r_VG, r_VH = RL("VG", 4), RL("VH", 4)
        r_QN, r_CN = RL("QN", 4), RL("CN", 4)
        r_GS = RL("GS", 8)
        r_MT = RL("MT", 16)
        r_HID = RL("HID", 44)
        r_QT, r_KT = RL("QT", 6), RL("KT", 6)
        r_KTT = [RL("KTT%d_" % a, 4) for a in range(6)]
        r_ET = RL("ET", 3)
        r_OT = RL("OT", 8)
        r_ATM = RL("ATM", 8)
        r_QNOPE, r_QPE = RL("QNOPE", 8), RL("QPE", 8)
        r_KS, r_KPS, r_VS = RL("KS", 4), RL("KPS", 4), RL("VS", 4)
        r_PT = RL("PT", 4)
        r_KTO, r_VTO = RL("KTO", 2), RL("VTO", 2)
        r_OMLA = RL("OMLA", 8)
        r_FT = RL("FT", 16)
        r_XB = R("XB")
        RA1 = r_QF + r_KF + r_BF + r_VG + r_VH + r_QN + r_CN + r_GS
        RA4 = r_MT
        RA5 = r_HID
        RB2 = r_QT + r_KT + sum(r_KTT, []) + r_ET + r_OT + r_ATM
        RB3 = r_QNOPE + r_QPE + r_KS + r_KPS + r_VS + r_PT + r_KTO + r_VTO + r_OMLA
        RB5 = r_FT
        r_xmid = RL("xmid", NT)
        r_xres = RL("xres", NT)
        r_kv = RL("kv", NT)
        r_ccss = RL("ccss", NT)
        r_cache = RL("wcache", 80)

        rot = {"pa": 0, "ptr": 0, "xs": 0, "sq": 0, "rs": 0, "wb": 0, "et": 0, "pt": 0, "atm": 0, "ks": 0, "pa4": 0, "pa3": 0}

        def nxt(name, n):
            i = rot[name]
            rot[name] = (i + 1) % n
            return i

        def next_pa():
            i = nxt("pa", 6)
            return PA[i], r_PA[i]

        def next_wb():
            i = nxt("wb", 4)
            return WB[i], r_WB[i]

        def next_xs():
            i = nxt("xs", 4)
            return XS[:, i, :], r_XS[i]

        def next_sq():
            i = nxt("sq", 3)
            return SQ[:, i, :], r_SQ[i]

        def next_rs():
            i = nxt("rs", 3)
            return RS[:, i, :], r_RS[i]

        def mm(out, lhsT, rhs, start, stop, reads, writes):
            S.op("pe", lambda e: e.matmul(out, lhsT=lhsT, rhs=rhs, start=start, stop=stop), reads, writes)

        def act(out, in_, func, reads, writes, scale=None, bias=None):
            kw = {}
            if scale is not None:
                kw["scale"] = scale
            if bias is not None:
                kw["bias"] = bias
            S.op("act", lambda e: e.activation(out=out, in_=in_, func=func, **kw), reads, writes)

        def tt(out, in0, in1, op, reads, writes, eng="dve"):
            S.op(eng, lambda e: e.tensor_tensor(out=out, in0=in0, in1=in1, op=op), reads, writes)

        def ts(out, in0, s1, s2, op0, op1, reads, writes, eng="dve"):
            if op1 is None:
                S.op(eng, lambda e: e.tensor_scalar(out=out, in0=in0, scalar1=s1, scalar2=None, op0=op0), reads, writes)
            else:
                S.op(eng, lambda e: e.tensor_scalar(out=out, in0=in0, scalar1=s1, scalar2=s2, op0=op0, op1=op1),
                     reads, writes)

        def stt(out, in0, scalar, in1, op0, op1, reads, writes):
            S.op("dve", lambda e: e.scalar_tensor_tensor(out=out, in0=in0, scalar=scalar, in1=in1, op0=op0, op1=op1),
                 reads, writes)

        def cpy(out, in_, reads, writes, eng="dve"):
            if eng == "act":
                S.op("act", lambda e: e.copy(out=out, in_=in_), reads, writes)
            else:
                S.op(eng, lambda e: e.tensor_copy(out=out, in_=in_), reads, writes)

        def rstd_from_psum(ps, r_ps, dim):
            t1, r1 = next_rs()
            act(t1, ps[:, :], AF.Ln, [r_ps, r_const], [r1], scale=1.0 / dim, bias=EPSC[:, 0:1])
            t2, r2 = next_rs()
            act(t2, t1, AF.Exp, [r1], [r2], scale=-0.5)
            return t2, r2

        S.dma("sp", CONF[:], consts_in[:, :], [], [r_const])
        S.dma("sp", PV[:], pvec_in[:, :], [], [r_PV])
        for l in range(L):
            S.dma("pool", W2[:, l, :], w2_in[l], [], [r_const])
        cpy(IDENT[:], CONF[:, 0:128], [r_const], [r_const])
        cpy(MASKC[:], CONF[:, 256:384], [r_const], [r_const])
        cpy(PERM[:], CONF[:, 384:448], [r_const], [r_const])
        MASKB = CONF[:, 128:256]
        EPSC = CONF[:, 512:640]
        S.op("dve", lambda e: e.memset(ONESB[:], 1.0), [], [r_const])
        S.op("dve", lambda e: e.memset(ONEF[:], 1.0), [], [r_const])
        S.op("dve", lambda e: e.memset(SG[:], 0.0), [], r_SG)
        pvl = PV[:, 0:L * NPL].rearrange("p (l c) -> p l c", c=NPL)
        ts(NEGB[:], pvl[:, :, 82:84], -1.0, None, ALU.mult, None, [r_PV], [r_PV])
        lg = pvl[:, :, 84:88].rearrange("p l t -> p t l")
        S.op("dve", lambda e: e.tensor_reduce(out=SMT[:, :, 0:1], in_=lg, axis=mybir.AxisListType.X, op=ALU.max),
             [r_PV], [r_CH])
        tt(LBt[:], lg, SMT[:, :, 0:1].broadcast_to([128, 4, L]), ALU.subtract, [r_PV, r_CH], [r_PV])
        act(LBt[:], LBt[:], AF.Exp, [r_PV], [r_PV])
        S.op("dve", lambda e: e.tensor_reduce(out=SMT[:, :, 1:2], in_=LBt[:], axis=mybir.AxisListType.X, op=ALU.add),
             [r_PV], [r_CH])
        S.op("dve", lambda e: e.reciprocal(out=SMT[:, :, 2:3], in_=SMT[:, :, 1:2]), [r_CH], [r_CH])
        tt(LBt[:], LBt[:], SMT[:, :, 2:3].broadcast_to([128, 4, L]), ALU.mult, [r_PV, r_CH], [r_PV])
        cpy(SMT[:, :, 3:4], LBt[:, :, 0:1], [r_PV], [r_CH])
        for l in range(1, L):
            tt(LBt[:, :, l:l + 1], LBt[:, :, l:l + 1], LBt[:, :, l - 1:l], ALU.add, [r_PV], [r_PV])
        tt(LBt[:], LBt[:], SMT[:, :, 3:4].broadcast_to([128, 4, L]), ALU.subtract, [r_PV, r_CH], [r_PV])
        ts(OMLt[:], LBt[:], -1.0, 1.0, ALU.mult, ALU.add, [r_PV], [r_PV])

        INVF = PV[0:64, L * NPL:L * NPL + 1]
        SGN = PV[0:64, L * NPL + 1:L * NPL + 2]
        for ti in range(NT):
            t0 = ti * TT
            xi, rxi = next_xs()
            S.dma("sp", xi[0:64, :].bitcast(I32), pos_in[:, t0:t0 + TT].partition_broadcast(64), [], [rxi])
            rr, rrr = next_xs()
            cpy(rr[0:64, :], xi[0:64, :].bitcast(I32), [rxi], [rrr])
            ts(rr[0:64, :], rr[0:64, :], INVF, 1.0 / (2.0 * np.pi), ALU.mult, ALU.mult, [rrr, r_PV], [rrr])
            for which in range(2):
                a, ra = next_rs()
                b, rb = next_rs()
                if which == 0:
                    ts(a[0:64, :], rr[0:64, :], 0.25, None, ALU.add, None, [rrr], [ra])
                    src = a
                    rsrc = ra
                else:
                    src = rr
                    rsrc = rrr
                ts(b[0:64, :], src[0:64, :], MAGIC, None, ALU.add, None, [rsrc], [rb])
                ts(b[0:64, :], b[0:64, :], MAGIC, None, ALU.subtract, None, [rb], [rb])
                tt(b[0:64, :], src[0:64, :], b[0:64, :], ALU.subtract, [rsrc, rb], [rb])
                o, ro = next_xs()
                act(o[0:64, :], b[0:64, :], AF.Sin, [rb], [ro], scale=6.283185)
                if which == 1:
                    ts(o[0:64, :], o[0:64, :], SGN, None, ALU.mult, None, [ro, r_PV], [ro])
                    S.dma("sp", ss_d[:, t0:t0 + TT], o[0:64, :], [ro], [r_ccss[ti]])
                else:
                    S.dma("sp", cc_d[:, t0:t0 + TT], o[0:64, :], [ro], [r_ccss[ti]])

        ucnt = [0]
        cur_ti = [0]

        def load_w(dst, rdst, src, wb):
            u = ucnt[0]
            ucnt[0] += 1
            n = 1
            for dd in dst.shape[1:]:
                n *= dd
            flat = wb[:, 0:n]
            if cur_ti[0] == 0 or NT == 1:
                S.dma("pool", dst, src, [], [rdst])
                if NT > 1:
                    S.dma("sp", wcache[u, :, 0:n], flat, [rdst], [r_cache[u]])
            else:
                S.dma("pool", flat, wcache[u, :, 0:n], [r_cache[u]], [rdst])

        def rms_stats_from_dram(src_d, r_src, t0):
            for kc in range(16):
                xb, rx = next_xs()
                S.dma("sp", xb, src_d[kc * 128:(kc + 1) * 128, t0:t0 + TT], [r_src], [rx])
                sq, rsq = next_sq()
                act(sq, xb, AF.Square, [rx], [rsq])
                mm(PSTAT[:, :], ONESB[:, :], sq, kc == 0, kc == 15, [rsq, r_const], [r_PSTAT])
            return rstd_from_psum(PSTAT, r_PSTAT, float(D))

        def normed_from_dram(src_d, r_src, t0, gcol, dst_res, rstd, r_rstd):
            for kc in range(16):
                xb, rx = next_xs()
                S.dma("sp", xb, src_d[kc * 128:(kc + 1) * 128, t0:t0 + TT], [r_src], [rx])
                stt(HY[:, kc, :], xb, PV[:, gcol + kc:gcol + kc + 1], rstd, ALU.mult, ALU.mult,
                    [rx, r_PV, r_rstd], [dst_res[kc]])

        def proj_fm(wsrc_cols, kdim_chunks, rhs_fn, rhs_res, nchunk, evac):
            for c in range(nchunk):
                ps, rps = next_pa()
                for kc in range(kdim_chunks):
                    lhsT, rw, m = wsrc_cols(kc, c)
                    mm(ps[0:m, :], lhsT, rhs_fn(kc), kc == 0, kc == kdim_chunks - 1,
                       [rw, rhs_res[kc]], [rps])
                evac(c, ps, rps)

        def prologue(pl, pti):
            psrc = xT_in if pl == 0 else xres_d
            pres = Res("xin") if pl == 0 else r_xres[pti]
            pt0 = pti * TT
            S.fence(r_u + r_y, r_h)
            prstd, pr_rstd = rms_stats_from_dram(psrc, pres, pt0)
            normed_from_dram(psrc, pres, pt0, pl * NPL + 0, r_h, prstd, pr_rstd)
            S.dma("sp", CS[0:64, 0, :], cc_d[:, pt0:pt0 + TT], [r_ccss[pti]], [r_CS])
            S.dma("sp", CS[0:64, 1, :], ss_d[:, pt0:pt0 + TT], [r_ccss[pti]], [r_CS])

        for l in range(L):
            pb = l * NPL
            src_d = xT_in if l == 0 else xres_d
            r_src_t = None if l == 0 else r_xres
            dst_d = yT_out if l == L - 1 else xres_d
            winv = w_in[l].rearrange("(kc p) n -> p kc n", p=128)
            wqv = wq_in[l].rearrange("(kc p) n -> p kc n", p=128)
            wkvv = wkv_in[l].rearrange("(kc p) n -> p kc n", p=128)
            wov = wo_in[l].rearrange("(kc p) n -> p kc n", p=128)
            wgv = wg_in[l].rearrange("(kc p) n -> p kc n", p=128)
            wuv = wu_in[l].rearrange("(kc p) n -> p kc n", p=128)
            wdv = wd_in[l].rearrange("(kc p) n -> p kc n", p=128)
            S.op("dve", lambda e: e.memset(SG[:], 0.0), [], r_SG)

            for ti in range(NT):
                t0 = ti * TT
                ucnt[0] = 0
                cur_ti[0] = ti
                r_src = [] if r_src_t is None else [r_src_t[ti]]
                rsrc1 = r_src[0] if r_src else Res("xin")

                if l == 0 and ti == 0:
                    prologue(0, 0)
                S.fence(RA5 + RA4, RA1)

                def hrhs(kc):
                    return HY[:, kc, :]

                S.dma("pool", WSM[:, :, 0:16], winv[:, :, O_GLOW:O_GLOW + 16], [], [r_WSM])
                S.dma("pool", WSM[:, :, 16:80], winv[:, :, O_KPE:O_KPE + 64], [], [r_WSM])

                def load_group(col0, ncols):
                    wb, rwb = next_wb()
                    wv = wb[:, 0:16 * ncols].rearrange("p (k n) -> p k n", k=16)
                    load_w(wv, rwb, winv[:, :, col0:col0 + ncols], wb)
                    return wv, rwb

                def fm_group(col0, nchunk, evac):
                    c = 0
                    while c < nchunk:
                        g = min(3, nchunk - c)
                        wv, rwb = load_group(col0 + c * 128, g * 128)
                        base = c
                        proj_fm(lambda kc, cc, wv=wv, rwb=rwb: (wv[:, kc, cc * 128:(cc + 1) * 128], rwb, 128),
                                16, hrhs, r_h, g, lambda cc, ps, rps, base=base: evac(base + cc, ps, rps))
                        c += g

                def tm_group(col0, dstv, dres):
                    for half in range(2):
                        wv, rwb = load_group(col0 + half * 256, 256)
                        for sub in range(4):
                            ps, rps = next_pa()
                            for kc in range(16):
                                mm(ps[:, 0:256], HY[:, kc, sub * 128:(sub + 1) * 128], wv[:, kc, :],
                                   kc == 0, kc == 15, [rwb, r_h[kc]], [rps])
                            cpy(dstv[:, sub, half * 256:(half + 1) * 256], ps[:, 0:256], [rps], [dres[sub]],
                                eng="act")

                def ev_gqk(c, ps, rps):
                    if c < 2:
                        act(QF[:, c, :], ps[:, :], AF.Copy, [rps], [r_QF[c]], scale=0.125)
                    else:
                        cpy(KF[:, c - 2, :], ps[:, :], [rps], [r_KF[c - 2]])
                fm_group(O_GQ, 4, ev_gqk)
                tm_group(O_GV, VG, r_VG)
                ps, rps = next_pa()
                for kc in range(16):
                    mm(ps[0:16, :], WSM[:, kc, 0:16], HY[:, kc, :], kc == 0, kc == 15, [r_WSM, r_h[kc]], [rps])
                cpy(GLOW[:, :], ps[0:16, :], [rps], [r_GLOW], eng="act")
                for c in range(2):
                    ps, rps = next_pa()
                    mm(ps[:, :], W2[:, l, c * 128:(c + 1) * 128], GLOW[:, :], True, True, [r_const, r_GLOW], [rps])
                    e1, re1 = next_rs()
                    act(e1, ps[:, :], AF.Exp, [rps, r_PV], [re1], scale=-1.0, bias=NEGB[:, l, c:c + 1])
                    e2, re2 = next_rs()
                    act(e2, e1, AF.Ln, [re1, r_const], [re2], bias=EPSC[:, 1:2])
                    S.op("dve", lambda e, e2=e2, c=c: e.tensor_tensor_scan(
                        out=BFv[:, c, :], data0=ONEF[:, :], data1=e2, initial=0.0, op0=ALU.mult, op1=ALU.subtract),
                        [re2, r_const], [r_BF[c]])
                fm_group(O_GOUT, 4, lambda c, ps, rps: act(GS[:, c, :], ps[:, :], AF.Silu, [rps], [r_GS[c]]))
                fm_group(O_HQ, 4, lambda c, ps, rps: act(QF[:, 2 + c, :], ps[:, :], AF.Silu, [rps], [r_QF[2 + c]]))

                def ev_hf(c, ps, rps):
                    sg, rsg = next_rs()
                    act(sg, ps[:, :], AF.Sigmoid, [rps], [rsg])
                    f, rf = next_rs()
                    ts(f, sg, OMLt[:, c, l:l + 1], LBt[:, c, l:l + 1], ALU.mult, ALU.add, [rsg, r_PV], [rf])
                    ts(KF[:, 2 + c, :], f, -1.0, 1.0, ALU.mult, ALU.add, [rf], [r_KF[2 + c]])
                    lf, rlf = next_rs()
                    act(lf, f, AF.Ln, [rf], [rlf])
                    S.op("dve", lambda e, lf=lf, c=c: e.tensor_tensor_scan(
                        out=BFv[:, 2 + c, :], data0=ONEF[:, :], data1=lf, initial=0.0, op0=ALU.mult, op1=ALU.add),
                        [rlf, r_const], [r_BF[2 + c]])
                fm_group(O_HF, 4, ev_hf)
                tm_group(O_HI, VH, r_VH)
                fm_group(O_HOUT, 4, lambda c, ps, rps: act(GS[:, 4 + c, :], ps[:, :], AF.Silu, [rps], [r_GS[4 + c]]))

                def latent(col0, gcol, dstv, dres):
                    tmp = []

                    def ev(c, ps, rps):
                        xb, rx = next_xs()
                        cpy(xb, ps[:, :], [rps], [rx], eng="act")
                        sq, rsq = next_sq()
                        act(sq, ps[:, :], AF.Square, [rps], [rsq])
                        mm(PSTAT[:, :], ONESB[:, :], sq, c == 0, c == 3, [rsq, r_const], [r_PSTAT])
                        tmp.append((xb, rx))
                    c = 0
                    wv, rwb = load_group(col0, 384)
                    proj_fm(lambda kc, cc: (wv[:, kc, cc * 128:(cc + 1) * 128], rwb, 128), 16, hrhs, r_h, 3, ev)
                    wv2, rwb2 = load_group(col0 + 384, 128)
                    proj_fm(lambda kc, cc: (wv2[:, kc, 0:128], rwb2, 128), 16, hrhs, r_h, 1,
                            lambda cc, ps, rps: ev(3, ps, rps))
                    rstd2, r_rstd2 = rstd_from_psum(PSTAT, r_PSTAT, 512.0)
                    for c in range(4):
                        xb, rx = tmp[c]
                        stt(dstv[:, c, :], xb, PV[:, gcol + c:gcol + c + 1], rstd2, ALU.mult, ALU.mult,
                            [rx, r_PV, r_rstd2], [dres[c]])
                latent(O_QC, pb + 64, QN, r_QN)
                latent(O_KVC, pb + 68, CN, r_CN)

                S.fence(RB5 + RB3 + [r_XB], RB2)
                ps, rps = next_pa()
                for kc in range(16):
                    mm(ps[0:64, :], WSM[:, kc, 16:80], HY[:, kc, :], kc == 0, kc == 15, [r_WSM, r_h[kc]], [rps])

                def rope(ps, rps, dst, rdst, scale):
                    qb, rqb = next_sq()
                    act(qb[0:64, :], ps[0:64, :], AF.Copy, [rps], [rqb], scale=scale)
                    ps2, rps2 = next_pa()
                    mm(ps2[0:64, :], PERM[0:64, :], qb[0:64, :], True, True, [r_const, rqb], [rps2])
                    a, ra = next_xs()
                    tt(a[0:64, :], qb[0:64, :], CS[0:64, 0, :], ALU.mult, [rqb, r_CS], [ra])
                    b, rb = next_xs()
                    tt(b[0:64, :], ps2[0:64, :], CS[0:64, 1, :], ALU.mult, [rps2, r_CS], [rb])
                    tt(dst, a[0:64, :], b[0:64, :], ALU.add, [ra, rb], [rdst])
                kpo, rkpo = next_sq()
                rope(ps, rps, kpo[0:64, :], rkpo, 1.0)
                S.dma("sp", kpe_d[:, t0:t0 + TT], kpo[0:64, :], [rkpo], [r_kv[ti]])

                for (a0, a1, sc) in ((0, 2, 1.0 / 16.0), (2, 6, 1.0)):
                    BL = BFv[:, a0:a1, 63:TT:64]
                    BM = BFv[:, a0:a1, 31:TT:64]
                    rb_ = r_BF[a0:a1]
                    S.op("dve", lambda e, a0=a0, a1=a1: e.memset(CT[:, a0:a1, 0:1], 0.0), [], [r_CH])
                    cpy(CT[:, a0:a1, 1:8], BFv[:, a0:a1, 63:TT - 64:64], rb_, [r_CH])
                    tt(DEC[:, a0:a1, :], BL, CT[:, a0:a1, :], ALU.subtract, rb_ + [r_CH], [r_CH])
                    act(DEC[:, a0:a1, :], DEC[:, a0:a1, :], AF.Exp, [r_CH], [r_CH], scale=sc)
                    tt(WG[:, a0:a1, :], BL, BM, ALU.subtract, rb_, [r_CH])
                    act(WG[:, a0:a1, :], WG[:, a0:a1, :], AF.Exp, [r_CH], [r_CH], scale=sc)
                    tt(SI[:, a0:a1, :], BM, CT[:, a0:a1, :], ALU.subtract, rb_ + [r_CH], [r_CH])
                    act(SI[:, a0:a1, :], SI[:, a0:a1, :], AF.Exp, [r_CH], [r_CH], scale=sc)
                for a in range(6):
                    sc = 1.0 / 16.0 if a < 2 else 1.0
                    i = nxt("et", 3)
                    arg, rarg = ET[:, i, :], r_ET[i]
                    tt(arg.rearrange("p (c t) -> p c t", c=8), BFv[:, a, :].rearrange("p (c t) -> p c t", c=8),
                       BFv[:, a, 31:TT:64].unsqueeze(2).broadcast_to([128, 8, 64]), ALU.subtract, [r_BF[a]], [rarg])
                    i = nxt("et", 3)
                    e1, re1 = ET[:, i, :], r_ET[i]
                    act(e1, arg, AF.Exp, [rarg], [re1], scale=sc)
                    tt(QT[:, a, :], QF[:, a, :], e1, ALU.mult, [r_QF[a], re1], [r_QT[a]])
                    i = nxt("et", 3)
                    e2, re2 = ET[:, i, :], r_ET[i]
                    act(e2, arg, AF.Exp, [rarg], [re2], scale=-sc)
                    tt(KT[:, a, :], KF[:, a, :], e2, ALU.mult, [r_KF[a], re2], [r_KT[a]])
                    for sub in range(4):
                        S.op("pe", lambda e, a=a, sub=sub: e.transpose(
                            PTR[:, sub, :], KT[:, a, sub * 128:(sub + 1) * 128], IDENT[:, :]),
                            [r_KT[a], r_const], [r_PTR[0]])
                    cpy(KTT[:, a, :, :], PTR[:, 0:4, :], [r_PTR[0]], r_KTT[a], eng="act")

                def rot3():
                    k = nxt("pa3", 3)
                    return ((PA[4], r_PA[4]), (PA[5], r_PA[5]), (PSTAT, r_PSTAT))[k]

                for sub in range(4):
                    tcs = slice(sub * 128, (sub + 1) * 128)
                    for wave in range(2):
                        hinfo = []
                        if wave == 0:
                            for a in range(2):
                                for k in range(2):
                                    hh = 2 * a + k
                                    hinfo.append((a, hh, 64 * k, 64, VG, r_VG, slice(hh * 128, (hh + 1) * 128), hh))
                            alist = [0, 1]
                        else:
                            for hh in range(4):
                                hinfo.append((2 + hh, 4 + hh, 0, 128, VH, r_VH, slice(hh * 128, (hh + 1) * 128), hh))
                            alist = [2, 3, 4, 5]
                        for (a, oh, p0, dk, Vt, rV, vcol, bk) in hinfo:
                            psA, rpsA = rot3()
                            mm(psA[:, 0:128], KT[p0:p0 + dk, a, tcs], QT[p0:p0 + dk, a, tcs], True, True,
                               [r_KT[a], r_QT[a]], [rpsA])
                            tt(ATMv[:, oh, :], psA[:, 0:128], MASKB, ALU.mult, [rpsA, r_const], [r_ATM[oh]])
                        for (a, oh, p0, dk, Vt, rV, vcol, bk) in hinfo:
                            mm(PA[bk][:, 0:128], Vt[:, sub, vcol], ATMv[:, oh, :], True, False,
                               [rV[sub], r_ATM[oh]], [r_PA[bk]])
                        for half in range(2):
                            c = sub * 2 + half
                            pr = slice(half * 64, half * 64 + 64)
                            qcs = slice(c * 64, (c + 1) * 64)
                            for a in alist:
                                S.op("act", lambda e, a=a, c=c: e.activation(
                                    out=SBs[:, a, :], in_=SG[:, a, :], func=AF.Copy, scale=SI[:, a, c:c + 1]),
                                    [r_SG[a], r_CH], [r_SBs[a]])
                            for (a, oh, p0, dk, Vt, rV, vcol, bk) in hinfo:
                                mm(PA[bk][:, half * 64:half * 64 + 64], SBs[p0:p0 + dk, a, :], QT[p0:p0 + dk, a, qcs],
                                   False, half == 1, [r_SBs[a], r_QT[a]], [r_PA[bk]])
                            for a in alist:
                                psU, rpsU = rot3()
                                if a < 2:
                                    mm(psU[:, 0:256], KTT[pr, a, sub, :], VG[pr, sub, a * 256:(a + 1) * 256], True, True,
                                       [r_KTT[a][sub], r_VG[sub]], [rpsU])
                                else:
                                    mm(psU[:, 0:128], KTT[pr, a, sub, :], VH[pr, sub, (a - 2) * 128:(a - 1) * 128],
                                       True, True, [r_KTT[a][sub], r_VH[sub]], [rpsU])
                                u, ru = next_xs()
                                if a < 2:
                                    S.op("act", lambda e, a=a, c=c, u=u, psU=psU: e.activation(
                                        out=u[0:64, 0:128], in_=psU[0:64, 0:128], func=AF.Copy,
                                        scale=WG[0:64, a, c:c + 1]), [rpsU, r_CH], [ru])
                                    S.op("act", lambda e, a=a, c=c, u=u, psU=psU: e.activation(
                                        out=u[64:128, 0:128], in_=psU[64:128, 128:256], func=AF.Copy,
                                        scale=WG[64:128, a, c:c + 1]), [rpsU, r_CH], [ru])
                                else:
                                    S.op("act", lambda e, a=a, c=c, u=u, psU=psU: e.activation(
                                        out=u[:, 0:128], in_=psU[:, 0:128], func=AF.Copy, scale=WG[:, a, c:c + 1]),
                                        [rpsU, r_CH], [ru])
                                stt(SG[:, a, :], SG[:, a, :], DEC[:, a, c:c + 1], u[:, 0:128], ALU.mult, ALU.add,
                                    [r_SG[a], r_CH, ru], [r_SG[a]])
                        for (a, oh, p0, dk, Vt, rV, vcol, bk) in hinfo:
                            cpy(OT[:, oh, tcs], PA[bk][:, 0:128], [r_PA[bk]], [r_OT[oh]], eng="act")

                S.fence(r_h + r_u, r_y)
                for oh in range(8):
                    sq, rsq = next_sq()
                    act(sq, OT[:, oh, :], AF.Square, [r_OT[oh]], [rsq])
                    mm(PSTAT[:, :], ONESB[:, :], sq, True, True, [rsq, r_const], [r_PSTAT])
                    rstd2, r_rstd2 = rstd_from_psum(PSTAT, r_PSTAT, 128.0)
                    gcol = pb + (80 if oh < 4 else 81)
                    t1, rt1 = next_xs()
                    stt(t1, OT[:, oh, :], PV[:, gcol:gcol + 1], rstd2, ALU.mult, ALU.mult,
                        [r_OT[oh], r_PV, r_rstd2], [rt1])
                    tt(HY[:, oh, :], t1, GS[:, oh, :], ALU.mult, [rt1, r_GS[oh]], [r_y[oh]])

                S.fence(RB2 + RB5 + [r_XB], RB3)
                wb, rwb = next_wb()
                wq = wb[:, 0:6144].rearrange("p (k n) -> p k n", k=4)
                load_w(wq, rwb, wqv[:, :, :], wb)
                qscale = 192.0 ** -0.5
                for h in range(8):
                    ps, rps = next_pa()
                    for kc in range(4):
                        mm(ps[:, :], wq[:, kc, h * 192:h * 192 + 128], QN[:, kc, :], kc == 0, kc == 3,
                           [rwb, r_QN[kc]], [rps])
                    act(QNOPE[:, h, :], ps[:, :], AF.Copy, [rps], [r_QNOPE[h]], scale=qscale)
                    ps, rps = next_pa()
                    for kc in range(4):
                        mm(ps[0:64, :], wq[:, kc, h * 192 + 128:h * 192 + 192], QN[:, kc, :], kc == 0, kc == 3,
                           [rwb, r_QN[kc]], [rps])
                    rope(ps, rps, QPE[0:64, h, :], r_QPE[h], qscale)
                S.fence_merge(r_KS[2:4] + r_KPS[2:4] + r_VS[2:4], r_KTO + r_VTO)
                wkvs = []
                for half in range(2):
                    wb, rwb = next_wb()
                    wk = wb[:, 0:4096].rearrange("p (k n) -> p k n", k=4)
                    load_w(wk, rwb, wkvv[:, :, half * 1024:(half + 1) * 1024], wb)
                    wkvs.append((wk, rwb))
                for h in range(8):
                    wk, rwb = wkvs[h // 4]
                    hc = (h % 4) * 256
                    ps, rps = next_pa()
                    for kc in range(4):
                        mm(ps[:, :], wk[:, kc, hc:hc + 128], CN[:, kc, :], kc == 0, kc == 3, [rwb, r_CN[kc]], [rps])
                    i = h % 2
                    cpy(KTO[:, i, :], ps[:, :], [rps], [r_KTO[i]], eng="act")
                    S.dma("sp", kn_d[h, :, t0:t0 + TT], KTO[:, i, :], [r_KTO[i]], [r_kv[ti]])
                for sub in range(4):
                    i = sub % 2
                    for half in range(2):
                        wk, rwb = wkvs[half]
                        ps, rps = next_pa()
                        rhsv = wk.rearrange("p k (h c) -> p k h c", c=256)
                        for kc in range(4):
                            mm(ps[:, :].rearrange("p (h e) -> p h e", h=4), CN[:, kc, sub * 128:(sub + 1) * 128],
                               rhsv[:, kc, :, 128:256], kc == 0, kc == 3, [rwb, r_CN[kc]], [rps])
                        cpy(VTO[:, i, half * 4:(half + 1) * 4, :], ps[:, :].rearrange("p (h e) -> p h e", h=4),
                            [rps], [r_VTO[i]], eng="act")
                    S.dma("sp", vd_d[:, ti, :, sub, :].rearrange("h p e -> p h e"), VTO[:, i, :, :],
                          [r_VTO[i]], [r_kv[ti]])

                LOOK = int(os.environ.get('K_LOOK', '2'))
                S.fence_merge(r_KTO + r_VTO, r_KS[2:4] + r_KPS[2:4] + r_VS[2:4])
                for h in range(8):
                    psO, rpsO = PA[4], r_PA[4]
                    psD, rpsD = PA[5], r_PA[5]
                    blocks = [(kt, kb) for kt in range(ti + 1) for kb in range(4)]
                    loaded = {}

                    def ensure_loaded(kt, h=h, loaded=loaded):
                        if kt not in loaded:
                            i = nxt("ks", 4)
                            loaded[kt] = i
                            S.dma("sp", KSL[i], kn_d[h, :, kt * TT:(kt + 1) * TT], [r_kv[kt]], [r_KS[i]])
                            S.dma("sp", KPSL[i][0:64, :], kpe_d[:, kt * TT:(kt + 1) * TT], [r_kv[kt]], [r_KPS[i]])
                            S.dma("sp", VSL[i], vd_d[h, kt], [r_kv[kt]], [r_VS[i]])
                        return loaded[kt]

                    def emit_qk(bi, h=h):
                        kt, kb = blocks[bi]
                        i = ensure_loaded(kt)
                        q0 = kb * 128 if kt == ti else 0
                        qs = slice(q0, TT)
                        k4 = nxt("pa4", 4)
                        psS, rpsS = PA[k4], r_PA[k4]
                        mm(psS[:, qs], KSL[i][:, kb * 128:(kb + 1) * 128], QNOPE[:, h, qs], True, False,
                           [r_KS[i], r_QNOPE[h]], [rpsS])
                        mm(psS[:, qs], KPSL[i][0:64, kb * 128:(kb + 1) * 128], QPE[0:64, h, qs], False, True,
                           [r_KPS[i], r_QPE[h]], [rpsS])
                        j = nxt("pt", 4)
                        act(PT[:, j, qs], psS[:, qs], AF.Exp, [rpsS], [r_PT[j]])
                        if kt == ti:
                            tt(PT[:, j, q0:q0 + 128], PT[:, j, q0:q0 + 128], MASKC[:, :], ALU.mult,
                               [r_PT[j], r_const], [r_PT[j]])
                        return (i, j, qs, kb)

                    def emit_pv(bi, st):
                        i, j, qs, kb = st
                        first = bi == 0
                        last = bi == len(blocks) - 1
                        mm(psO[:, qs], VSL[i][:, kb, :], PT[:, j, qs], first, last, [r_VS[i], r_PT[j]], [rpsO])
                        mm(psD[:, qs], ONESB[:, :], PT[:, j, qs], first, last, [r_const, r_PT[j]], [rpsD])

                    pend = []
                    for bi in range(len(blocks) + LOOK):
                        if bi < len(blocks):
                            pend.append(emit_qk(bi))
                        if bi >= LOOK:
                            emit_pv(bi - LOOK, pend[bi - LOOK])
                    rd, rrd = next_rs()
                    S.op("dve", lambda e, rd=rd, psD=psD: e.reciprocal(out=rd, in_=psD[:, :]), [rpsD], [rrd])
                    tt(OMLA[:, h, :], psO[:, :], rd, ALU.mult, [rpsO, rrd], [r_OMLA[h]])
                for h in range(8):
                    sq, rsq = next_sq()
                    act(sq, OMLA[:, h, :], AF.Square, [r_OMLA[h]], [rsq])
                    mm(PSTAT[:, :], ONESB[:, :], sq, h == 0, h == 7, [rsq, r_const], [r_PSTAT])
                rstd3, r_rstd3 = rstd_from_psum(PSTAT, r_PSTAT, 1024.0)
                for h in range(8):
                    stt(HY[:, 8 + h, :], OMLA[:, h, :], PV[:, pb + 72 + h:pb + 73 + h], rstd3, ALU.mult, ALU.mult,
                        [r_OMLA[h], r_PV, r_rstd3], [r_y[8 + h]])

                S.fence(RA1 + RA5, RA4)
                S.fence(RB2 + RB3 + RB5, [r_XB])
                for q4 in range(4):
                    S.dma("sp", XB[:, 4 * q4:4 * q4 + 4, :],
                          src_d[q4 * 512:(q4 + 1) * 512, t0:t0 + TT].rearrange("(g p) t -> p g t", p=128),
                          [rsrc1], [r_XB])
                for g in range(16):
                    if g % 3 == 0:
                        gn = min(3, 16 - g)
                        wb, rwb = next_wb()
                        wv = wb[:, 0:16 * gn * 128].rearrange("p (k n) -> p k n", k=16)
                        load_w(wv, rwb, wov[:, :, g * 128:(g + gn) * 128], wb)
                        gbase = g
                    ps, rps = next_pa()
                    cc = g - gbase
                    for kc in range(16):
                        mm(ps[:, :], wv[:, kc, cc * 128:(cc + 1) * 128], HY[:, kc, :], kc == 0, kc == 15,
                           [rwb, r_y[kc]], [rps])
                    cpy(MT[:, g, :], ps[:, :], [rps], [r_MT[g]], eng="act")
                    sq, rsq = next_sq()
                    act(sq, ps[:, :], AF.Square, [rps], [rsq])
                    mm(PSTAT[:, :], ONESB[:, :], sq, g == 0, g == 15, [rsq, r_const], [r_PSTAT])
                rstd4, r_rstd4 = rstd_from_psum(PSTAT, r_PSTAT, float(D))
                for g in range(16):
                    stt(MT[:, g, :], MT[:, g, :], PV[:, pb + 16 + g:pb + 17 + g], rstd4, ALU.mult, ALU.mult,
                        [r_MT[g], r_PV, r_rstd4], [r_MT[g]])
                    tt(MT[:, g, :], MT[:, g, :], XB[:, g, :], ALU.add, [r_MT[g], r_XB], [r_MT[g]])
                    S.dma("sp", xmid_d[g * 128:(g + 1) * 128, t0:t0 + TT], MT[:, g, :], [r_MT[g]], [r_xmid[ti]])

                S.fence(r_y + r_h, r_u)
                for g in range(16):
                    sq, rsq = next_sq()
                    act(sq, MT[:, g, :], AF.Square, [r_MT[g]], [rsq])
                    mm(PSTAT[:, :], ONESB[:, :], sq, g == 0, g == 15, [rsq, r_const], [r_PSTAT])
                rstd5, r_rstd5 = rstd_from_psum(PSTAT, r_PSTAT, float(D))
                for g in range(16):
                    stt(HY[:, g, :], MT[:, g, :], PV[:, pb + 32 + g:pb + 33 + g], rstd5, ALU.mult, ALU.mult,
                        [r_MT[g], r_PV, r_rstd5], [r_u[g]])
                S.fence(RA1 + RA4, RA5)
                S.fence(RB2 + RB3 + [r_XB], RB5)
                for c0 in range(0, 44, 3):
                    gn = min(3, 44 - c0)
                    wbg, rwbg = next_wb()
                    wvg = wbg[:, 0:16 * gn * 128].rearrange("p (k n) -> p k n", k=16)
                    load_w(wvg, rwbg, wgv[:, :, c0 * 128:(c0 + gn) * 128], wbg)
                    wbu, rwbu = next_wb()
                    wvu = wbu[:, 0:16 * gn * 128].rearrange("p (k n) -> p k n", k=16)
                    load_w(wvu, rwbu, wuv[:, :, c0 * 128:(c0 + gn) * 128], wbu)
                    for cc in range(gn):
                        c = c0 + cc
                        psg, rpsg = next_pa()
                        for kc in range(16):
                            mm(psg[:, :], wvg[:, kc, cc * 128:(cc + 1) * 128], HY[:, kc, :], kc == 0, kc == 15,
                               [rwbg, r_u[kc]], [rpsg])
                        psu, rpsu = next_pa()
                        for kc in range(16):
                            mm(psu[:, :], wvu[:, kc, cc * 128:(cc + 1) * 128], HY[:, kc, :], kc == 0, kc == 15,
                               [rwbu, r_u[kc]], [rpsu])
                        sg, rsg = next_rs()
                        act(sg, psg[:, :], AF.Silu, [rpsg], [rsg])
                        tt(HID[:, c, :], sg, psu[:, :], ALU.mult, [rsg, rpsu], [r_HID[c]])
                nl, nti = (l, ti + 1) if ti + 1 < NT else (l + 1, 0)
                hoist = nl < L and not (NT == 1) and os.environ.get('K_HOIST', '1') == '1'
                if hoist:
                    prologue(nl, nti)
                for g in range(16):
                    wb, rwb = next_wb()
                    wv = wb[:, 0:44 * 128].rearrange("p (k n) -> p k n", k=44)
                    load_w(wv, rwb, wdv[:, :, g * 128:(g + 1) * 128], wb)
                    ps, rps = next_pa()
                    for kc in range(44):
                        mm(ps[:, :], wv[:, kc, :], HID[:, kc, :], kc == 0, kc == 43, [rwb, r_HID[kc]], [rps])
                    cpy(FT[:, g, :], ps[:, :], [rps], [r_FT[g]], eng="act")
                    sq, rsq = next_sq()
                    act(sq, ps[:, :], AF.Square, [rps], [rsq])
                    mm(PSTAT[:, :], ONESB[:, :], sq, g == 0, g == 15, [rsq, r_const], [r_PSTAT])
                rstd6, r_rstd6 = rstd_from_psum(PSTAT, r_PSTAT, float(D))
                rdst = [r_xres[ti]] if l < L - 1 else [Res("yout")]
                for g in range(16):
                    xb, rx = next_xs()
                    S.dma("sp", xb, xmid_d[g * 128:(g + 1) * 128, t0:t0 + TT], [r_xmid[ti]], [rx])
                    stt(FT[:, g, :], FT[:, g, :], PV[:, pb + 48 + g:pb + 49 + g], rstd6, ALU.mult, ALU.mult,
                        [r_FT[g], r_PV, r_rstd6], [r_FT[g]])
                    tt(FT[:, g, :], FT[:, g, :], xb, ALU.add, [r_FT[g], rx], [r_FT[g]])
                    S.dma("sp", dst_d[g * 128:(g + 1) * 128, t0:t0 + TT], FT[:, g, :], [r_FT[g]], rdst)
                if nl < L and not hoist:
                    prologue(nl, nti)

        S.drain("sp")
        build_program.stats = (S.nins, S.nwaits)
    return nc


def host_consts(L):
    c = np.zeros((128, 5 * 128), np.float32)
    c[:, 0:128] = np.eye(128, dtype=np.float32)
    j = np.arange(128)[:, None]
    i = np.arange(128)[None, :]
    c[:, 128:256] = ((j // 64 == i // 64) & (i >= j)).astype(np.float32)
    c[:, 256:384] = (i >= j).astype(np.float32)
    jj = np.arange(64)[:, None]
    ii = np.arange(64)[None, :]
    c[0:64, 384:448] = (jj == (ii + 32) % 64).astype(np.float32)
    c[:, 512] = EPS
    c[:, 513] = 1.0
    return c


def host_pvec(L, p):
    pv = np.zeros((128, L * NPL + 2), np.float32)

    def cols(v):
        return np.ascontiguousarray(np.asarray(v, np.float32).reshape(-1, 128).T)
    for l in range(L):
        b = l * NPL
        pv[:, b + 0:b + 16] = cols(p["attn_pre_norm"][l])
        pv[:, b + 16:b + 32] = cols(p["attn_post_norm"][l])
        pv[:, b + 32:b + 48] = cols(p["ffn_pre_norm"][l])
        pv[:, b + 48:b + 64] = cols(p["ffn_post_norm"][l])
        pv[:, b + 64:b + 68] = cols(p["mla_q_norm"][l])
        pv[:, b + 68:b + 72] = cols(p["mla_kv_norm"][l])
        pv[:, b + 72:b + 80] = cols(p["mla_out_norm"][l])
        pv[:, b + 80:b + 81] = cols(p["gla_out_norm"][l])
        pv[:, b + 81:b + 82] = cols(p["hgrn_out_norm"][l])
        pv[:, b + 82:b + 84] = cols(p["gla_gate_b"][l])
        pv[:, b + 84:b + 88] = cols(p["hgrn_lb_logits"][l])
    inv_freq = (10000.0 ** (-np.arange(0, 64, 2, dtype=np.float32) / 64.0)).astype(np.float32)
    pv[0:64, L * NPL] = np.concatenate([inv_freq, inv_freq])
    pv[0:32, L * NPL + 1] = -1.0
    pv[32:64, L * NPL + 1] = 1.0
    return pv


_PROG_CACHE = {}


def run(inputs, T, L, B):
    key = (T, L)
    if key not in _PROG_CACHE:
        _PROG_CACHE[key] = build_program(T, L)
    nc = _PROG_CACHE[key]
    x = np.asarray(inputs["x"], np.float32)
    pos = np.asarray(inputs["positions"], np.int32)
    pv = host_pvec(L, inputs)
    cs = host_consts(L)
    shared = {
        "pvec": pv, "consts": cs,
        "w_in": np.ascontiguousarray(np.asarray(inputs["w_in"], np.float32)),
        "gla_gate_w2": np.ascontiguousarray(np.asarray(inputs["gla_gate_w2"], np.float32)),
        "mla_wq_b": np.ascontiguousarray(np.asarray(inputs["mla_wq_b"], np.float32)),
        "mla_wkv_b": np.ascontiguousarray(np.asarray(inputs["mla_wkv_b"], np.float32)),
        "w_out": np.ascontiguousarray(np.asarray(inputs["w_out"], np.float32)),
        "w_gate": np.ascontiguousarray(np.asarray(inputs["w_gate"], np.float32)),
        "w_up": np.ascontiguousarray(np.asarray(inputs["w_up"], np.float32)),
        "w_down": np.ascontiguousarray(np.asarray(inputs["w_down"], np.float32)),
    }
    work = {0: 0, 4: 1} if B == 2 else {c: c for c in range(B)}
    zeros = {k: np.zeros_like(v) for k, v in shared.items()}
    zx = np.zeros((D, T), np.float32)
    zp = np.zeros((1, T), np.int32)
    in_maps = []
    for c in range(NCORES):
        if c in work:
            b = work[c]
            m = dict(shared)
            m["xT"] = np.ascontiguousarray(x[b].T)
            m["pos"] = np.ascontiguousarray(pos[b].reshape(1, T))
        else:
            m = dict(zeros)
            m["xT"] = zx
            m["pos"] = zp
        in_maps.append(m)
    res = run_bass_kernel_spmd(nc, in_maps, core_ids=list(range(NCORES)))
    inv = {b: c for c, b in work.items()}
    out = np.stack([np.ascontiguousarray(res.results[inv[b]]["yT"].T) for b in range(B)], axis=0)
    return out.astype(np.float32)


def kernel(**inputs):
    x = inputs["x"]
    B, T, _ = x.shape
    L = inputs["w_in"].shape[0]
    return run(inputs, T, L, B)
```

```python
import contextlib
import os
import numpy as np
import concourse.bass as bass
import concourse.mybir as mybir
from concourse.bass_utils import run_bass_kernel_spmd

F32 = mybir.dt.float32
BF16 = mybir.dt.bfloat16
I32 = mybir.dt.int32
AF = mybir.ActivationFunctionType
ALU = mybir.AluOpType

D = 2048
DIN = 4688
DFF = 5632
TT = 512
EPS = 1e-6
O_GQ, O_GK, O_GV, O_GLOW, O_GOUT, O_HQ, O_HF, O_HI, O_HOUT, O_QC, O_KVC, O_KPE = (
    0, 256, 512, 1024, 1040, 1552, 2064, 2576, 3088, 3600, 4112, 4624)
NPL = 88
MAGIC = 12582912.0
NCORES = 8


class Res:
    __slots__ = ("n", "w", "r")

    def __init__(self, n):
        self.n = n
        self.w = {}
        self.r = {}


def _merge(d, k, v):
    if d.get(k, 0) < v:
        d[k] = v


class Sched:
    def __init__(self, nc, es):
        self.nc = nc
        self.E = {"pe": nc.tensor, "act": nc.scalar, "dve": nc.vector, "pool": nc.gpsimd, "sp": nc.sync}
        self.sem = {}
        self.cnt = {}
        for e in ("pe", "act", "dve", "pool"):
            self.sem[e] = es.enter_context(nc.semaphore("s_" + e))
            self.cnt[e] = 0
        self.dslots = {"sp": 12, "pool": 8}
        self.dnext = {"sp": 0, "pool": 0}
        for q, n in self.dslots.items():
            for i in range(n):
                self.sem[(q, i)] = es.enter_context(nc.semaphore("d_%s%d" % (q, i)))
                self.cnt[(q, i)] = 0
        self.waited = {e: {} for e in self.E}
        self.nwaits = 0
        self.nins = 0

    def _deps(self, reads, writes):
        deps = {}
        for r in reads:
            for k, v in r.w.items():
                _merge(deps, k, v)
        for w in writes:
            for k, v in w.w.items():
                _merge(deps, k, v)
            for k, v in w.r.items():
                _merge(deps, k, v)
        return deps

    def _wait(self, e, deps):
        wd = self.waited[e]
        for k, v in deps.items():
            if k == "pe" and e == "pe":
                continue
            if wd.get(k, 0) >= v:
                continue
            self.E[e].wait_ge(self.sem[k], v)
            wd[k] = v
            self.nwaits += 1

    def op(self, e, fn, reads=(), writes=()):
        self._wait(e, self._deps(reads, writes))
        ins = fn(self.E[e])
        self.cnt[e] += 1
        c = self.cnt[e]
        ins.then_inc(self.sem[e], 1)
        self.nins += 1
        for r in reads:
            _merge(r.r, e, c)
        for w in writes:
            w.w = {e: c}
            w.r = {}

    def dma(self, q, out, in_, reads=(), writes=()):
        i = self.dnext[q]
        self.dnext[q] = (i + 1) % self.dslots[q]
        key = (q, i)
        deps = self._deps(reads, writes)
        if self.cnt[key] > 0:
            _merge(deps, key, 16 * self.cnt[key])
        self._wait(q, deps)
        ins = self.E[q].dma_start(out=out, in_=in_)
        self.cnt[key] += 1
        v = 16 * self.cnt[key]
        ins.then_inc(self.sem[key], 16)
        self.nins += 1
        for r in reads:
            _merge(r.r, key, v)
        for w in writes:
            w.w = {key: v}
            w.r = {}

    def fence(self, old, new):
        d = {}
        for o in old:
            for k, v in o.w.items():
                _merge(d, k, v)
            for k, v in o.r.items():
                _merge(d, k, v)
        for n in new:
            n.w = dict(d)
            n.r = {}

    def fence_merge(self, old, new):
        d = {}
        for o in old:
            for k, v in o.w.items():
                _merge(d, k, v)
            for k, v in o.r.items():
                _merge(d, k, v)
        for n in new:
            for k, v in d.items():
                _merge(n.w, k, v)

    def drain(self, e="sp"):
        deps = {}
        for k, c in self.cnt.items():
            if c > 0:
                deps[k] = c * 16 if isinstance(k, tuple) else c
        self._wait(e, deps)


def build_program(T, L, dbg=False):
    NT = T // TT
    nc = bass.Bass("TRN2", target_bir_lowering=False)
    NPV = L * NPL + 2

    def din(name, shape, dt=F32):
        return nc.dram_tensor(name, list(shape), dt, kind="ExternalInput").ap()

    xT_in = din("xT", [D, T])
    pos_in = din("pos", [1, T], I32)
    pvec_in = din("pvec", [128, NPV])
    consts_in = din("consts", [128, 5 * 128])
    w_in = din("w_in", [L, D, DIN])
    w2_in = din("gla_gate_w2", [L, 16, 256])
    wq_in = din("mla_wq_b", [L, 512, 1536])
    wkv_in = din("mla_wkv_b", [L, 512, 2048])
    wo_in = din("w_out", [L, D, D])
    wg_in = din("w_gate", [L, D, DFF])
    wu_in = din("w_up", [L, D, DFF])
    wd_in = din("w_down", [L, DFF, D])
    yT_out = nc.dram_tensor("yT", [D, T], F32, kind="ExternalOutput").ap()

    xmid_d = nc.dram_tensor("xmid_d", [D, T], F32).ap()
    xres_d = nc.dram_tensor("xres_d", [D, T], F32).ap()
    kn_d = nc.dram_tensor("kn_d", [8, 128, T], BF16).ap()
    kpe_d = nc.dram_tensor("kpe_d", [128, T], BF16).ap()
    vd_d = nc.dram_tensor("vd_d", [8, NT, 128, 4, 128], BF16).ap()
    cc_d = nc.dram_tensor("cc_d", [64, T], F32).ap()
    ss_d = nc.dram_tensor("ss_d", [64, T], F32).ap()
    NUNIT = 80
    wcache = nc.dram_tensor("wcache", [NUNIT, 128, 6144], BF16).ap()

    es = contextlib.ExitStack()
    with es:
        S = Sched(nc, es)

        def sb(name, shape, dt):
            return es.enter_context(nc.sbuf_tensor(name, list(shape), dt))

        HY = sb("HY", [128, 16, TT], BF16)
        WB = [sb("WB%d" % i, [128, 6144], BF16) for i in range(4)]
        WSM = sb("WSM", [128, 16, 80], BF16)
        XS = sb("XS", [128, 4, TT], F32)
        SQ = sb("SQ", [128, 3, TT], BF16)
        RS = sb("RS", [128, 3, TT], F32)
        SG = sb("SG", [128, 6, 128], F32)
        SBs = sb("SBs", [128, 6, 128], BF16)
        CS = sb("CS", [128, 2, TT], F32)
        CONF = sb("CONF", [128, 5 * 128], F32)
        IDENT = sb("IDENT", [128, 128], BF16)
        ONESB = sb("ONESB", [128, 128], BF16)
        MASKC = sb("MASKC", [128, 128], BF16)
        PERM = sb("PERM", [128, 64], BF16)
        ONEF = sb("ONEF", [128, TT], F32)
        PV = sb("PV", [128, NPV], F32)
        NEGB = sb("NEGB", [128, L, 2], F32)
        LBt = sb("LBt", [128, 4, L], F32)
        OMLt = sb("OMLt", [128, 4, L], F32)
        SMT = sb("SMT", [128, 4, 8], F32)
        W2 = sb("W2", [16, L, 256], BF16)
        GLOW = sb("GLOW", [16, TT], BF16)
        DEC = sb("DEC", [128, 6, 8], F32)
        WG = sb("WG", [128, 6, 8], F32)
        SI = sb("SI", [128, 6, 8], F32)
        CT = sb("CT", [128, 6, 8], F32)
        RA = sb("RA", [128, 12288], F32)
        RB = sb("RB", [128, 12288], F32)

        def view(reg, off, words, dt, pat=None, **kw):
            a = reg[:, off:off + words]
            if dt == BF16:
                a = a.bitcast(BF16)
            if pat:
                a = a.rearrange(pat, **kw)
            return a

        QF = view(RA, 0, 1536, BF16, "p (a t) -> p a t", a=6)
        KF = view(RA, 1536, 1536, BF16, "p (a t) -> p a t", a=6)
        BFv = view(RA, 3072, 3072, F32, "p (a t) -> p a t", a=6)
        VG = view(RA, 6144, 1024, BF16, "p (a t) -> p a t", a=4)
        VH = view(RA, 7168, 1024, BF16, "p (a t) -> p a t", a=4)
        QN = view(RA, 8192, 1024, BF16, "p (a t) -> p a t", a=4)
        CN = view(RA, 9216, 1024, BF16, "p (a t) -> p a t", a=4)
        GS = view(RA, 10240, 2048, BF16, "p (a t) -> p a t", a=8)
        MT = view(RA, 0, 8192, F32, "p (a t) -> p a t", a=16)
        HID = view(RA, 0, 11264, BF16, "p (a t) -> p a t", a=44)
        QT = view(RB, 0, 1536, BF16, "p (a t) -> p a t", a=6)
        KT = view(RB, 1536, 1536, BF16, "p (a t) -> p a t", a=6)
        KTT = view(RB, 3072, 1536, BF16, "p (a s d) -> p a s d", a=6, s=4)
        ET = view(RB, 4608, 1536, F32, "p (a t) -> p a t", a=3)
        OT = view(RB, 6144, 4096, F32, "p (a t) -> p a t", a=8)
        ATMv = view(RB, 10240, 512, BF16, "p (a t) -> p a t", a=8)
        K2 = view(RB, 10752, 1536, BF16, "p (a t) -> p a t", a=6)
        QNOPE = view(RB, 0, 2048, BF16, "p (a t) -> p a t", a=8)
        QPE = view(RB, 2048, 2048, BF16, "p (a t) -> p a t", a=8)
        KS = view(RB, 4096, 512, BF16, "p (a t) -> p a t", a=2)
        KPS = view(RB, 4608, 512, BF16, "p (a t) -> p a t", a=2)
        VS = view(RB, 5120, 512, BF16, "p (a b e) -> p a b e", a=2, b=4)
        PT = view(RB, 5632, 1024, BF16, "p (a t) -> p a t", a=4)
        KSL = [KS[:, 0, :], KS[:, 1, :]]
        KPSL = [KPS[:, 0, :], KPS[:, 1, :]]
        VSL = [VS[:, 0, :, :], VS[:, 1, :, :]]
        for _k in range(2):
            _b = 6656 + _k * 768
            KSL.append(view(RB, _b, 256, BF16))
            KPSL.append(view(RB, _b + 256, 256, BF16))
            VSL.append(view(RB, _b + 512, 256, BF16, "p (b e) -> p b e", b=4))
        KTO = view(RB, 6656, 512, BF16, "p (a t) -> p a t", a=2)
        VTO = view(RB, 7168, 1024, BF16, "p (a h e) -> p a h e", a=2, h=8)
        OMLA = view(RB, 8192, 4096, F32, "p (a t) -> p a t", a=8)
        FT = view(RB, 0, 8192, F32, "p (a t) -> p a t", a=16)
        XB = view(RB, 0, 8192, F32, "p (a t) -> p a t", a=16)

        PA = [es.enter_context(nc.psum_tensor("PA%d" % i, [128, TT], F32)) for i in range(6)]
        PSTAT = es.enter_context(nc.psum_tensor("PSTAT", [128, TT], F32))
        PTR = es.enter_context(nc.psum_tensor("PTR", [128, 8, 128], BF16))

        def R(n):
            return Res(n)

        def RL(n, k):
            return [Res("%s%d" % (n, i)) for i in range(k)]

        r_PA = RL("PA", 6)
        r_PSTAT = R("PSTAT")
        r_PTR = RL("PTR", 8)
        r_WB = RL("WB", 4)
        r_WSM = R("WSM")
        r_XS = RL("XS", 4)
        r_SQ = RL("SQ", 3)
        r_RS = RL("RS", 3)
        r_SG = RL("SG", 6)
        r_SBs = RL("SBs", 6)
        r_CS = R("CS")
        r_const = R("const")
        r_PV = R("PV")
        r_GLOW = R("GLOW")
        r_CH = R("CH")
        r_h = RL("h", 16)
        r_y = RL("y", 16)
        r_u = RL("u", 16)
        r_QF, r_KF, r_BF = RL("QF", 6), RL("KF", 6), RL("BF", 6)
        r_VG, r_VH = RL("VG", 4), RL("VH", 4)
        r_QN, r_CN = RL("QN", 4), RL("CN", 4)
        r_GS = RL("GS", 8)
        r_MT = RL("MT", 16)
        r_HID = RL("HID", 44)
        r_QT, r_KT = RL("QT", 6), RL("KT", 6)
        r_KTT = [RL("KTT%d_" % a, 4) for a in range(6)]
        r_ET = RL("ET", 3)
        r_OT = RL("OT", 8)
        r_ATM = RL("ATM", 8)
        r_K2 = RL("K2", 6)
        r_QNOPE, r_QPE = RL("QNOPE", 8), RL("QPE", 8)
        r_KS, r_KPS, r_VS = RL("KS", 4), RL("KPS", 4), RL("VS", 4)
        r_PT = RL("PT", 4)
        r_KTO, r_VTO = RL("KTO", 2), RL("VTO", 2)
        r_OMLA = RL("OMLA", 8)
        r_FT = RL("FT", 16)
        r_XB = R("XB")
        RA1 = r_QF + r_KF + r_BF + r_VG + r_VH + r_QN + r_CN + r_GS
        RA4 = r_MT
        RA5 = r_HID
        RB2 = r_QT + r_KT + sum(r_KTT, []) + r_ET + r_OT + r_ATM + r_K2
        RB3 = r_QNOPE + r_QPE + r_KS + r_KPS + r_VS + r_PT + r_KTO + r_VTO + r_OMLA
        RB5 = r_FT
        r_xmid = RL("xmid", NT)
        r_xres = RL("xres", NT)
        r_kv = RL("kv", NT)
        r_ccss = RL("ccss", NT)
        r_cache = RL("wcache", 80)

        rot = {"pa": 0, "ptr": 0, "xs": 0, "sq": 0, "rs": 0, "wb": 0, "et": 0, "pt": 0, "atm": 0, "ks": 0, "pa4": 0, "pa3": 0}

        def nxt(name, n):
            i = rot[name]
            rot[name] = (i + 1) % n
            return i

        def next_pa():
            i = nxt("pa", 6)
            return PA[i], r_PA[i]

        def next_wb():
            i = nxt("wb", 4)
            return WB[i], r_WB[i]

        def next_xs():
            i = nxt("xs", 4)
            return XS[:, i, :], r_XS[i]

        def next_sq():
            i = nxt("sq", 3)
            return SQ[:, i, :], r_SQ[i]

        def next_rs():
            i = nxt("rs", 3)
            return RS[:, i, :], r_RS[i]

        def mm(out, lhsT, rhs, start, stop, reads, writes):
            S.op("pe", lambda e: e.matmul(out, lhsT=lhsT, rhs=rhs, start=start, stop=stop), reads, writes)

        def act(out, in_, func, reads, writes, scale=None, bias=None):
            kw = {}
            if scale is not None:
                kw["scale"] = scale
            if bias is not None:
                kw["bias"] = bias
            S.op("act", lambda e: e.activation(out=out, in_=in_, func=func, **kw), reads, writes)

        def tt(out, in0, in1, op, reads, writes, eng="dve"):
            S.op(eng, lambda e: e.tensor_tensor(out=out, in0=in0, in1=in1, op=op), reads, writes)

        def ts(out, in0, s1, s2, op0, op1, reads, writes, eng="dve"):
            if op1 is None:
                S.op(eng, lambda e: e.tensor_scalar(out=out, in0=in0, scalar1=s1, scalar2=None, op0=op0), reads, writes)
            else:
                S.op(eng, lambda e: e.tensor_scalar(out=out, in0=in0, scalar1=s1, scalar2=s2, op0=op0, op1=op1),
                     reads, writes)

        def stt(out, in0, scalar, in1, op0, op1, reads, writes):
            S.op("dve", lambda e: e.scalar_tensor_tensor(out=out, in0=in0, scalar=scalar, in1=in1, op0=op0, op1=op1),
                 reads, writes)

        def cpy(out, in_, reads, writes, eng="dve"):
            if eng == "act":
                S.op("act", lambda e: e.copy(out=out, in_=in_), reads, writes)
            else:
                S.op(eng, lambda e: e.tensor_copy(out=out, in_=in_), reads, writes)

        def rstd_from_psum(ps, r_ps, dim):
            t1, r1 = next_rs()
            act(t1, ps[:, :], AF.Ln, [r_ps, r_const], [r1], scale=1.0 / dim, bias=EPSC[:, 0:1])
            t2, r2 = next_rs()
            act(t2, t1, AF.Exp, [r1], [r2], scale=-0.5)
            return t2, r2

        S.dma("sp", CONF[:], consts_in[:, :], [], [r_const])
        S.dma("sp", PV[:], pvec_in[:, :], [], [r_PV])
        for l in range(L):
            S.dma("pool", W2[:, l, :], w2_in[l], [], [r_const])
        cpy(IDENT[:], CONF[:, 0:128], [r_const], [r_const])
        cpy(MASKC[:], CONF[:, 256:384], [r_const], [r_const])
        cpy(PERM[:], CONF[:, 384:448], [r_const], [r_const])
        MASKB = CONF[:, 128:256]
        EPSC = CONF[:, 512:640]
        S.op("dve", lambda e: e.memset(ONESB[:], 1.0), [], [r_const])
        S.op("dve", lambda e: e.memset(ONEF[:], 1.0), [], [r_const])
        S.op("dve", lambda e: e.memset(SG[:], 0.0), [], r_SG)
        pvl = PV[:, 0:L * NPL].rearrange("p (l c) -> p l c", c=NPL)
        ts(NEGB[:], pvl[:, :, 82:84], -1.0, None, ALU.mult, None, [r_PV], [r_PV])
        lg = pvl[:, :, 84:88].rearrange("p l t -> p t l")
        S.op("dve", lambda e: e.tensor_reduce(out=SMT[:, :, 0:1], in_=lg, axis=mybir.AxisListType.X, op=ALU.max),
             [r_PV], [r_CH])
        tt(LBt[:], lg, SMT[:, :, 0:1].broadcast_to([128, 4, L]), ALU.subtract, [r_PV, r_CH], [r_PV])
        act(LBt[:], LBt[:], AF.Exp, [r_PV], [r_PV])
        S.op("dve", lambda e: e.tensor_reduce(out=SMT[:, :, 1:2], in_=LBt[:], axis=mybir.AxisListType.X, op=ALU.add),
             [r_PV], [r_CH])
        S.op("dve", lambda e: e.reciprocal(out=SMT[:, :, 2:3], in_=SMT[:, :, 1:2]), [r_CH], [r_CH])
        tt(LBt[:], LBt[:], SMT[:, :, 2:3].broadcast_to([128, 4, L]), ALU.mult, [r_PV, r_CH], [r_PV])
        cpy(SMT[:, :, 3:4], LBt[:, :, 0:1], [r_PV], [r_CH])
        for l in range(1, L):
            tt(LBt[:, :, l:l + 1], LBt[:, :, l:l + 1], LBt[:, :, l - 1:l], ALU.add, [r_PV], [r_PV])
        tt(LBt[:], LBt[:], SMT[:, :, 3:4].broadcast_to([128, 4, L]), ALU.subtract, [r_PV, r_CH], [r_PV])
        ts(OMLt[:], LBt[:], -1.0, 1.0, ALU.mult, ALU.add, [r_PV], [r_PV])

        INVF = PV[0:64, L * NPL:L * NPL + 1]
        SGN = PV[0:64, L * NPL + 1:L * NPL + 2]
        for ti in range(NT):
            t0 = ti * TT
            xi, rxi = next_xs()
            S.dma("sp", xi[0:64, :].bitcast(I32), pos_in[:, t0:t0 + TT].partition_broadcast(64), [], [rxi])
            rr, rrr = next_xs()
            cpy(rr[0:64, :], xi[0:64, :].bitcast(I32), [rxi], [rrr])
            ts(rr[0:64, :], rr[0:64, :], INVF, 1.0 / (2.0 * np.pi), ALU.mult, ALU.mult, [rrr, r_PV], [rrr])
            for which in range(2):
                a, ra = next_rs()
                b, rb = next_rs()
                if which == 0:
                    ts(a[0:64, :], rr[0:64, :], 0.25, None, ALU.add, None, [rrr], [ra])
                    src = a
                    rsrc = ra
                else:
                    src = rr
                    rsrc = rrr
                ts(b[0:64, :], src[0:64, :], MAGIC, None, ALU.add, None, [rsrc], [rb])
                ts(b[0:64, :], b[0:64, :], MAGIC, None, ALU.subtract, None, [rb], [rb])
                tt(b[0:64, :], src[0:64, :], b[0:64, :], ALU.subtract, [rsrc, rb], [rb])
                o, ro = next_xs()
                act(o[0:64, :], b[0:64, :], AF.Sin, [rb], [ro], scale=6.283185)
                if which == 1:
                    ts(o[0:64, :], o[0:64, :], SGN, None, ALU.mult, None, [ro, r_PV], [ro])
                    S.dma("sp", ss_d[:, t0:t0 + TT], o[0:64, :], [ro], [r_ccss[ti]])
                else:
                    S.dma("sp", cc_d[:, t0:t0 + TT], o[0:64, :], [ro], [r_ccss[ti]])

        zt, rzt = next_sq()
        S.op("dve", lambda e: e.memset(zt, 0.0), [], [rzt])
        for ti in range(NT):
            S.dma("sp", kpe_d[64:128, ti * TT:(ti + 1) * TT], zt[0:64, :], [rzt], [r_kv[ti]])

        ucnt = [0]
        cur_ti = [0]

        def load_w(dst, rdst, src, wb):
            u = ucnt[0]
            ucnt[0] += 1
            n = 1
            for dd in dst.shape[1:]:
                n *= dd
            flat = wb[:, 0:n]
            if cur_ti[0] == 0 or NT == 1:
                S.dma("pool", dst, src, [], [rdst])
                if NT > 1:
                    S.dma("sp", wcache[u, :, 0:n], flat, [rdst], [r_cache[u]])
            else:
                S.dma("pool", flat, wcache[u, :, 0:n], [r_cache[u]], [rdst])

        def rms_stats_from_dram(src_d, r_src, t0):
            for kc in range(16):
                xb, rx = next_xs()
                S.dma("sp", xb, src_d[kc * 128:(kc + 1) * 128, t0:t0 + TT], [r_src], [rx])
                sq, rsq = next_sq()
                act(sq, xb, AF.Square, [rx], [rsq])
                mm(PSTAT[:, :], ONESB[:, :], sq, kc == 0, kc == 15, [rsq, r_const], [r_PSTAT])
            return rstd_from_psum(PSTAT, r_PSTAT, float(D))

        def normed_from_dram(src_d, r_src, t0, gcol, dst_res, rstd, r_rstd):
            for kc in range(16):
                xb, rx = next_xs()
                S.dma("sp", xb, src_d[kc * 128:(kc + 1) * 128, t0:t0 + TT], [r_src], [rx])
                stt(HY[:, kc, :], xb, PV[:, gcol + kc:gcol + kc + 1], rstd, ALU.mult, ALU.mult,
                    [rx, r_PV, r_rstd], [dst_res[kc]])

        def proj_fm(wsrc_cols, kdim_chunks, rhs_fn, rhs_res, nchunk, evac):
            for c in range(nchunk):
                ps, rps = next_pa()
                for kc in range(kdim_chunks):
                    lhsT, rw, m = wsrc_cols(kc, c)
                    mm(ps[0:m, :], lhsT, rhs_fn(kc), kc == 0, kc == kdim_chunks - 1,
                       [rw, rhs_res[kc]], [rps])
                evac(c, ps, rps)

        def prologue(pl, pti):
            psrc = xT_in if pl == 0 else xres_d
            pres = Res("xin") if pl == 0 else r_xres[pti]
            pt0 = pti * TT
            S.fence(r_u + r_y, r_h)
            prstd, pr_rstd = rms_stats_from_dram(psrc, pres, pt0)
            normed_from_dram(psrc, pres, pt0, pl * NPL + 0, r_h, prstd, pr_rstd)
            S.dma("sp", CS[0:64, 0, :], cc_d[:, pt0:pt0 + TT], [r_ccss[pti]], [r_CS])
            S.dma("sp", CS[0:64, 1, :], ss_d[:, pt0:pt0 + TT], [r_ccss[pti]], [r_CS])

        for l in range(L):
            pb = l * NPL
            src_d = xT_in if l == 0 else xres_d
            r_src_t = None if l == 0 else r_xres
            dst_d = yT_out if l == L - 1 else xres_d
            winv = w_in[l].rearrange("(kc p) n -> p kc n", p=128)
            wqv = wq_in[l].rearrange("(kc p) n -> p kc n", p=128)
            wkvv = wkv_in[l].rearrange("(kc p) n -> p kc n", p=128)
            wov = wo_in[l].rearrange("(kc p) n -> p kc n", p=128)
            wgv = wg_in[l].rearrange("(kc p) n -> p kc n", p=128)
            wuv = wu_in[l].rearrange("(kc p) n -> p kc n", p=128)
            wdv = wd_in[l].rearrange("(kc p) n -> p kc n", p=128)
            S.op("dve", lambda e: e.memset(SG[:], 0.0), [], r_SG)

            for ti in range(NT):
                t0 = ti * TT
                ucnt[0] = 0
                cur_ti[0] = ti
                r_src = [] if r_src_t is None else [r_src_t[ti]]
                rsrc1 = r_src[0] if r_src else Res("xin")

                if l == 0 and ti == 0:
                    prologue(0, 0)
                S.fence(RA5 + RA4, RA1)

                def hrhs(kc):
                    return HY[:, kc, :]

                wsm_c = wcache[79, :, 0:1280].rearrange("p (k n) -> p k n", k=16)
                if ti == 0 or NT == 1:
                    S.dma("pool", WSM[:, :, 0:16], winv[:, :, O_GLOW:O_GLOW + 16], [], [r_WSM])
                    S.dma("pool", WSM[:, :, 16:80], winv[:, :, O_KPE:O_KPE + 64], [], [r_WSM])
                    if NT > 1:
                        S.dma("sp", wsm_c, WSM[:, :, :], [r_WSM], [r_cache[79]])
                else:
                    S.dma("pool", WSM[:, :, :], wsm_c, [r_cache[79]], [r_WSM])

                def load_group(col0, ncols):
                    wb, rwb = next_wb()
                    wv = wb[:, 0:16 * ncols].rearrange("p (k n) -> p k n", k=16)
                    load_w(wv, rwb, winv[:, :, col0:col0 + ncols], wb)
                    return wv, rwb

                def fm_group(col0, nchunk, evac):
                    c = 0
                    while c < nchunk:
                        g = min(3, nchunk - c)
                        wv, rwb = load_group(col0 + c * 128, g * 128)
                        base = c
                        proj_fm(lambda kc, cc, wv=wv, rwb=rwb: (wv[:, kc, cc * 128:(cc + 1) * 128], rwb, 128),
                                16, hrhs, r_h, g, lambda cc, ps, rps, base=base: evac(base + cc, ps, rps))
                        c += g

                def tm_group(col0, dstv, dres):
                    for half in range(2):
                        wv, rwb = load_group(col0 + half * 256, 256)
                        for sub in range(4):
                            ps, rps = next_pa()
                            for kc in range(16):
                                mm(ps[:, 0:256], HY[:, kc, sub * 128:(sub + 1) * 128], wv[:, kc, :],
                                   kc == 0, kc == 15, [rwb, r_h[kc]], [rps])
                            cpy(dstv[:, sub, half * 256:(half + 1) * 256], ps[:, 0:256], [rps], [dres[sub]],
                                eng="act")

                def ev_gqk(c, ps, rps):
                    if c < 2:
                        act(QF[:, c, :], ps[:, :], AF.Copy, [rps], [r_QF[c]], scale=0.125)
                    else:
                        cpy(KF[:, c - 2, :], ps[:, :], [rps], [r_KF[c - 2]])
                fm_group(O_GQ, 4, ev_gqk)
                tm_group(O_GV, VG, r_VG)
                ps, rps = next_pa()
                for kc in range(16):
                    mm(ps[0:16, :], WSM[:, kc, 0:16], HY[:, kc, :], kc == 0, kc == 15, [r_WSM, r_h[kc]], [rps])
                cpy(GLOW[:, :], ps[0:16, :], [rps], [r_GLOW], eng="act")
                for c in range(2):
                    ps, rps = next_pa()
                    mm(ps[:, :], W2[:, l, c * 128:(c + 1) * 128], GLOW[:, :], True, True, [r_const, r_GLOW], [rps])
                    e1, re1 = next_rs()
                    act(e1, ps[:, :], AF.Exp, [rps, r_PV], [re1], scale=-1.0, bias=NEGB[:, l, c:c + 1])
                    e2, re2 = next_rs()
                    act(e2, e1, AF.Ln, [re1, r_const], [re2], bias=EPSC[:, 1:2])
                    S.op("dve", lambda e, e2=e2, c=c: e.tensor_tensor_scan(
                        out=BFv[:, c, :], data0=ONEF[:, :], data1=e2, initial=0.0, op0=ALU.mult, op1=ALU.subtract),
                        [re2, r_const], [r_BF[c]])
                fm_group(O_GOUT, 4, lambda c, ps, rps: act(GS[:, c, :], ps[:, :], AF.Silu, [rps], [r_GS[c]]))
                fm_group(O_HQ, 4, lambda c, ps, rps: act(QF[:, 2 + c, :], ps[:, :], AF.Silu, [rps], [r_QF[2 + c]]))

                def ev_hf(c, ps, rps):
                    sg, rsg = next_rs()
                    act(sg, ps[:, :], AF.Sigmoid, [rps], [rsg])
                    f, rf = next_rs()
                    ts(f, sg, OMLt[:, c, l:l + 1], LBt[:, c, l:l + 1], ALU.mult, ALU.add, [rsg, r_PV], [rf])
                    ts(KF[:, 2 + c, :], f, -1.0, 1.0, ALU.mult, ALU.add, [rf], [r_KF[2 + c]])
                    lf, rlf = next_rs()
                    act(lf, f, AF.Ln, [rf], [rlf])
                    S.op("dve", lambda e, lf=lf, c=c: e.tensor_tensor_scan(
                        out=BFv[:, 2 + c, :], data0=ONEF[:, :], data1=lf, initial=0.0, op0=ALU.mult, op1=ALU.add),
                        [rlf, r_const], [r_BF[2 + c]])
                fm_group(O_HF, 4, ev_hf)
                tm_group(O_HI, VH, r_VH)
                fm_group(O_HOUT, 4, lambda c, ps, rps: act(GS[:, 4 + c, :], ps[:, :], AF.Silu, [rps], [r_GS[4 + c]]))

                def latent(col0, gcol, dstv, dres):
                    tmp = []

                    def ev(c, ps, rps):
                        xb, rx = next_xs()
                        cpy(xb, ps[:, :], [rps], [rx], eng="act")
                        sq, rsq = next_sq()
                        act(sq, ps[:, :], AF.Square, [rps], [rsq])
                        mm(PSTAT[:, :], ONESB[:, :], sq, c == 0, c == 3, [rsq, r_const], [r_PSTAT])
                        tmp.append((xb, rx))
                    c = 0
                    wv, rwb = load_group(col0, 384)
                    proj_fm(lambda kc, cc: (wv[:, kc, cc * 128:(cc + 1) * 128], rwb, 128), 16, hrhs, r_h, 3, ev)
                    wv2, rwb2 = load_group(col0 + 384, 128)
                    proj_fm(lambda kc, cc: (wv2[:, kc, 0:128], rwb2, 128), 16, hrhs, r_h, 1,
                            lambda cc, ps, rps: ev(3, ps, rps))
                    rstd2, r_rstd2 = rstd_from_psum(PSTAT, r_PSTAT, 512.0)
                    for c in range(4):
                        xb, rx = tmp[c]
                        stt(dstv[:, c, :], xb, PV[:, gcol + c:gcol + c + 1], rstd2, ALU.mult, ALU.mult,
                            [rx, r_PV, r_rstd2], [dres[c]])
                latent(O_QC, pb + 64, QN, r_QN)
                latent(O_KVC, pb + 68, CN, r_CN)

                S.fence(RB5 + RB3 + [r_XB], RB2)
                ps, rps = next_pa()
                for kc in range(16):
                    mm(ps[0:64, :], WSM[:, kc, 16:80], HY[:, kc, :], kc == 0, kc == 15, [r_WSM, r_h[kc]], [rps])

                def rope(ps, rps, dst, rdst, scale):
                    qb, rqb = next_sq()
                    act(qb[0:64, :], ps[0:64, :], AF.Copy, [rps], [rqb], scale=scale)
                    ps2, rps2 = next_pa()
                    mm(ps2[0:64, :], PERM[0:64, :], qb[0:64, :], True, True, [r_const, rqb], [rps2])
                    a, ra = next_xs()
                    tt(a[0:64, :], qb[0:64, :], CS[0:64, 0, :], ALU.mult, [rqb, r_CS], [ra])
                    b, rb = next_xs()
                    tt(b[0:64, :], ps2[0:64, :], CS[0:64, 1, :], ALU.mult, [rps2, r_CS], [rb])
                    tt(dst, a[0:64, :], b[0:64, :], ALU.add, [ra, rb], [rdst])
                kpo, rkpo = next_sq()
                rope(ps, rps, kpo[0:64, :], rkpo, 1.0)
                S.dma("sp", kpe_d[0:64, t0:t0 + TT], kpo[0:64, :], [rkpo], [r_kv[ti]])

                for (a0, a1, sc) in ((0, 2, 1.0 / 16.0), (2, 6, 1.0)):
                    BL = BFv[:, a0:a1, 63:TT:64]
                    BM = BFv[:, a0:a1, 31:TT:64]
                    rb_ = r_BF[a0:a1]
                    S.op("dve", lambda e, a0=a0, a1=a1: e.memset(CT[:, a0:a1, 0:1], 0.0), [], [r_CH])
                    cpy(CT[:, a0:a1, 1:8], BFv[:, a0:a1, 63:TT - 64:64], rb_, [r_CH])
                    tt(DEC[:, a0:a1, :], BL, CT[:, a0:a1, :], ALU.subtract, rb_ + [r_CH], [r_CH])
                    act(DEC[:, a0:a1, :], DEC[:, a0:a1, :], AF.Exp, [r_CH], [r_CH], scale=sc)
                    tt(WG[:, a0:a1, :], BL, BM, ALU.subtract, rb_, [r_CH])
                    act(WG[:, a0:a1, :], WG[:, a0:a1, :], AF.Exp, [r_CH], [r_CH], scale=sc)
                    tt(SI[:, a0:a1, :], BM, CT[:, a0:a1, :], ALU.subtract, rb_ + [r_CH], [r_CH])
                    act(SI[:, a0:a1, :], SI[:, a0:a1, :], AF.Exp, [r_CH], [r_CH], scale=sc)
                for a in range(6):
                    sc = 1.0 / 16.0 if a < 2 else 1.0
                    i = nxt("et", 3)
                    arg, rarg = ET[:, i, :], r_ET[i]
                    tt(arg.rearrange("p (c t) -> p c t", c=8), BFv[:, a, :].rearrange("p (c t) -> p c t", c=8),
                       BFv[:, a, 31:TT:64].unsqueeze(2).broadcast_to([128, 8, 64]), ALU.subtract, [r_BF[a]], [rarg])
                    i = nxt("et", 3)
                    e1, re1 = ET[:, i, :], r_ET[i]
                    act(e1, arg, AF.Exp, [rarg], [re1], scale=sc)
                    tt(QT[:, a, :], QF[:, a, :], e1, ALU.mult, [r_QF[a], re1], [r_QT[a]])
                    i = nxt("et", 3)
                    e2, re2 = ET[:, i, :], r_ET[i]
                    act(e2, arg, AF.Exp, [rarg], [re2], scale=-sc)
                    tt(KT[:, a, :], KF[:, a, :], e2, ALU.mult, [r_KF[a], re2], [r_KT[a]])
                    tt(K2[:, a, :].rearrange("p (c t) -> p c t", c=8), KT[:, a, :].rearrange("p (c t) -> p c t", c=8),
                       WG[:, a, :].unsqueeze(2).broadcast_to([128, 8, 64]), ALU.mult, [r_KT[a], r_CH], [r_K2[a]])
                    for sub in range(4):
                        S.op("pe", lambda e, a=a, sub=sub: e.transpose(
                            PTR[:, sub, :], K2[:, a, sub * 128:(sub + 1) * 128], IDENT[:, :]),
                            [r_K2[a], r_const], [r_PTR[0]])
                    cpy(KTT[:, a, :, :], PTR[:, 0:4, :], [r_PTR[0]], r_KTT[a], eng="act")

                def rot3():
                    k = nxt("pa3", 3)
                    return ((PA[4], r_PA[4]), (PA[5], r_PA[5]), (PSTAT, r_PSTAT))[k]

                for sub in range(4):
                    tcs = slice(sub * 128, (sub + 1) * 128)
                    for wave in range(2):
                        hinfo = []
                        if wave == 0:
                            for a in range(2):
                                for k in range(2):
                                    hh = 2 * a + k
                                    hinfo.append((a, hh, 64 * k, 64, VG, r_VG, slice(hh * 128, (hh + 1) * 128), hh))
                            alist = [0, 1]
                        else:
                            for hh in range(4):
                                hinfo.append((2 + hh, 4 + hh, 0, 128, VH, r_VH, slice(hh * 128, (hh + 1) * 128), hh))
                            alist = [2, 3, 4, 5]
                        for (a, oh, p0, dk, Vt, rV, vcol, bk) in hinfo:
                            psA, rpsA = rot3()
                            mm(psA[:, 0:128], KT[p0:p0 + dk, a, tcs], QT[p0:p0 + dk, a, tcs], True, True,
                               [r_KT[a], r_QT[a]], [rpsA])
                            tt(ATMv[:, oh, :], psA[:, 0:128], MASKB, ALU.mult, [rpsA, r_const], [r_ATM[oh]])
                        for (a, oh, p0, dk, Vt, rV, vcol, bk) in hinfo:
                            mm(PA[bk][:, 0:128], Vt[:, sub, vcol], ATMv[:, oh, :], True, False,
                               [rV[sub], r_ATM[oh]], [r_PA[bk]])
                        for half in range(2):
                            c = sub * 2 + half
                            pr = slice(half * 64, half * 64 + 64)
                            qcs = slice(c * 64, (c + 1) * 64)
                            for a in alist:
                                S.op("act", lambda e, a=a, c=c: e.activation(
                                    out=SBs[:, a, :], in_=SG[:, a, :], func=AF.Copy, scale=SI[:, a, c:c + 1]),
                                    [r_SG[a], r_CH], [r_SBs[a]])
                            for (a, oh, p0, dk, Vt, rV, vcol, bk) in hinfo:
                                mm(PA[bk][:, half * 64:half * 64 + 64], SBs[p0:p0 + dk, a, :], QT[p0:p0 + dk, a, qcs],
                                   False, half == 1, [r_SBs[a], r_QT[a]], [r_PA[bk]])
                            for a in alist:
                                psU, rpsU = rot3()
                                if a < 2:
                                    mm(psU[:, 0:256], KTT[pr, a, sub, :], VG[pr, sub, a * 256:(a + 1) * 256], True, True,
                                       [r_KTT[a][sub], r_VG[sub]], [rpsU])
                                else:
                                    mm(psU[:, 0:128], KTT[pr, a, sub, :], VH[pr, sub, (a - 2) * 128:(a - 1) * 128],
                                       True, True, [r_KTT[a][sub], r_VH[sub]], [rpsU])
                                if a < 2:
                                    stt(SG[0:64, a, :], SG[0:64, a, :], DEC[0:64, a, c:c + 1], psU[0:64, 0:128],
                                        ALU.mult, ALU.add, [r_SG[a], r_CH, rpsU], [r_SG[a]])
                                    stt(SG[64:128, a, :], SG[64:128, a, :], DEC[64:128, a, c:c + 1], psU[64:128, 128:256],
                                        ALU.mult, ALU.add, [r_SG[a], r_CH, rpsU], [r_SG[a]])
                                else:
                                    stt(SG[:, a, :], SG[:, a, :], DEC[:, a, c:c + 1], psU[:, 0:128], ALU.mult, ALU.add,
                                        [r_SG[a], r_CH, rpsU], [r_SG[a]])
                        for (a, oh, p0, dk, Vt, rV, vcol, bk) in hinfo:
                            cpy(OT[:, oh, tcs], PA[bk][:, 0:128], [r_PA[bk]], [r_OT[oh]], eng="act")

                S.fence(r_h + r_u, r_y)
                for oh in range(8):
                    sq, rsq = next_sq()
                    act(sq, OT[:, oh, :], AF.Square, [r_OT[oh]], [rsq])
                    mm(PSTAT[:, :], ONESB[:, :], sq, True, True, [rsq, r_const], [r_PSTAT])
                    rstd2, r_rstd2 = rstd_from_psum(PSTAT, r_PSTAT, 128.0)
                    gcol = pb + (80 if oh < 4 else 81)
                    t1, rt1 = next_xs()
                    stt(t1, OT[:, oh, :], PV[:, gcol:gcol + 1], rstd2, ALU.mult, ALU.mult,
                        [r_OT[oh], r_PV, r_rstd2], [rt1])
                    tt(HY[:, oh, :], t1, GS[:, oh, :], ALU.mult, [rt1, r_GS[oh]], [r_y[oh]])

                S.fence(RB2 + RB5 + [r_XB], RB3)
                wb, rwb = next_wb()
                wq = wb[:, 0:6144].rearrange("p (k n) -> p k n", k=4)
                load_w(wq, rwb, wqv[:, :, :], wb)
                qscale = 192.0 ** -0.5
                S.op("dve", lambda e: e.memset(QPE[64:128, :, :], 0.0), [], r_QPE)
                for h in range(8):
                    ps, rps = next_pa()
                    for kc in range(4):
                        mm(ps[:, :], wq[:, kc, h * 192:h * 192 + 128], QN[:, kc, :], kc == 0, kc == 3,
                           [rwb, r_QN[kc]], [rps])
                    act(QNOPE[:, h, :], ps[:, :], AF.Copy, [rps], [r_QNOPE[h]], scale=qscale)
                    ps, rps = next_pa()
                    for kc in range(4):
                        mm(ps[0:64, :], wq[:, kc, h * 192 + 128:h * 192 + 192], QN[:, kc, :], kc == 0, kc == 3,
                           [rwb, r_QN[kc]], [rps])
                    rope(ps, rps, QPE[0:64, h, :], r_QPE[h], qscale)
                S.fence_merge(r_KS[2:4] + r_KPS[2:4] + r_VS[2:4], r_KTO + r_VTO)
                wkvs = []
                for half in range(2):
                    wb, rwb = next_wb()
                    wk = wb[:, 0:4096].rearrange("p (k n) -> p k n", k=4)
                    load_w(wk, rwb, wkvv[:, :, half * 1024:(half + 1) * 1024], wb)
                    wkvs.append((wk, rwb))
                for h in range(8):
                    wk, rwb = wkvs[h // 4]
                    hc = (h % 4) * 256
                    ps, rps = next_pa()
                    for kc in range(4):
                        mm(ps[:, :], wk[:, kc, hc:hc + 128], CN[:, kc, :], kc == 0, kc == 3, [rwb, r_CN[kc]], [rps])
                    i = h % 2
                    cpy(KTO[:, i, :], ps[:, :], [rps], [r_KTO[i]], eng="act")
                    S.dma("sp", kn_d[h, :, t0:t0 + TT], KTO[:, i, :], [r_KTO[i]], [r_kv[ti]])
                for sub in range(4):
                    i = sub % 2
                    for half in range(2):
                        wk, rwb = wkvs[half]
                        ps, rps = next_pa()
                        rhsv = wk.rearrange("p k (h c) -> p k h c", c=256)
                        for kc in range(4):
                            mm(ps[:, :].rearrange("p (h e) -> p h e", h=4), CN[:, kc, sub * 128:(sub + 1) * 128],
                               rhsv[:, kc, :, 128:256], kc == 0, kc == 3, [rwb, r_CN[kc]], [rps])
                        cpy(VTO[:, i, half * 4:(half + 1) * 4, :], ps[:, :].rearrange("p (h e) -> p h e", h=4),
                            [rps], [r_VTO[i]], eng="act")
                    S.dma("sp", vd_d[:, ti, :, sub, :].rearrange("h p e -> p h e"), VTO[:, i, :, :],
                          [r_VTO[i]], [r_kv[ti]])

                LOOK = int(os.environ.get('K_LOOK', '2'))
                S.fence_merge(r_KTO + r_VTO, r_KS[2:4] + r_KPS[2:4] + r_VS[2:4])
                for h in range(8):
                    psO, rpsO = PA[4], r_PA[4]
                    psD, rpsD = PA[5], r_PA[5]
                    blocks = [(kt, kb) for kt in range(ti + 1) for kb in range(4)]
                    loaded = {}

                    def ensure_loaded(kt, h=h, loaded=loaded):
                        if kt not in loaded:
                            i = nxt("ks", 4)
                            loaded[kt] = i
                            S.dma("sp", KSL[i], kn_d[h, :, kt * TT:(kt + 1) * TT], [r_kv[kt]], [r_KS[i]])
                            S.dma("sp", KPSL[i], kpe_d[:, kt * TT:(kt + 1) * TT], [r_kv[kt]], [r_KPS[i]])
                            S.dma("sp", VSL[i], vd_d[h, kt], [r_kv[kt]], [r_VS[i]])
                        return loaded[kt]

                    def emit_qk(bi, h=h):
                        kt, kb = blocks[bi]
                        i = ensure_loaded(kt)
                        q0 = kb * 128 if kt == ti else 0
                        qs = slice(q0, TT)
                        k4 = nxt("pa4", 4)
                        psS, rpsS = PA[k4], r_PA[k4]
                        mm(psS[:, qs], KSL[i][:, kb * 128:(kb + 1) * 128], QNOPE[:, h, qs], True, False,
                           [r_KS[i], r_QNOPE[h]], [rpsS])
                        mm(psS[:, qs], KPSL[i][:, kb * 128:(kb + 1) * 128], QPE[:, h, qs], False, True,
                           [r_KPS[i], r_QPE[h]], [rpsS])
                        j = nxt("pt", 4)
                        act(PT[:, j, qs], psS[:, qs], AF.Exp, [rpsS], [r_PT[j]])
                        if kt == ti:
                            tt(PT[:, j, q0:q0 + 128], PT[:, j, q0:q0 + 128], MASKC[:, :], ALU.mult,
                               [r_PT[j], r_const], [r_PT[j]])
                        return (i, j, qs, kb)

                    def emit_pv(bi, st):
                        i, j, qs, kb = st
                        first = bi == 0
                        last = bi == len(blocks) - 1
                        mm(psO[:, qs], VSL[i][:, kb, :], PT[:, j, qs], first, last, [r_VS[i], r_PT[j]], [rpsO])
                        mm(psD[:, qs], ONESB[:, :], PT[:, j, qs], first, last, [r_const, r_PT[j]], [rpsD])

                    pend = []
                    for bi in range(len(blocks) + LOOK):
                        if bi < len(blocks):
                            pend.append(emit_qk(bi))
                        if bi >= LOOK:
                            emit_pv(bi - LOOK, pend[bi - LOOK])
                    rd, rrd = next_rs()
                    S.op("dve", lambda e, rd=rd, psD=psD: e.reciprocal(out=rd, in_=psD[:, :]), [rpsD], [rrd])
                    tt(OMLA[:, h, :], psO[:, :], rd, ALU.mult, [rpsO, rrd], [r_OMLA[h]])
                for h in range(8):
                    sq, rsq = next_sq()
                    act(sq, OMLA[:, h, :], AF.Square, [r_OMLA[h]], [rsq])
                    mm(PSTAT[:, :], ONESB[:, :], sq, h == 0, h == 7, [rsq, r_const], [r_PSTAT])
                rstd3, r_rstd3 = rstd_from_psum(PSTAT, r_PSTAT, 1024.0)
                for h in range(8):
                    stt(HY[:, 8 + h, :], OMLA[:, h, :], PV[:, pb + 72 + h:pb + 73 + h], rstd3, ALU.mult, ALU.mult,
                        [r_OMLA[h], r_PV, r_rstd3], [r_y[8 + h]])

                S.fence(RA1 + RA5, RA4)
                S.fence(RB2 + RB3 + RB5, [r_XB])
                for q4 in range(4):
                    S.dma("sp", XB[:, 4 * q4:4 * q4 + 4, :],
                          src_d[q4 * 512:(q4 + 1) * 512, t0:t0 + TT].rearrange("(g p) t -> p g t", p=128),
                          [rsrc1], [r_XB])
                for g in range(16):
                    if g % 3 == 0:
                        gn = min(3, 16 - g)
                        wb, rwb = next_wb()
                        wv = wb[:, 0:16 * gn * 128].rearrange("p (k n) -> p k n", k=16)
                        load_w(wv, rwb, wov[:, :, g * 128:(g + gn) * 128], wb)
                        gbase = g
                    ps, rps = next_pa()
                    cc = g - gbase
                    for kc in range(16):
                        mm(ps[:, :], wv[:, kc, cc * 128:(cc + 1) * 128], HY[:, kc, :], kc == 0, kc == 15,
                           [rwb, r_y[kc]], [rps])
                    cpy(MT[:, g, :], ps[:, :], [rps], [r_MT[g]], eng="act")
                    sq, rsq = next_sq()
                    act(sq, ps[:, :], AF.Square, [rps], [rsq])
                    mm(PSTAT[:, :], ONESB[:, :], sq, g == 0, g == 15, [rsq, r_const], [r_PSTAT])
                rstd4, r_rstd4 = rstd_from_psum(PSTAT, r_PSTAT, float(D))
                for g in range(16):
                    stt(MT[:, g, :], MT[:, g, :], PV[:, pb + 16 + g:pb + 17 + g], rstd4, ALU.mult, ALU.mult,
                        [r_MT[g], r_PV, r_rstd4], [r_MT[g]])
                    tt(MT[:, g, :], MT[:, g, :], XB[:, g, :], ALU.add, [r_MT[g], r_XB], [r_MT[g]])
                    S.dma("sp", xmid_d[g * 128:(g + 1) * 128, t0:t0 + TT], MT[:, g, :], [r_MT[g]], [r_xmid[ti]])

                S.fence(r_y + r_h, r_u)
                for g in range(16):
                    sq, rsq = next_sq()
                    act(sq, MT[:, g, :], AF.Square, [r_MT[g]], [rsq])
                    mm(PSTAT[:, :], ONESB[:, :], sq, g == 0, g == 15, [rsq, r_const], [r_PSTAT])
                rstd5, r_rstd5 = rstd_from_psum(PSTAT, r_PSTAT, float(D))
                for g in range(16):
                    stt(HY[:, g, :], MT[:, g, :], PV[:, pb + 32 + g:pb + 33 + g], rstd5, ALU.mult, ALU.mult,
                        [r_MT[g], r_PV, r_rstd5], [r_u[g]])
                S.fence(RA1 + RA4, RA5)
                S.fence(RB2 + RB3 + [r_XB], RB5)
                for c0 in range(0, 44, 3):
                    gn = min(3, 44 - c0)
                    wbg, rwbg = next_wb()
                    wvg = wbg[:, 0:16 * gn * 128].rearrange("p (k n) -> p k n", k=16)
                    load_w(wvg, rwbg, wgv[:, :, c0 * 128:(c0 + gn) * 128], wbg)
                    wbu, rwbu = next_wb()
                    wvu = wbu[:, 0:16 * gn * 128].rearrange("p (k n) -> p k n", k=16)
                    load_w(wvu, rwbu, wuv[:, :, c0 * 128:(c0 + gn) * 128], wbu)
                    for cc in range(gn):
                        c = c0 + cc
                        psg, rpsg = next_pa()
                        for kc in range(16):
                            mm(psg[:, :], wvg[:, kc, cc * 128:(cc + 1) * 128], HY[:, kc, :], kc == 0, kc == 15,
                               [rwbg, r_u[kc]], [rpsg])
                        psu, rpsu = next_pa()
                        for kc in range(16):
                            mm(psu[:, :], wvu[:, kc, cc * 128:(cc + 1) * 128], HY[:, kc, :], kc == 0, kc == 15,
                               [rwbu, r_u[kc]], [rpsu])
                        sg, rsg = next_rs()
                        act(sg, psg[:, :], AF.Silu, [rpsg], [rsg])
                        tt(HID[:, c, :], sg, psu[:, :], ALU.mult, [rsg, rpsu], [r_HID[c]])
                nl, nti = (l, ti + 1) if ti + 1 < NT else (l + 1, 0)
                hoist = nl < L and not (NT == 1) and os.environ.get('K_HOIST', '1') == '1'
                if hoist:
                    prologue(nl, nti)
                for g in range(16):
                    wb, rwb = next_wb()
                    wv = wb[:, 0:44 * 128].rearrange("p (k n) -> p k n", k=44)
                    load_w(wv, rwb, wdv[:, :, g * 128:(g + 1) * 128], wb)
                    ps, rps = next_pa()
                    for kc in range(44):
                        mm(ps[:, :], wv[:, kc, :], HID[:, kc, :], kc == 0, kc == 43, [rwb, r_HID[kc]], [rps])
                    cpy(FT[:, g, :], ps[:, :], [rps], [r_FT[g]], eng="act")
                    sq, rsq = next_sq()
                    act(sq, ps[:, :], AF.Square, [rps], [rsq])
                    mm(PSTAT[:, :], ONESB[:, :], sq, g == 0, g == 15, [rsq, r_const], [r_PSTAT])
                rstd6, r_rstd6 = rstd_from_psum(PSTAT, r_PSTAT, float(D))
                rdst = [r_xres[ti]] if l < L - 1 else [Res("yout")]
                for g in range(16):
                    xb, rx = next_xs()
                    S.dma("sp", xb, xmid_d[g * 128:(g + 1) * 128, t0:t0 + TT], [r_xmid[ti]], [rx])
                    stt(FT[:, g, :], FT[:, g, :], PV[:, pb + 48 + g:pb + 49 + g], rstd6, ALU.mult, ALU.mult,
                        [r_FT[g], r_PV, r_rstd6], [r_FT[g]])
                    tt(FT[:, g, :], FT[:, g, :], xb, ALU.add, [r_FT[g], rx], [r_FT[g]])
                    S.dma("sp", dst_d[g * 128:(g + 1) * 128, t0:t0 + TT], FT[:, g, :], [r_FT[g]], rdst)
                if nl < L and not hoist:
                    prologue(nl, nti)

        S.drain("sp")
        build_program.stats = (S.nins, S.nwaits)
    return nc


def host_consts(L):
    c = np.zeros((128, 5 * 128), np.float32)
    c[:, 0:128] = np.eye(128, dtype=np.float32)
    j = np.arange(128)[:, None]
    i = np.arange(128)[None, :]
    c[:, 128:256] = ((j // 64 == i // 64) & (i >= j)).astype(np.float32)
    c[:, 256:384] = (i >= j).astype(np.float32)
    jj = np.arange(64)[:, None]
    ii = np.arange(64)[None, :]
    c[0:64, 384:448] = (jj == (ii + 32) % 64).astype(np.float32)
    c[:, 512] = EPS
    c[:, 513] = 1.0
    return c


def host_pvec(L, p):
    pv = np.zeros((128, L * NPL + 2), np.float32)

    def cols(v):
        return np.ascontiguousarray(np.asarray(v, np.float32).reshape(-1, 128).T)
    for l in range(L):
        b = l * NPL
        pv[:, b + 0:b + 16] = cols(p["attn_pre_norm"][l])
        pv[:, b + 16:b + 32] = cols(p["attn_post_norm"][l])
        pv[:, b + 32:b + 48] = cols(p["ffn_pre_norm"][l])
        pv[:, b + 48:b + 64] = cols(p["ffn_post_norm"][l])
        pv[:, b + 64:b + 68] = cols(p["mla_q_norm"][l])
        pv[:, b + 68:b + 72] = cols(p["mla_kv_norm"][l])
        pv[:, b + 72:b + 80] = cols(p["mla_out_norm"][l])
        pv[:, b + 80:b + 81] = cols(p["gla_out_norm"][l])
        pv[:, b + 81:b + 82] = cols(p["hgrn_out_norm"][l])
        pv[:, b + 82:b + 84] = cols(p["gla_gate_b"][l])
        pv[:, b + 84:b + 88] = cols(p["hgrn_lb_logits"][l])
    inv_freq = (10000.0 ** (-np.arange(0, 64, 2, dtype=np.float32) / 64.0)).astype(np.float32)
    pv[0:64, L * NPL] = np.concatenate([inv_freq, inv_freq])
    pv[0:32, L * NPL + 1] = -1.0
    pv[32:64, L * NPL + 1] = 1.0
    return pv


_PROG_CACHE = {}


def run(inputs, T, L, B):
    key = (T, L)
    if key not in _PROG_CACHE:
        _PROG_CACHE[key] = build_program(T, L)
    nc = _PROG_CACHE[key]
    x = np.asarray(inputs["x"], np.float32)
    pos = np.asarray(inputs["positions"], np.int32)
    pv = host_pvec(L, inputs)
    cs = host_consts(L)
    shared = {
        "pvec": pv, "consts": cs,
        "w_in": np.ascontiguousarray(np.asarray(inputs["w_in"], np.float32)),
        "gla_gate_w2": np.ascontiguousarray(np.asarray(inputs["gla_gate_w2"], np.float32)),
        "mla_wq_b": np.ascontiguousarray(np.asarray(inputs["mla_wq_b"], np.float32)),
        "mla_wkv_b": np.ascontiguousarray(np.asarray(inputs["mla_wkv_b"], np.float32)),
        "w_out": np.ascontiguousarray(np.asarray(inputs["w_out"], np.float32)),
        "w_gate": np.ascontiguousarray(np.asarray(inputs["w_gate"], np.float32)),
        "w_up": np.ascontiguousarray(np.asarray(inputs["w_up"], np.float32)),
        "w_down": np.ascontiguousarray(np.asarray(inputs["w_down"], np.float32)),
    }
    work = {0: 0, 4: 1} if B == 2 else {c: c for c in range(B)}
    zeros = {k: np.zeros_like(v) for k, v in shared.items()}
    zx = np.zeros((D, T), np.float32)
    zp = np.zeros((1, T), np.int32)
    in_maps = []
    for c in range(NCORES):
        if c in work:
            b = work[c]
            m = dict(shared)
            m["xT"] = np.ascontiguousarray(x[b].T)
            m["pos"] = np.ascontiguousarray(pos[b].reshape(1, T))
        else:
            m = dict(zeros)
            m["xT"] = zx
            m["pos"] = zp
        in_maps.append(m)
    res = run_bass_kernel_spmd(nc, in_maps, core_ids=list(range(NCORES)))
    inv = {b: c for c, b in work.items()}
    out = np.stack([np.ascontiguousarray(res.results[inv[b]]["yT"].T) for b in range(B)], axis=0)
    return out.astype(np.float32)


def kernel(**inputs):
    x = inputs["x"]
    B, T, _ = x.shape
    L = inputs["w_in"].shape[0]
    return run(inputs, T, L, B)
```

```python
import contextlib
import os
import numpy as np
import concourse.bass as bass
import concourse.mybir as mybir
from concourse.bass_utils import run_bass_kernel_spmd

F32 = mybir.dt.float32
BF16 = mybir.dt.bfloat16
I32 = mybir.dt.int32
AF = mybir.ActivationFunctionType
ALU = mybir.AluOpType

D = 2048
DIN = 4688
DFF = 5632
TT = 512
EPS = 1e-6
O_GQ, O_GK, O_GV, O_GLOW, O_GOUT, O_HQ, O_HF, O_HI, O_HOUT, O_QC, O_KVC, O_KPE = (
    0, 256, 512, 1024, 1040, 1552, 2064, 2576, 3088, 3600, 4112, 4624)
NPL = 88
MAGIC = 12582912.0
NCORES = 8


class Res:
    __slots__ = ("n", "w", "r")

    def __init__(self, n):
        self.n = n
        self.w = {}
        self.r = {}


def _merge(d, k, v):
    if d.get(k, 0) < v:
        d[k] = v


class Sched:
    def __init__(self, nc, es):
        self.nc = nc
        self.E = {"pe": nc.tensor, "act": nc.scalar, "dve": nc.vector, "pool": nc.gpsimd, "sp": nc.sync}
        self.sem = {}
        self.cnt = {}
        for e in ("pe", "act", "dve", "pool"):
            self.sem[e] = es.enter_context(nc.semaphore("s_" + e))
            self.cnt[e] = 0
        self.dslots = {"sp": 12, "pool": 8}
        self.dnext = {"sp": 0, "pool": 0}
        for q, n in self.dslots.items():
            for i in range(n):
                self.sem[(q, i)] = es.enter_context(nc.semaphore("d_%s%d" % (q, i)))
                self.cnt[(q, i)] = 0
        self.waited = {e: {} for e in self.E}
        self.nwaits = 0
        self.nins = 0

    def _deps(self, reads, writes):
        deps = {}
        for r in reads:
            for k, v in r.w.items():
                _merge(deps, k, v)
        for w in writes:
            for k, v in w.w.items():
                _merge(deps, k, v)
            for k, v in w.r.items():
                _merge(deps, k, v)
        return deps

    def _wait(self, e, deps):
        wd = self.waited[e]
        for k, v in deps.items():
            if k == "pe" and e == "pe":
                continue
            if wd.get(k, 0) >= v:
                continue
            self.E[e].wait_ge(self.sem[k], v)
            wd[k] = v
            self.nwaits += 1

    def op(self, e, fn, reads=(), writes=()):
        self._wait(e, self._deps(reads, writes))
        ins = fn(self.E[e])
        self.cnt[e] += 1
        c = self.cnt[e]
        ins.then_inc(self.sem[e], 1)
        self.nins += 1
        for r in reads:
            _merge(r.r, e, c)
        for w in writes:
            w.w = {e: c}
            w.r = {}

    def dma(self, q, out, in_, reads=(), writes=()):
        i = self.dnext[q]
        self.dnext[q] = (i + 1) % self.dslots[q]
        key = (q, i)
        deps = self._deps(reads, writes)
        if self.cnt[key] > 0:
            _merge(deps, key, 16 * self.cnt[key])
        self._wait(q, deps)
        ins = self.E[q].dma_start(out=out, in_=in_)
        self.cnt[key] += 1
        v = 16 * self.cnt[key]
        ins.then_inc(self.sem[key], 16)
        self.nins += 1
        for r in reads:
            _merge(r.r, key, v)
        for w in writes:
            w.w = {key: v}
            w.r = {}

    def fence(self, old, new):
        d = {}
        for o in old:
            for k, v in o.w.items():
                _merge(d, k, v)
            for k, v in o.r.items():
                _merge(d, k, v)
        for n in new:
            n.w = dict(d)
            n.r = {}

    def fence_merge(self, old, new):
        d = {}
        for o in old:
            for k, v in o.w.items():
                _merge(d, k, v)
            for k, v in o.r.items():
                _merge(d, k, v)
        for n in new:
            for k, v in d.items():
                _merge(n.w, k, v)

    def drain(self, e="sp"):
        deps = {}
        for k, c in self.cnt.items():
            if c > 0:
                deps[k] = c * 16 if isinstance(k, tuple) else c
        self._wait(e, deps)


def build_program(T, L, dbg=False):
    NT = T // TT
    nc = bass.Bass("TRN2", target_bir_lowering=False)
    NPV = L * NPL + 2

    def din(name, shape, dt=F32):
        return nc.dram_tensor(name, list(shape), dt, kind="ExternalInput").ap()

    xT_in = din("xT", [D, T])
    pos_in = din("pos", [1, T], I32)
    pvec_in = din("pvec", [128, NPV])
    consts_in = din("consts", [128, 5 * 128])
    w_in = din("w_in", [L, D, DIN])
    w2_in = din("gla_gate_w2", [L, 16, 256])
    wq_in = din("mla_wq_b", [L, 512, 1536])
    wkv_in = din("mla_wkv_b", [L, 512, 2048])
    wo_in = din("w_out", [L, D, D])
    wg_in = din("w_gate", [L, D, DFF])
    wu_in = din("w_up", [L, D, DFF])
    wd_in = din("w_down", [L, DFF, D])
    yT_out = nc.dram_tensor("yT", [D, T], F32, kind="ExternalOutput").ap()

    xmid_d = nc.dram_tensor("xmid_d", [D, T], F32).ap()
    xres_d = nc.dram_tensor("xres_d", [D, T], F32).ap()
    kn_d = nc.dram_tensor("kn_d", [8, 128, T], BF16).ap()
    kpe_d = nc.dram_tensor("kpe_d", [128, T], BF16).ap()
    vd_d = nc.dram_tensor("vd_d", [8, NT, 128, 4, 128], BF16).ap()
    cc_d = nc.dram_tensor("cc_d", [64, T], F32).ap()
    ss_d = nc.dram_tensor("ss_d", [64, T], F32).ap()
    NUNIT = 80
    wcache = nc.dram_tensor("wcache", [NUNIT, 128, 6144], BF16).ap()

    es = contextlib.ExitStack()
    with es:
        S = Sched(nc, es)

        def sb(name, shape, dt):
            return es.enter_context(nc.sbuf_tensor(name, list(shape), dt))

        HY = sb("HY", [128, 16, TT], BF16)
        WB = [sb("WB%d" % i, [128, 6144], BF16) for i in range(4)]
        WSM = sb("WSM", [128, 16, 80], BF16)
        XS = sb("XS", [128, 4, TT], F32)
        SQ = sb("SQ", [128, 3, TT], BF16)
        RS = sb("RS", [128, 3, TT], F32)
        SG = sb("SG", [128, 6, 128], F32)
        SBs = sb("SBs", [128, 6, 128], BF16)
        CS = sb("CS", [128, 2, TT], F32)
        CONF = sb("CONF", [128, 5 * 128], F32)
        IDENT = sb("IDENT", [128, 128], BF16)
        ONESB = sb("ONESB", [128, 128], BF16)
        MASKC = sb("MASKC", [128, 128], BF16)
        PERM = sb("PERM", [128, 64], BF16)
        ONEF = sb("ONEF", [128, TT], F32)
        PV = sb("PV", [128, NPV], F32)
        NEGB = sb("NEGB", [128, L, 2], F32)
        LBt = sb("LBt", [128, 4, L], F32)
        OMLt = sb("OMLt", [128, 4, L], F32)
        SMT = sb("SMT", [128, 4, 8], F32)
        W2 = sb("W2", [16, L, 256], BF16)
        GLOW = sb("GLOW", [16, TT], BF16)
        DEC = sb("DEC", [128, 6, 8], F32)
        WG = sb("WG", [128, 6, 8], F32)
        SI = sb("SI", [128, 6, 8], F32)
        CT = sb("CT", [128, 6, 8], F32)
        RA = sb("RA", [128, 12288], F32)
        RB = sb("RB", [128, 12288], F32)

        def view(reg, off, words, dt, pat=None, **kw):
            a = reg[:, off:off + words]
            if dt == BF16:
                a = a.bitcast(BF16)
            if pat:
                a = a.rearrange(pat, **kw)
            return a

        QF = view(RA, 0, 1536, BF16, "p (a t) -> p a t", a=6)
        KF = view(RA, 1536, 1536, BF16, "p (a t) -> p a t", a=6)
        BFv = view(RA, 3072, 3072, F32, "p (a t) -> p a t", a=6)
        VG = view(RA, 6144, 1024, BF16, "p (a t) -> p a t", a=4)
        VH = view(RA, 7168, 1024, BF16, "p (a t) -> p a t", a=4)
        QN = view(RA, 8192, 1024, BF16, "p (a t) -> p a t", a=4)
        CN = view(RA, 9216, 1024, BF16, "p (a t) -> p a t", a=4)
        GS = view(RA, 10240, 2048, BF16, "p (a t) -> p a t", a=8)
        MT = view(RA, 0, 8192, F32, "p (a t) -> p a t", a=16)
        HID = view(RA, 0, 11264, BF16, "p (a t) -> p a t", a=44)
        QT = view(RB, 0, 1536, BF16, "p (a t) -> p a t", a=6)
        KT = view(RB, 1536, 1536, BF16, "p (a t) -> p a t", a=6)
        KTT = view(RB, 3072, 1536, BF16, "p (a s d) -> p a s d", a=6, s=4)
        ET = view(RB, 4608, 1536, F32, "p (a t) -> p a t", a=3)
        OT = view(RB, 6144, 4096, F32, "p (a t) -> p a t", a=8)
        ATMv = view(RB, 10240, 512, BF16, "p (a t) -> p a t", a=8)
        K2 = view(RB, 10752, 1536, BF16, "p (a t) -> p a t", a=6)
        QNOPE = view(RB, 0, 2048, BF16, "p (a t) -> p a t", a=8)
        QPE = view(RB, 2048, 2048, BF16, "p (a t) -> p a t", a=8)
        KS = view(RB, 4096, 512, BF16, "p (a t) -> p a t", a=2)
        KPS = view(RB, 4608, 512, BF16, "p (a t) -> p a t", a=2)
        VS = view(RB, 5120, 512, BF16, "p (a b e) -> p a b e", a=2, b=4)
        PT = view(RB, 5632, 1024, BF16, "p (a t) -> p a t", a=4)
        KSL = [KS[:, 0, :], KS[:, 1, :]]
        KPSL = [KPS[:, 0, :], KPS[:, 1, :]]
        VSL = [VS[:, 0, :, :], VS[:, 1, :, :]]
        for _k in range(2):
            _b = 6656 + _k * 768
            KSL.append(view(RB, _b, 256, BF16))
            KPSL.append(view(RB, _b + 256, 256, BF16))
            VSL.append(view(RB, _b + 512, 256, BF16, "p (b e) -> p b e", b=4))
        KTO = view(RB, 6656, 512, BF16, "p (a t) -> p a t", a=2)
        VTO = view(RB, 7168, 1024, BF16, "p (a h e) -> p a h e", a=2, h=8)
        OMLA = view(RB, 8192, 4096, F32, "p (a t) -> p a t", a=8)
        FT = view(RB, 0, 8192, F32, "p (a t) -> p a t", a=16)
        XB = view(RB, 0, 8192, F32, "p (a t) -> p a t", a=16)

        PA = [es.enter_context(nc.psum_tensor("PA%d" % i, [128, TT], F32)) for i in range(6)]
        PSTAT = es.enter_context(nc.psum_tensor("PSTAT", [128, TT], F32))
        PTR = es.enter_context(nc.psum_tensor("PTR", [128, 8, 128], BF16))

        def R(n):
            return Res(n)

        def RL(n, k):
            return [Res("%s%d" % (n, i)) for i in range(k)]

        r_PA = RL("PA", 6)
        r_PSTAT = R("PSTAT")
        r_PTR = RL("PTR", 8)
        r_WB = RL("WB", 4)
        r_WSM = R("WSM")
        r_XS = RL("XS", 4)
        r_SQ = RL("SQ", 3)
        r_RS = RL("RS", 3)
        r_SG = RL("SG", 6)
        r_SBs = RL("SBs", 6)
        r_CS = R("CS")
        r_const = R("const")
        r_PV = R("PV")
        r_GLOW = R("GLOW")
        r_CH = R("CH")
        r_h = RL("h", 16)
        r_y = RL("y", 16)
        r_u = RL("u", 16)
        r_QF, r_KF, r_BF = RL("QF", 6), RL("KF", 6), RL("BF", 6)
        r_VG, r_VH = RL("VG", 4), RL("VH", 4)
        r_QN, r_CN = RL("QN", 4), RL("CN", 4)
        r_GS = RL("GS", 8)
        r_MT = RL("MT", 16)
        r_HID = RL("HID", 44)
        r_QT, r_KT = RL("QT", 6), RL("KT", 6)
        r_KTT = [RL("KTT%d_" % a, 4) for a in range(6)]
        r_ET = RL("ET", 3)
        r_OT = RL("OT", 8)
        r_ATM = RL("ATM", 8)
        r_K2 = RL("K2", 6)
        r_QNOPE, r_QPE = RL("QNOPE", 8), RL("QPE", 8)
        r_KS, r_KPS, r_VS = RL("KS", 4), RL("KPS", 4), RL("VS", 4)
        r_PT = RL("PT", 4)
        r_KTO, r_VTO = RL("KTO", 2), RL("VTO", 2)
        r_OMLA = RL("OMLA", 8)
        r_FT = RL("FT", 16)
        r_XB = R("XB")
        RA1 = r_QF + r_KF + r_BF + r_VG + r_VH + r_QN + r_CN + r_GS
        RA4 = r_MT
        RA5 = r_HID
        RB2 = r_QT + r_KT + sum(r_KTT, []) + r_ET + r_OT + r_ATM + r_K2
        RB3 = r_QNOPE + r_QPE + r_KS + r_KPS + r_VS + r_PT + r_KTO + r_VTO + r_OMLA
        RB5 = r_FT
        r_xmid = RL("xmid", NT)
        r_xres = RL("xres", NT)
        r_kv = RL("kv", NT)
        r_ccss = RL("ccss", NT)
        r_cache = RL("wcache", 80)

        rot = {"pa": 0, "ptr": 0, "xs": 0, "sq": 0, "rs": 0, "wb": 0, "et": 0, "pt": 0, "atm": 0, "ks": 0, "pa4": 0, "pa3": 0}

        def nxt(name, n):
            i = rot[name]
            rot[name] = (i + 1) % n
            return i

        def next_pa():
            i = nxt("pa", 6)
            return PA[i], r_PA[i]

        def next_wb():
            i = nxt("wb", 4)
            return WB[i], r_WB[i]

        def next_xs():
            i = nxt("xs", 4)
            return XS[:, i, :], r_XS[i]

        def next_sq():
            i = nxt("sq", 3)
            return SQ[:, i, :], r_SQ[i]

        def next_rs():
            i = nxt("rs", 3)
            return RS[:, i, :], r_RS[i]

        def mm(out, lhsT, rhs, start, stop, reads, writes):
            S.op("pe", lambda e: e.matmul(out, lhsT=lhsT, rhs=rhs, start=start, stop=stop), reads, writes)

        def act(out, in_, func, reads, writes, scale=None, bias=None):
            kw = {}
            if scale is not None:
                kw["scale"] = scale
            if bias is not None:
                kw["bias"] = bias
            S.op("act", lambda e: e.activation(out=out, in_=in_, func=func, **kw), reads, writes)

        def tt(out, in0, in1, op, reads, writes, eng="dve"):
            S.op(eng, lambda e: e.tensor_tensor(out=out, in0=in0, in1=in1, op=op), reads, writes)

        def ts(out, in0, s1, s2, op0, op1, reads, writes, eng="dve"):
            if op1 is None:
                S.op(eng, lambda e: e.tensor_scalar(out=out, in0=in0, scalar1=s1, scalar2=None, op0=op0), reads, writes)
            else:
                S.op(eng, lambda e: e.tensor_scalar(out=out, in0=in0, scalar1=s1, scalar2=s2, op0=op0, op1=op1),
                     reads, writes)

        def stt(out, in0, scalar, in1, op0, op1, reads, writes):
            S.op("dve", lambda e: e.scalar_tensor_tensor(out=out, in0=in0, scalar=scalar, in1=in1, op0=op0, op1=op1),
                 reads, writes)

        def cpy(out, in_, reads, writes, eng="dve"):
            if eng == "act":
                S.op("act", lambda e: e.copy(out=out, in_=in_), reads, writes)
            else:
                S.op(eng, lambda e: e.tensor_copy(out=out, in_=in_), reads, writes)

        def rstd_from_psum(ps, r_ps, dim):
            t1, r1 = next_rs()
            act(t1, ps[:, :], AF.Ln, [r_ps, r_const], [r1], scale=1.0 / dim, bias=EPSC[:, 0:1])
            t2, r2 = next_rs()
            act(t2, t1, AF.Exp, [r1], [r2], scale=-0.5)
            return t2, r2

        S.dma("sp", CONF[:], consts_in[:, :], [], [r_const])
        S.dma("sp", PV[:], pvec_in[:, :], [], [r_PV])
        for l in range(L):
            S.dma("pool", W2[:, l, :], w2_in[l], [], [r_const])
        cpy(IDENT[:], CONF[:, 0:128], [r_const], [r_const])
        cpy(MASKC[:], CONF[:, 256:384], [r_const], [r_const])
        cpy(PERM[:], CONF[:, 384:448], [r_const], [r_const])
        MASKB = CONF[:, 128:256]
        EPSC = CONF[:, 512:640]
        S.op("dve", lambda e: e.memset(ONESB[:], 1.0), [], [r_const])
        S.op("dve", lambda e: e.memset(ONEF[:], 1.0), [], [r_const])
        S.op("dve", lambda e: e.memset(SG[:], 0.0), [], r_SG)
        pvl = PV[:, 0:L * NPL].rearrange("p (l c) -> p l c", c=NPL)
        ts(NEGB[:], pvl[:, :, 82:84], -1.0, None, ALU.mult, None, [r_PV], [r_PV])
        lg = pvl[:, :, 84:88].rearrange("p l t -> p t l")
        S.op("dve", lambda e: e.tensor_reduce(out=SMT[:, :, 0:1], in_=lg, axis=mybir.AxisListType.X, op=ALU.max),
             [r_PV], [r_CH])
        tt(LBt[:], lg, SMT[:, :, 0:1].broadcast_to([128, 4, L]), ALU.subtract, [r_PV, r_CH], [r_PV])
        act(LBt[:], LBt[:], AF.Exp, [r_PV], [r_PV])
        S.op("dve", lambda e: e.tensor_reduce(out=SMT[:, :, 1:2], in_=LBt[:], axis=mybir.AxisListType.X, op=ALU.add),
             [r_PV], [r_CH])
        S.op("dve", lambda e: e.reciprocal(out=SMT[:, :, 2:3], in_=SMT[:, :, 1:2]), [r_CH], [r_CH])
        tt(LBt[:], LBt[:], SMT[:, :, 2:3].broadcast_to([128, 4, L]), ALU.mult, [r_PV, r_CH], [r_PV])
        cpy(SMT[:, :, 3:4], LBt[:, :, 0:1], [r_PV], [r_CH])
        for l in range(1, L):
            tt(LBt[:, :, l:l + 1], LBt[:, :, l:l + 1], LBt[:, :, l - 1:l], ALU.add, [r_PV], [r_PV])
        tt(LBt[:], LBt[:], SMT[:, :, 3:4].broadcast_to([128, 4, L]), ALU.subtract, [r_PV, r_CH], [r_PV])
        ts(OMLt[:], LBt[:], -1.0, 1.0, ALU.mult, ALU.add, [r_PV], [r_PV])

        INVF = PV[0:64, L * NPL:L * NPL + 1]
        SGN = PV[0:64, L * NPL + 1:L * NPL + 2]
        for ti in range(NT):
            t0 = ti * TT
            xi, rxi = next_xs()
            S.dma("sp", xi[0:64, :].bitcast(I32), pos_in[:, t0:t0 + TT].partition_broadcast(64), [], [rxi])
            rr, rrr = next_xs()
            cpy(rr[0:64, :], xi[0:64, :].bitcast(I32), [rxi], [rrr])
            ts(rr[0:64, :], rr[0:64, :], INVF, 1.0 / (2.0 * np.pi), ALU.mult, ALU.mult, [rrr, r_PV], [rrr])
            for which in range(2):
                a, ra = next_rs()
                b, rb = next_rs()
                if which == 0:
                    ts(a[0:64, :], rr[0:64, :], 0.25, None, ALU.add, None, [rrr], [ra])
                    src = a
                    rsrc = ra
                else:
                    src = rr
                    rsrc = rrr
                ts(b[0:64, :], src[0:64, :], MAGIC, None, ALU.add, None, [rsrc], [rb])
                ts(b[0:64, :], b[0:64, :], MAGIC, None, ALU.subtract, None, [rb], [rb])
                tt(b[0:64, :], src[0:64, :], b[0:64, :], ALU.subtract, [rsrc, rb], [rb])
                o, ro = next_xs()
                act(o[0:64, :], b[0:64, :], AF.Sin, [rb], [ro], scale=6.283185)
                if which == 1:
                    ts(o[0:64, :], o[0:64, :], SGN, None, ALU.mult, None, [ro, r_PV], [ro])
                    S.dma("sp", ss_d[:, t0:t0 + TT], o[0:64, :], [ro], [r_ccss[ti]])
                else:
                    S.dma("sp", cc_d[:, t0:t0 + TT], o[0:64, :], [ro], [r_ccss[ti]])

        zt, rzt = next_sq()
        S.op("dve", lambda e: e.memset(zt, 0.0), [], [rzt])
        for ti in range(NT):
            S.dma("sp", kpe_d[64:128, ti * TT:(ti + 1) * TT], zt[0:64, :], [rzt], [r_kv[ti]])

        ucnt = [0]
        cur_ti = [0]

        def load_w(dst, rdst, src, wb):
            u = ucnt[0]
            ucnt[0] += 1
            n = 1
            for dd in dst.shape[1:]:
                n *= dd
            flat = wb[:, 0:n]
            if cur_ti[0] == 0 or NT == 1:
                S.dma("pool", dst, src, [], [rdst])
                if NT > 1:
                    S.dma("sp", wcache[u, :, 0:n], flat, [rdst], [r_cache[u]])
            else:
                S.dma("pool", flat, wcache[u, :, 0:n], [r_cache[u]], [rdst])

        def rms_stats_from_dram(src_d, r_src, t0):
            for kc in range(16):
                xb, rx = next_xs()
                S.dma("sp", xb, src_d[kc * 128:(kc + 1) * 128, t0:t0 + TT], [r_src], [rx])
                sq, rsq = next_sq()
                act(sq, xb, AF.Square, [rx], [rsq])
                mm(PSTAT[:, :], ONESB[:, :], sq, kc == 0, kc == 15, [rsq, r_const], [r_PSTAT])
            return rstd_from_psum(PSTAT, r_PSTAT, float(D))

        def normed_from_dram(src_d, r_src, t0, gcol, dst_res, rstd, r_rstd):
            for kc in range(16):
                xb, rx = next_xs()
                S.dma("sp", xb, src_d[kc * 128:(kc + 1) * 128, t0:t0 + TT], [r_src], [rx])
                stt(HY[:, kc, :], xb, PV[:, gcol + kc:gcol + kc + 1], rstd, ALU.mult, ALU.mult,
                    [rx, r_PV, r_rstd], [dst_res[kc]])

        def proj_fm(wsrc_cols, kdim_chunks, rhs_fn, rhs_res, nchunk, evac):
            for c in range(nchunk):
                ps, rps = next_pa()
                for kc in range(kdim_chunks):
                    lhsT, rw, m = wsrc_cols(kc, c)
                    mm(ps[0:m, :], lhsT, rhs_fn(kc), kc == 0, kc == kdim_chunks - 1,
                       [rw, rhs_res[kc]], [rps])
                evac(c, ps, rps)

        def prologue(pl, pti):
            psrc = xT_in if pl == 0 else xres_d
            pres = Res("xin") if pl == 0 else r_xres[pti]
            pt0 = pti * TT
            S.fence(r_u + r_y, r_h)
            prstd, pr_rstd = rms_stats_from_dram(psrc, pres, pt0)
            normed_from_dram(psrc, pres, pt0, pl * NPL + 0, r_h, prstd, pr_rstd)
            S.dma("sp", CS[0:64, 0, :], cc_d[:, pt0:pt0 + TT], [r_ccss[pti]], [r_CS])
            S.dma("sp", CS[0:64, 1, :], ss_d[:, pt0:pt0 + TT], [r_ccss[pti]], [r_CS])

        for l in range(L):
            pb = l * NPL
            src_d = xT_in if l == 0 else xres_d
            r_src_t = None if l == 0 else r_xres
            dst_d = yT_out if l == L - 1 else xres_d
            winv = w_in[l].rearrange("(kc p) n -> p kc n", p=128)
            wqv = wq_in[l].rearrange("(kc p) n -> p kc n", p=128)
            wkvv = wkv_in[l].rearrange("(kc p) n -> p kc n", p=128)
            wov = wo_in[l].rearrange("(kc p) n -> p kc n", p=128)
            wgv = wg_in[l].rearrange("(kc p) n -> p kc n", p=128)
            wuv = wu_in[l].rearrange("(kc p) n -> p kc n", p=128)
            wdv = wd_in[l].rearrange("(kc p) n -> p kc n", p=128)
            S.op("dve", lambda e: e.memset(SG[:], 0.0), [], r_SG)

            for ti in range(NT):
                t0 = ti * TT
                ucnt[0] = 0
                cur_ti[0] = ti
                r_src = [] if r_src_t is None else [r_src_t[ti]]
                rsrc1 = r_src[0] if r_src else Res("xin")

                if l == 0 and ti == 0:
                    prologue(0, 0)
                S.fence(RA5 + RA4, RA1)

                def hrhs(kc):
                    return HY[:, kc, :]

                wsm_c = wcache[79, :, 0:1280].rearrange("p (k n) -> p k n", k=16)
                if ti == 0 or NT == 1:
                    S.dma("pool", WSM[:, :, 0:16], winv[:, :, O_GLOW:O_GLOW + 16], [], [r_WSM])
                    S.dma("pool", WSM[:, :, 16:80], winv[:, :, O_KPE:O_KPE + 64], [], [r_WSM])
                    if NT > 1:
                        S.dma("sp", wsm_c, WSM[:, :, :], [r_WSM], [r_cache[79]])
                else:
                    S.dma("pool", WSM[:, :, :], wsm_c, [r_cache[79]], [r_WSM])

                def load_group(col0, ncols):
                    wb, rwb = next_wb()
                    wv = wb[:, 0:16 * ncols].rearrange("p (k n) -> p k n", k=16)
                    load_w(wv, rwb, winv[:, :, col0:col0 + ncols], wb)
                    return wv, rwb

                def fm_group(col0, nchunk, evac):
                    c = 0
                    while c < nchunk:
                        g = min(3, nchunk - c)
                        wv, rwb = load_group(col0 + c * 128, g * 128)
                        base = c
                        proj_fm(lambda kc, cc, wv=wv, rwb=rwb: (wv[:, kc, cc * 128:(cc + 1) * 128], rwb, 128),
                                16, hrhs, r_h, g, lambda cc, ps, rps, base=base: evac(base + cc, ps, rps))
                        c += g

                def tm_group(col0, dstv, dres):
                    for half in range(2):
                        wv, rwb = load_group(col0 + half * 256, 256)
                        for sub in range(4):
                            ps, rps = next_pa()
                            for kc in range(16):
                                mm(ps[:, 0:256], HY[:, kc, sub * 128:(sub + 1) * 128], wv[:, kc, :],
                                   kc == 0, kc == 15, [rwb, r_h[kc]], [rps])
                            cpy(dstv[:, sub, half * 256:(half + 1) * 256], ps[:, 0:256], [rps], [dres[sub]],
                                eng="act")

                def ev_gqk(c, ps, rps):
                    if c < 2:
                        act(QF[:, c, :], ps[:, :], AF.Copy, [rps], [r_QF[c]], scale=0.125)
                    else:
                        cpy(KF[:, c - 2, :], ps[:, :], [rps], [r_KF[c - 2]])
                fm_group(O_GQ, 4, ev_gqk)
                tm_group(O_GV, VG, r_VG)
                ps, rps = next_pa()
                for kc in range(16):
                    mm(ps[0:16, :], WSM[:, kc, 0:16], HY[:, kc, :], kc == 0, kc == 15, [r_WSM, r_h[kc]], [rps])
                cpy(GLOW[:, :], ps[0:16, :], [rps], [r_GLOW], eng="act")
                for c in range(2):
                    ps, rps = next_pa()
                    mm(ps[:, :], W2[:, l, c * 128:(c + 1) * 128], GLOW[:, :], True, True, [r_const, r_GLOW], [rps])
                    e1, re1 = next_rs()
                    act(e1, ps[:, :], AF.Exp, [rps, r_PV], [re1], scale=-1.0, bias=NEGB[:, l, c:c + 1])
                    e2, re2 = next_rs()
                    act(e2, e1, AF.Ln, [re1, r_const], [re2], bias=EPSC[:, 1:2])
                    S.op("dve", lambda e, e2=e2, c=c: e.tensor_tensor_scan(
                        out=BFv[:, c, :], data0=ONEF[:, :], data1=e2, initial=0.0, op0=ALU.mult, op1=ALU.subtract),
                        [re2, r_const], [r_BF[c]])
                fm_group(O_GOUT, 4, lambda c, ps, rps: act(GS[:, c, :], ps[:, :], AF.Silu, [rps], [r_GS[c]]))
                fm_group(O_HQ, 4, lambda c, ps, rps: act(QF[:, 2 + c, :], ps[:, :], AF.Silu, [rps], [r_QF[2 + c]]))

                def ev_hf(c, ps, rps):
                    sg, rsg = next_rs()
                    act(sg, ps[:, :], AF.Sigmoid, [rps], [rsg])
                    f, rf = next_rs()
                    ts(f, sg, OMLt[:, c, l:l + 1], LBt[:, c, l:l + 1], ALU.mult, ALU.add, [rsg, r_PV], [rf])
                    ts(KF[:, 2 + c, :], f, -1.0, 1.0, ALU.mult, ALU.add, [rf], [r_KF[2 + c]])
                    lf, rlf = next_rs()
                    act(lf, f, AF.Ln, [rf], [rlf])
                    S.op("dve", lambda e, lf=lf, c=c: e.tensor_tensor_scan(
                        out=BFv[:, 2 + c, :], data0=ONEF[:, :], data1=lf, initial=0.0, op0=ALU.mult, op1=ALU.add),
                        [rlf, r_const], [r_BF[2 + c]])
                fm_group(O_HF, 4, ev_hf)
                tm_group(O_HI, VH, r_VH)
                fm_group(O_HOUT, 4, lambda c, ps, rps: act(GS[:, 4 + c, :], ps[:, :], AF.Silu, [rps], [r_GS[4 + c]]))

                def latent(col0, gcol, dstv, dres):
                    tmp = []

                    def ev(c, ps, rps):
                        xb, rx = next_xs()
                        cpy(xb, ps[:, :], [rps], [rx], eng="act")
                        sq, rsq = next_sq()
                        act(sq, ps[:, :], AF.Square, [rps], [rsq])
                        mm(PSTAT[:, :], ONESB[:, :], sq, c == 0, c == 3, [rsq, r_const], [r_PSTAT])
                        tmp.append((xb, rx))
                    c = 0
                    wv, rwb = load_group(col0, 384)
                    proj_fm(lambda kc, cc: (wv[:, kc, cc * 128:(cc + 1) * 128], rwb, 128), 16, hrhs, r_h, 3, ev)
                    wv2, rwb2 = load_group(col0 + 384, 128)
                    proj_fm(lambda kc, cc: (wv2[:, kc, 0:128], rwb2, 128), 16, hrhs, r_h, 1,
                            lambda cc, ps, rps: ev(3, ps, rps))
                    rstd2, r_rstd2 = rstd_from_psum(PSTAT, r_PSTAT, 512.0)
                    for c in range(4):
                        xb, rx = tmp[c]
                        stt(dstv[:, c, :], xb, PV[:, gcol + c:gcol + c + 1], rstd2, ALU.mult, ALU.mult,
                            [rx, r_PV, r_rstd2], [dres[c]])
                latent(O_QC, pb + 64, QN, r_QN)
                latent(O_KVC, pb + 68, CN, r_CN)

                S.fence(RB5 + RB3 + [r_XB], RB2)
                ps, rps = next_pa()
                for kc in range(16):
                    mm(ps[0:64, :], WSM[:, kc, 16:80], HY[:, kc, :], kc == 0, kc == 15, [r_WSM, r_h[kc]], [rps])

                def rope(ps, rps, dst, rdst, scale):
                    qb, rqb = next_sq()
                    act(qb[0:64, :], ps[0:64, :], AF.Copy, [rps], [rqb], scale=scale)
                    ps2, rps2 = next_pa()
                    mm(ps2[0:64, :], PERM[0:64, :], qb[0:64, :], True, True, [r_const, rqb], [rps2])
                    a, ra = next_xs()
                    tt(a[0:64, :], qb[0:64, :], CS[0:64, 0, :], ALU.mult, [rqb, r_CS], [ra])
                    b, rb = next_xs()
                    tt(b[0:64, :], ps2[0:64, :], CS[0:64, 1, :], ALU.mult, [rps2, r_CS], [rb])
                    tt(dst, a[0:64, :], b[0:64, :], ALU.add, [ra, rb], [rdst])
                kpo, rkpo = next_sq()
                rope(ps, rps, kpo[0:64, :], rkpo, 1.0)
                S.dma("sp", kpe_d[0:64, t0:t0 + TT], kpo[0:64, :], [rkpo], [r_kv[ti]])

                for (a0, a1, sc) in ((0, 2, 1.0 / 16.0), (2, 6, 1.0)):
                    BL = BFv[:, a0:a1, 63:TT:64]
                    BM = BFv[:, a0:a1, 31:TT:64]
                    rb_ = r_BF[a0:a1]
                    S.op("dve", lambda e, a0=a0, a1=a1: e.memset(CT[:, a0:a1, 0:1], 0.0), [], [r_CH])
                    cpy(CT[:, a0:a1, 1:8], BFv[:, a0:a1, 63:TT - 64:64], rb_, [r_CH])
                    tt(DEC[:, a0:a1, :], BL, CT[:, a0:a1, :], ALU.subtract, rb_ + [r_CH], [r_CH])
                    act(DEC[:, a0:a1, :], DEC[:, a0:a1, :], AF.Exp, [r_CH], [r_CH], scale=sc)
                    tt(WG[:, a0:a1, :], BL, BM, ALU.subtract, rb_, [r_CH])
                    act(WG[:, a0:a1, :], WG[:, a0:a1, :], AF.Exp, [r_CH], [r_CH], scale=sc)
                    tt(SI[:, a0:a1, :], BM, CT[:, a0:a1, :], ALU.subtract, rb_ + [r_CH], [r_CH])
                    act(SI[:, a0:a1, :], SI[:, a0:a1, :], AF.Exp, [r_CH], [r_CH], scale=sc)
                for a in range(6):
                    sc = 1.0 / 16.0 if a < 2 else 1.0
                    i = nxt("et", 3)
                    arg, rarg = ET[:, i, :], r_ET[i]
                    tt(arg.rearrange("p (c t) -> p c t", c=8), BFv[:, a, :].rearrange("p (c t) -> p c t", c=8),
                       BFv[:, a, 31:TT:64].unsqueeze(2).broadcast_to([128, 8, 64]), ALU.subtract, [r_BF[a]], [rarg])
                    i = nxt("et", 3)
                    e1, re1 = ET[:, i, :], r_ET[i]
                    act(e1, arg, AF.Exp, [rarg], [re1], scale=sc)
                    tt(QT[:, a, :], QF[:, a, :], e1, ALU.mult, [r_QF[a], re1], [r_QT[a]])
                    i = nxt("et", 3)
                    e2, re2 = ET[:, i, :], r_ET[i]
                    act(e2, arg, AF.Exp, [rarg], [re2], scale=-sc)
                    tt(KT[:, a, :], KF[:, a, :], e2, ALU.mult, [r_KF[a], re2], [r_KT[a]])
                    tt(K2[:, a, :].rearrange("p (c t) -> p c t", c=8), KT[:, a, :].rearrange("p (c t) -> p c t", c=8),
                       WG[:, a, :].unsqueeze(2).broadcast_to([128, 8, 64]), ALU.mult, [r_KT[a], r_CH], [r_K2[a]])
                    for sub in range(4):
                        S.op("pe", lambda e, a=a, sub=sub: e.transpose(
                            PTR[:, sub, :], K2[:, a, sub * 128:(sub + 1) * 128], IDENT[:, :]),
                            [r_K2[a], r_const], [r_PTR[0]])
                    cpy(KTT[:, a, :, :], PTR[:, 0:4, :], [r_PTR[0]], r_KTT[a], eng="act")

                def rot3():
                    k = nxt("pa3", 3)
                    return ((PA[4], r_PA[4]), (PA[5], r_PA[5]), (PSTAT, r_PSTAT))[k]

                for sub in range(4):
                    tcs = slice(sub * 128, (sub + 1) * 128)
                    for wave in range(2):
                        hinfo = []
                        if wave == 0:
                            for a in range(2):
                                for k in range(2):
                                    hh = 2 * a + k
                                    hinfo.append((a, hh, 64 * k, 64, VG, r_VG, slice(hh * 128, (hh + 1) * 128), hh))
                            alist = [0, 1]
                        else:
                            for hh in range(4):
                                hinfo.append((2 + hh, 4 + hh, 0, 128, VH, r_VH, slice(hh * 128, (hh + 1) * 128), hh))
                            alist = [2, 3, 4, 5]
                        for (a, oh, p0, dk, Vt, rV, vcol, bk) in hinfo:
                            psA, rpsA = rot3()
                            mm(psA[:, 0:128], KT[p0:p0 + dk, a, tcs], QT[p0:p0 + dk, a, tcs], True, True,
                               [r_KT[a], r_QT[a]], [rpsA])
                            tt(ATMv[:, oh, :], psA[:, 0:128], MASKB, ALU.mult, [rpsA, r_const], [r_ATM[oh]])
                        for (a, oh, p0, dk, Vt, rV, vcol, bk) in hinfo:
                            mm(PA[bk][:, 0:128], Vt[:, sub, vcol], ATMv[:, oh, :], True, False,
                               [rV[sub], r_ATM[oh]], [r_PA[bk]])
                        for half in range(2):
                            c = sub * 2 + half
                            pr = slice(half * 64, half * 64 + 64)
                            qcs = slice(c * 64, (c + 1) * 64)
                            for a in alist:
                                S.op("act", lambda e, a=a, c=c: e.activation(
                                    out=SBs[:, a, :], in_=SG[:, a, :], func=AF.Copy, scale=SI[:, a, c:c + 1]),
                                    [r_SG[a], r_CH], [r_SBs[a]])
                            for (a, oh, p0, dk, Vt, rV, vcol, bk) in hinfo:
                                mm(PA[bk][:, half * 64:half * 64 + 64], SBs[p0:p0 + dk, a, :], QT[p0:p0 + dk, a, qcs],
                                   False, half == 1, [r_SBs[a], r_QT[a]], [r_PA[bk]])
                            for a in alist:
                                psU, rpsU = rot3()
                                if a < 2:
                                    mm(psU[:, 0:256], KTT[pr, a, sub, :], VG[pr, sub, a * 256:(a + 1) * 256], True, True,
                                       [r_KTT[a][sub], r_VG[sub]], [rpsU])
                                else:
                                    mm(psU[:, 0:128], KTT[pr, a, sub, :], VH[pr, sub, (a - 2) * 128:(a - 1) * 128],
                                       True, True, [r_KTT[a][sub], r_VH[sub]], [rpsU])
                                if a < 2:
                                    stt(SG[0:64, a, :], SG[0:64, a, :], DEC[0:64, a, c:c + 1], psU[0:64, 0:128],
                                        ALU.mult, ALU.add, [r_SG[a], r_CH, rpsU], [r_SG[a]])
                                    stt(SG[64:128, a, :], SG[64:128, a, :], DEC[64:128, a, c:c + 1], psU[64:128, 128:256],
                                        ALU.mult, ALU.add, [r_SG[a], r_CH, rpsU], [r_SG[a]])
                                else:
                                    stt(SG[:, a, :], SG[:, a, :], DEC[:, a, c:c + 1], psU[:, 0:128], ALU.mult, ALU.add,
                                        [r_SG[a], r_CH, rpsU], [r_SG[a]])
                        for (a, oh, p0, dk, Vt, rV, vcol, bk) in hinfo:
                            cpy(OT[:, oh, tcs], PA[bk][:, 0:128], [r_PA[bk]], [r_OT[oh]], eng="act")

                S.fence(r_h + r_u, r_y)
                for oh in range(8):
                    sq, rsq = next_sq()
                    act(sq, OT[:, oh, :], AF.Square, [r_OT[oh]], [rsq])
                    mm(PSTAT[:, :], ONESB[:, :], sq, True, True, [rsq, r_const], [r_PSTAT])
                    rstd2, r_rstd2 = rstd_from_psum(PSTAT, r_PSTAT, 128.0)
                    gcol = pb + (80 if oh < 4 else 81)
                    t1, rt1 = next_xs()
                    stt(t1, OT[:, oh, :], PV[:, gcol:gcol + 1], rstd2, ALU.mult, ALU.mult,
                        [r_OT[oh], r_PV, r_rstd2], [rt1])
                    tt(HY[:, oh, :], t1, GS[:, oh, :], ALU.mult, [rt1, r_GS[oh]], [r_y[oh]])

                S.fence(RB2 + RB5 + [r_XB], RB3)
                wb, rwb = next_wb()
                wq = wb[:, 0:6144].rearrange("p (k n) -> p k n", k=4)
                load_w(wq, rwb, wqv[:, :, :], wb)
                qscale = 192.0 ** -0.5
                S.op("dve", lambda e: e.memset(QPE[64:128, :, :], 0.0), [], r_QPE)
                for h in range(8):
                    ps, rps = next_pa()
                    for kc in range(4):
                        mm(ps[:, :], wq[:, kc, h * 192:h * 192 + 128], QN[:, kc, :], kc == 0, kc == 3,
                           [rwb, r_QN[kc]], [rps])
                    act(QNOPE[:, h, :], ps[:, :], AF.Copy, [rps], [r_QNOPE[h]], scale=qscale)
                    ps, rps = next_pa()
                    for kc in range(4):
                        mm(ps[0:64, :], wq[:, kc, h * 192 + 128:h * 192 + 192], QN[:, kc, :], kc == 0, kc == 3,
                           [rwb, r_QN[kc]], [rps])
                    rope(ps, rps, QPE[0:64, h, :], r_QPE[h], qscale)
                S.fence_merge(r_KS[2:4] + r_KPS[2:4] + r_VS[2:4], r_KTO + r_VTO)
                wkvs = []
                for half in range(2):
                    wb, rwb = next_wb()
                    wk = wb[:, 0:4096].rearrange("p (k n) -> p k n", k=4)
                    load_w(wk, rwb, wkvv[:, :, half * 1024:(half + 1) * 1024], wb)
                    wkvs.append((wk, rwb))
                for h in range(8):
                    wk, rwb = wkvs[h // 4]
                    hc = (h % 4) * 256
                    ps, rps = next_pa()
                    for kc in range(4):
                        mm(ps[:, :], wk[:, kc, hc:hc + 128], CN[:, kc, :], kc == 0, kc == 3, [rwb, r_CN[kc]], [rps])
                    i = h % 2
                    cpy(KTO[:, i, :], ps[:, :], [rps], [r_KTO[i]], eng="act")
                    S.dma("sp", kn_d[h, :, t0:t0 + TT], KTO[:, i, :], [r_KTO[i]], [r_kv[ti]])
                for sub in range(4):
                    i = sub % 2
                    for half in range(2):
                        wk, rwb = wkvs[half]
                        ps, rps = next_pa()
                        rhsv = wk.rearrange("p k (h c) -> p k h c", c=256)
                        for kc in range(4):
                            mm(ps[:, :].rearrange("p (h e) -> p h e", h=4), CN[:, kc, sub * 128:(sub + 1) * 128],
                               rhsv[:, kc, :, 128:256], kc == 0, kc == 3, [rwb, r_CN[kc]], [rps])
                        cpy(VTO[:, i, half * 4:(half + 1) * 4, :], ps[:, :].rearrange("p (h e) -> p h e", h=4),
                            [rps], [r_VTO[i]], eng="act")
                    S.dma("sp", vd_d[:, ti, :, sub, :].rearrange("h p e -> p h e"), VTO[:, i, :, :],
                          [r_VTO[i]], [r_kv[ti]])

                LOOK = int(os.environ.get('K_LOOK', '2'))
                S.fence_merge(r_KTO + r_VTO, r_KS[2:4] + r_KPS[2:4] + r_VS[2:4])
                for h in range(8):
                    psO, rpsO = PA[4], r_PA[4]
                    psD, rpsD = PA[5], r_PA[5]
                    blocks = [(kt, kb) for kt in range(ti + 1) for kb in range(4)]
                    loaded = {}

                    def ensure_loaded(kt, h=h, loaded=loaded):
                        if kt not in loaded:
                            i = nxt("ks", 4)
                            loaded[kt] = i
                            S.dma("sp", KSL[i], kn_d[h, :, kt * TT:(kt + 1) * TT], [r_kv[kt]], [r_KS[i]])
                            S.dma("sp", KPSL[i], kpe_d[:, kt * TT:(kt + 1) * TT], [r_kv[kt]], [r_KPS[i]])
                            S.dma("sp", VSL[i], vd_d[h, kt], [r_kv[kt]], [r_VS[i]])
                        return loaded[kt]

                    def emit_qk(bi, h=h):
                        kt, kb = blocks[bi]
                        i = ensure_loaded(kt)
                        q0 = kb * 128 if kt == ti else 0
                        qs = slice(q0, TT)
                        k4 = nxt("pa4", 4)
                        psS, rpsS = PA[k4], r_PA[k4]
                        mm(psS[:, qs], KSL[i][:, kb * 128:(kb + 1) * 128], QNOPE[:, h, qs], True, False,
                           [r_KS[i], r_QNOPE[h]], [rpsS])
                        mm(psS[:, qs], KPSL[i][:, kb * 128:(kb + 1) * 128], QPE[:, h, qs], False, True,
                           [r_KPS[i], r_QPE[h]], [rpsS])
                        j = nxt("pt", 4)
                        act(PT[:, j, qs], psS[:, qs], AF.Exp, [rpsS], [r_PT[j]])
                        if kt == ti:
                            tt(PT[:, j, q0:q0 + 128], PT[:, j, q0:q0 + 128], MASKC[:, :], ALU.mult,
                               [r_PT[j], r_const], [r_PT[j]])
                        return (i, j, qs, kb)

                    def emit_pv(bi, st):
                        i, j, qs, kb = st
                        first = bi == 0
                        last = bi == len(blocks) - 1
                        mm(psO[:, qs], VSL[i][:, kb, :], PT[:, j, qs], first, last, [r_VS[i], r_PT[j]], [rpsO])
                        mm(psD[:, qs], ONESB[:, :], PT[:, j, qs], first, last, [r_const, r_PT[j]], [rpsD])

                    pend = []
                    for bi in range(len(blocks) + LOOK):
                        if bi < len(blocks):
                            pend.append(emit_qk(bi))
                        if bi >= LOOK:
                            emit_pv(bi - LOOK, pend[bi - LOOK])
                    rd, rrd = next_rs()
                    S.op("dve", lambda e, rd=rd, psD=psD: e.reciprocal(out=rd, in_=psD[:, :]), [rpsD], [rrd])
                    tt(OMLA[:, h, :], psO[:, :], rd, ALU.mult, [rpsO, rrd], [r_OMLA[h]])
                for h in range(8):
                    sq, rsq = next_sq()
                    act(sq, OMLA[:, h, :], AF.Square, [r_OMLA[h]], [rsq])
                    mm(PSTAT[:, :], ONESB[:, :], sq, h == 0, h == 7, [rsq, r_const], [r_PSTAT])
                rstd3, r_rstd3 = rstd_from_psum(PSTAT, r_PSTAT, 1024.0)
                for h in range(8):
                    stt(HY[:, 8 + h, :], OMLA[:, h, :], PV[:, pb + 72 + h:pb + 73 + h], rstd3, ALU.mult, ALU.mult,
                        [r_OMLA[h], r_PV, r_rstd3], [r_y[8 + h]])

                S.fence(RA1 + RA5, RA4)
                S.fence(RB2 + RB3 + RB5, [r_XB])
                for q4 in range(4):
                    S.dma("sp", XB[:, 4 * q4:4 * q4 + 4, :],
                          src_d[q4 * 512:(q4 + 1) * 512, t0:t0 + TT].rearrange("(g p) t -> p g t", p=128),
                          [rsrc1], [r_XB])
                for g in range(16):
                    if g % 3 == 0:
                        gn = min(3, 16 - g)
                        wb, rwb = next_wb()
                        wv = wb[:, 0:16 * gn * 128].rearrange("p (k n) -> p k n", k=16)
                        load_w(wv, rwb, wov[:, :, g * 128:(g + gn) * 128], wb)
                        gbase = g
                    ps, rps = next_pa()
                    cc = g - gbase
                    for kc in range(16):
                        mm(ps[:, :], wv[:, kc, cc * 128:(cc + 1) * 128], HY[:, kc, :], kc == 0, kc == 15,
                           [rwb, r_y[kc]], [rps])
                    if g > 0:
                        mm(PSTAT[:, :], ONESB[:, :], pend_sq[0], g - 1 == 0, False, [pend_sq[1], r_const], [r_PSTAT])
                    cpy(MT[:, g, :], ps[:, :], [rps], [r_MT[g]], eng="act")
                    sq, rsq = next_sq()
                    act(sq, ps[:, :], AF.Square, [rps], [rsq])
                    pend_sq = (sq, rsq)
                mm(PSTAT[:, :], ONESB[:, :], pend_sq[0], False, True, [pend_sq[1], r_const], [r_PSTAT])
                rstd4, r_rstd4 = rstd_from_psum(PSTAT, r_PSTAT, float(D))
                for g in range(16):
                    stt(MT[:, g, :], MT[:, g, :], PV[:, pb + 16 + g:pb + 17 + g], rstd4, ALU.mult, ALU.mult,
                        [r_MT[g], r_PV, r_rstd4], [r_MT[g]])
                    tt(MT[:, g, :], MT[:, g, :], XB[:, g, :], ALU.add, [r_MT[g], r_XB], [r_MT[g]])
                    S.dma("sp", xmid_d[g * 128:(g + 1) * 128, t0:t0 + TT], MT[:, g, :], [r_MT[g]], [r_xmid[ti]])

                S.fence(r_y + r_h, r_u)
                for g in range(16):
                    sq, rsq = next_sq()
                    act(sq, MT[:, g, :], AF.Square, [r_MT[g]], [rsq])
                    mm(PSTAT[:, :], ONESB[:, :], sq, g == 0, g == 15, [rsq, r_const], [r_PSTAT])
                rstd5, r_rstd5 = rstd_from_psum(PSTAT, r_PSTAT, float(D))
                for g in range(16):
                    stt(HY[:, g, :], MT[:, g, :], PV[:, pb + 32 + g:pb + 33 + g], rstd5, ALU.mult, ALU.mult,
                        [r_MT[g], r_PV, r_rstd5], [r_u[g]])
                S.fence(RA1 + RA4, RA5)
                S.fence(RB2 + RB3 + [r_XB], RB5)
                for c0 in range(0, 44, 3):
                    gn = min(3, 44 - c0)
                    wbg, rwbg = next_wb()
                    wvg = wbg[:, 0:16 * gn * 128].rearrange("p (k n) -> p k n", k=16)
                    load_w(wvg, rwbg, wgv[:, :, c0 * 128:(c0 + gn) * 128], wbg)
                    wbu, rwbu = next_wb()
                    wvu = wbu[:, 0:16 * gn * 128].rearrange("p (k n) -> p k n", k=16)
                    load_w(wvu, rwbu, wuv[:, :, c0 * 128:(c0 + gn) * 128], wbu)
                    for cc in range(gn):
                        c = c0 + cc
                        psg, rpsg = next_pa()
                        for kc in range(16):
                            mm(psg[:, :], wvg[:, kc, cc * 128:(cc + 1) * 128], HY[:, kc, :], kc == 0, kc == 15,
                               [rwbg, r_u[kc]], [rpsg])
                        psu, rpsu = next_pa()
                        for kc in range(16):
                            mm(psu[:, :], wvu[:, kc, cc * 128:(cc + 1) * 128], HY[:, kc, :], kc == 0, kc == 15,
                               [rwbu, r_u[kc]], [rpsu])
                        sg, rsg = next_rs()
                        act(sg, psg[:, :], AF.Silu, [rpsg], [rsg])
                        tt(HID[:, c, :], sg, psu[:, :], ALU.mult, [rsg, rpsu], [r_HID[c]])
                nl, nti = (l, ti + 1) if ti + 1 < NT else (l + 1, 0)
                hoist = nl < L and not (NT == 1) and os.environ.get('K_HOIST', '1') == '1'
                if hoist:
                    prologue(nl, nti)
                for g in range(16):
                    wb, rwb = next_wb()
                    wv = wb[:, 0:44 * 128].rearrange("p (k n) -> p k n", k=44)
                    load_w(wv, rwb, wdv[:, :, g * 128:(g + 1) * 128], wb)
                    ps, rps = next_pa()
                    for kc in range(44):
                        mm(ps[:, :], wv[:, kc, :], HID[:, kc, :], kc == 0, kc == 43, [rwb, r_HID[kc]], [rps])
                    if g > 0:
                        mm(PSTAT[:, :], ONESB[:, :], pend_sq[0], g - 1 == 0, False, [pend_sq[1], r_const], [r_PSTAT])
                    cpy(FT[:, g, :], ps[:, :], [rps], [r_FT[g]], eng="act")
                    sq, rsq = next_sq()
                    act(sq, ps[:, :], AF.Square, [rps], [rsq])
                    pend_sq = (sq, rsq)
                mm(PSTAT[:, :], ONESB[:, :], pend_sq[0], False, True, [pend_sq[1], r_const], [r_PSTAT])
                rstd6, r_rstd6 = rstd_from_psum(PSTAT, r_PSTAT, float(D))
                rdst = [r_xres[ti]] if l < L - 1 else [Res("yout")]
                for g in range(16):
                    xb, rx = next_xs()
                    S.dma("sp", xb, xmid_d[g * 128:(g + 1) * 128, t0:t0 + TT], [r_xmid[ti]], [rx])
                    stt(FT[:, g, :], FT[:, g, :], PV[:, pb + 48 + g:pb + 49 + g], rstd6, ALU.mult, ALU.mult,
                        [r_FT[g], r_PV, r_rstd6], [r_FT[g]])
                    tt(FT[:, g, :], FT[:, g, :], xb, ALU.add, [r_FT[g], rx], [r_FT[g]])
                    S.dma("sp", dst_d[g * 128:(g + 1) * 128, t0:t0 + TT], FT[:, g, :], [r_FT[g]], rdst)
                if nl < L and not hoist:
                    prologue(nl, nti)

        S.drain("sp")
        build_program.stats = (S.nins, S.nwaits)
    return nc


def host_consts(L):
    c = np.zeros((128, 5 * 128), np.float32)
    c[:, 0:128] = np.eye(128, dtype=np.float32)
    j = np.arange(128)[:, None]
    i = np.arange(128)[None, :]
    c[:, 128:256] = ((j // 64 == i // 64) & (i >= j)).astype(np.float32)
    c[:, 256:384] = (i >= j).astype(np.float32)
    jj = np.arange(64)[:, None]
    ii = np.arange(64)[None, :]
    c[0:64, 384:448] = (jj == (ii + 32) % 64).astype(np.float32)
    c[:, 512] = EPS
    c[:, 513] = 1.0
    return c


def host_pvec(L, p):
    pv = np.zeros((128, L * NPL + 2), np.float32)

    def cols(v):
        return np.ascontiguousarray(np.asarray(v, np.float32).reshape(-1, 128).T)
    for l in range(L):
        b = l * NPL
        pv[:, b + 0:b + 16] = cols(p["attn_pre_norm"][l])
        pv[:, b + 16:b + 32] = cols(p["attn_post_norm"][l])
        pv[:, b + 32:b + 48] = cols(p["ffn_pre_norm"][l])
        pv[:, b + 48:b + 64] = cols(p["ffn_post_norm"][l])
        pv[:, b + 64:b + 68] = cols(p["mla_q_norm"][l])
        pv[:, b + 68:b + 72] = cols(p["mla_kv_norm"][l])
        pv[:, b + 72:b + 80] = cols(p["mla_out_norm"][l])
        pv[:, b + 80:b + 81] = cols(p["gla_out_norm"][l])
        pv[:, b + 81:b + 82] = cols(p["hgrn_out_norm"][l])
        pv[:, b + 82:b + 84] = cols(p["gla_gate_b"][l])
        pv[:, b + 84:b + 88] = cols(p["hgrn_lb_logits"][l])
    inv_freq = (10000.0 ** (-np.arange(0, 64, 2, dtype=np.float32) / 64.0)).astype(np.float32)
    pv[0:64, L * NPL] = np.concatenate([inv_freq, inv_freq])
    pv[0:32, L * NPL + 1] = -1.0
    pv[32:64, L * NPL + 1] = 1.0
    return pv


_PROG_CACHE = {}


def run(inputs, T, L, B):
    key = (T, L)
    if key not in _PROG_CACHE:
        _PROG_CACHE[key] = build_program(T, L)
    nc = _PROG_CACHE[key]
    x = np.asarray(inputs["x"], np.float32)
    pos = np.asarray(inputs["positions"], np.int32)
    pv = host_pvec(L, inputs)
    cs = host_consts(L)
    shared = {
        "pvec": pv, "consts": cs,
        "w_in": np.ascontiguousarray(np.asarray(inputs["w_in"], np.float32)),
        "gla_gate_w2": np.ascontiguousarray(np.asarray(inputs["gla_gate_w2"], np.float32)),
        "mla_wq_b": np.ascontiguousarray(np.asarray(inputs["mla_wq_b"], np.float32)),
        "mla_wkv_b": np.ascontiguousarray(np.asarray(inputs["mla_wkv_b"], np.float32)),
        "w_out": np.ascontiguousarray(np.asarray(inputs["w_out"], np.float32)),
        "w_gate": np.ascontiguousarray(np.asarray(inputs["w_gate"], np.float32)),
        "w_up": np.ascontiguousarray(np.asarray(inputs["w_up"], np.float32)),
        "w_down": np.ascontiguousarray(np.asarray(inputs["w_down"], np.float32)),
    }
    work = {0: 0, 4: 1} if B == 2 else {c: c for c in range(B)}
    zeros = {k: np.zeros_like(v) for k, v in shared.items()}
    zx = np.zeros((D, T), np.float32)
    zp = np.zeros((1, T), np.int32)
    in_maps = []
    for c in range(NCORES):
        if c in work:
            b = work[c]
            m = dict(shared)
            m["xT"] = np.ascontiguousarray(x[b].T)
            m["pos"] = np.ascontiguousarray(pos[b].reshape(1, T))
        else:
            m = dict(zeros)
            m["xT"] = zx
            m["pos"] = zp
        in_maps.append(m)
    res = run_bass_kernel_spmd(nc, in_maps, core_ids=list(range(NCORES)))
    inv = {b: c for c, b in work.items()}
    out = np.stack([np.ascontiguousarray(res.results[inv[b]]["yT"].T) for b in range(B)], axis=0)
    return out.astype(np.float32)


def kernel(**inputs):
    x = inputs["x"]
    B, T, _ = x.shape
    L = inputs["w_in"].shape[0]
    return run(inputs, T, L, B)
```

```python
import contextlib
import os
import numpy as np
import concourse.bass as bass
import concourse.mybir as mybir
from concourse.bass_utils import run_bass_kernel_spmd

F32 = mybir.dt.float32
BF16 = mybir.dt.bfloat16
I32 = mybir.dt.int32
AF = mybir.ActivationFunctionType
ALU = mybir.AluOpType

D = 2048
DIN = 4688
DFF = 5632
TT = 512
EPS = 1e-6
O_GQ, O_GK, O_GV, O_GLOW, O_GOUT, O_HQ, O_HF, O_HI, O_HOUT, O_QC, O_KVC, O_KPE = (
    0, 256, 512, 1024, 1040, 1552, 2064, 2576, 3088, 3600, 4112, 4624)
NPL = 88
MAGIC = 12582912.0
NCORES = 8


class Res:
    __slots__ = ("n", "w", "r")

    def __init__(self, n):
        self.n = n
        self.w = {}
        self.r = {}


def _merge(d, k, v):
    if d.get(k, 0) < v:
        d[k] = v


class Sched:
    def __init__(self, nc, es):
        self.nc = nc
        self.E = {"pe": nc.tensor, "act": nc.scalar, "dve": nc.vector, "pool": nc.gpsimd, "sp": nc.sync}
        self.sem = {}
        self.cnt = {}
        for e in ("pe", "act", "dve", "pool"):
            self.sem[e] = es.enter_context(nc.semaphore("s_" + e))
            self.cnt[e] = 0
        self.dslots = {"sp": 12, "pool": 8}
        self.dnext = {"sp": 0, "pool": 0}
        for q, n in self.dslots.items():
            for i in range(n):
                self.sem[(q, i)] = es.enter_context(nc.semaphore("d_%s%d" % (q, i)))
                self.cnt[(q, i)] = 0
        self.waited = {e: {} for e in self.E}
        self.nwaits = 0
        self.nins = 0

    def _deps(self, reads, writes):
        deps = {}
        for r in reads:
            for k, v in r.w.items():
                _merge(deps, k, v)
        for w in writes:
            for k, v in w.w.items():
                _merge(deps, k, v)
            for k, v in w.r.items():
                _merge(deps, k, v)
        return deps

    def _wait(self, e, deps):
        wd = self.waited[e]
        for k, v in deps.items():
            if k == "pe" and e == "pe":
                continue
            if wd.get(k, 0) >= v:
                continue
            self.E[e].wait_ge(self.sem[k], v)
            wd[k] = v
            self.nwaits += 1

    def op(self, e, fn, reads=(), writes=()):
        self._wait(e, self._deps(reads, writes))
        ins = fn(self.E[e])
        self.cnt[e] += 1
        c = self.cnt[e]
        ins.then_inc(self.sem[e], 1)
        self.nins += 1
        for r in reads:
            _merge(r.r, e, c)
        for w in writes:
            w.w = {e: c}
            w.r = {}

    def dma(self, q, out, in_, reads=(), writes=()):
        i = self.dnext[q]
        self.dnext[q] = (i + 1) % self.dslots[q]
        key = (q, i)
        deps = self._deps(reads, writes)
        if self.cnt[key] > 0:
            _merge(deps, key, 16 * self.cnt[key])
        self._wait(q, deps)
        ins = self.E[q].dma_start(out=out, in_=in_)
        self.cnt[key] += 1
        v = 16 * self.cnt[key]
        ins.then_inc(self.sem[key], 16)
        self.nins += 1
        for r in reads:
            _merge(r.r, key, v)
        for w in writes:
            w.w = {key: v}
            w.r = {}

    def fence(self, old, new):
        d = {}
        for o in old:
            for k, v in o.w.items():
                _merge(d, k, v)
            for k, v in o.r.items():
                _merge(d, k, v)
        for n in new:
            n.w = dict(d)
            n.r = {}

    def fence_merge(self, old, new):
        d = {}
        for o in old:
            for k, v in o.w.items():
                _merge(d, k, v)
            for k, v in o.r.items():
                _merge(d, k, v)
        for n in new:
            for k, v in d.items():
                _merge(n.w, k, v)

    def drain(self, e="sp"):
        deps = {}
        for k, c in self.cnt.items():
            if c > 0:
                deps[k] = c * 16 if isinstance(k, tuple) else c
        self._wait(e, deps)


def build_program(T, L, dbg=False):
    NT = T // TT
    nc = bass.Bass("TRN2", target_bir_lowering=False)
    NPV = L * NPL + 2

    def din(name, shape, dt=F32):
        return nc.dram_tensor(name, list(shape), dt, kind="ExternalInput").ap()

    xT_in = din("xT", [D, T])
    pos_in = din("pos", [1, T], I32)
    pvec_in = din("pvec", [128, NPV])
    consts_in = din("consts", [128, 5 * 128])
    w_in = din("w_in", [L, D, DIN])
    w2_in = din("gla_gate_w2", [L, 16, 256])
    wq_in = din("mla_wq_b", [L, 512, 1536])
    wkv_in = din("mla_wkv_b", [L, 512, 2048])
    wo_in = din("w_out", [L, D, D])
    wg_in = din("w_gate", [L, D, DFF])
    wu_in = din("w_up", [L, D, DFF])
    wd_in = din("w_down", [L, DFF, D])
    yT_out = nc.dram_tensor("yT", [D, T], F32, kind="ExternalOutput").ap()

    xmid_d = nc.dram_tensor("xmid_d", [D, T], F32).ap()
    xres_d = nc.dram_tensor("xres_d", [D, T], F32).ap()
    kn_d = nc.dram_tensor("kn_d", [8, 128, T], BF16).ap()
    kpe_d = nc.dram_tensor("kpe_d", [128, T], BF16).ap()
    vd_d = nc.dram_tensor("vd_d", [8, NT, 128, 4, 128], BF16).ap()
    cc_d = nc.dram_tensor("cc_d", [64, T], F32).ap()
    ss_d = nc.dram_tensor("ss_d", [64, T], F32).ap()
    NUNIT = 80
    wcache = nc.dram_tensor("wcache", [NUNIT, 128, 6144], BF16).ap()

    es = contextlib.ExitStack()
    with es:
        S = Sched(nc, es)

        def sb(name, shape, dt):
            return es.enter_context(nc.sbuf_tensor(name, list(shape), dt))

        HY = sb("HY", [128, 16, TT], BF16)
        WB = [sb("WB%d" % i, [128, 6144], BF16) for i in range(4)]
        WSM = sb("WSM", [128, 16, 80], BF16)
        XS = sb("XS", [128, 4, TT], F32)
        SQ = sb("SQ", [128, 3, TT], BF16)
        RS = sb("RS", [128, 3, TT], F32)
        SG = sb("SG", [128, 6, 128], F32)
        SBs = sb("SBs", [128, 6, 128], BF16)
        CS = sb("CS", [128, 2, TT], F32)
        CONF = sb("CONF", [128, 5 * 128], F32)
        IDENT = sb("IDENT", [128, 128], BF16)
        ONESB = sb("ONESB", [128, 128], BF16)
        MASKC = sb("MASKC", [128, 128], BF16)
        PERM = sb("PERM", [128, 64], BF16)
        ONEF = sb("ONEF", [128, TT], F32)
        PV = sb("PV", [128, NPV], F32)
        NEGB = sb("NEGB", [128, L, 2], F32)
        LBt = sb("LBt", [128, 4, L], F32)
        OMLt = sb("OMLt", [128, 4, L], F32)
        SMT = sb("SMT", [128, 4, 8], F32)
        W2 = sb("W2", [16, L, 256], BF16)
        GLOW = sb("GLOW", [16, TT], BF16)
        DEC = sb("DEC", [128, 6, 8], F32)
        WG = sb("WG", [128, 6, 8], F32)
        SI = sb("SI", [128, 6, 8], F32)
        CT = sb("CT", [128, 6, 8], F32)
        RA = sb("RA", [128, 12288], F32)
        RB = sb("RB", [128, 12288], F32)

        def view(reg, off, words, dt, pat=None, **kw):
            a = reg[:, off:off + words]
            if dt == BF16:
                a = a.bitcast(BF16)
            if pat:
                a = a.rearrange(pat, **kw)
            return a

        QF = view(RA, 0, 1536, BF16, "p (a t) -> p a t", a=6)
        KF = view(RA, 1536, 1536, BF16, "p (a t) -> p a t", a=6)
        BFv = view(RA, 3072, 3072, F32, "p (a t) -> p a t", a=6)
        VG = view(RA, 6144, 1024, BF16, "p (a t) -> p a t", a=4)
        VH = view(RA, 7168, 1024, BF16, "p (a t) -> p a t", a=4)
        QN = view(RA, 8192, 1024, BF16, "p (a t) -> p a t", a=4)
        CN = view(RA, 9216, 1024, BF16, "p (a t) -> p a t", a=4)
        GS = view(RA, 10240, 2048, BF16, "p (a t) -> p a t", a=8)
        MT = view(RA, 0, 8192, F32, "p (a t) -> p a t", a=16)
        HID = view(RA, 0, 11264, BF16, "p (a t) -> p a t", a=44)
        QT = view(RB, 0, 1536, BF16, "p (a t) -> p a t", a=6)
        KT = view(RB, 1536, 1536, BF16, "p (a t) -> p a t", a=6)
        KTT = view(RB, 3072, 1536, BF16, "p (a s d) -> p a s d", a=6, s=4)
        ET = view(RB, 4608, 1536, F32, "p (a t) -> p a t", a=3)
        OT = view(RB, 6144, 4096, F32, "p (a t) -> p a t", a=8)
        ATMv = view(RB, 10240, 512, BF16, "p (a t) -> p a t", a=8)
        K2 = view(RB, 10752, 1536, BF16, "p (a t) -> p a t", a=6)
        QNOPE = view(RB, 0, 2048, BF16, "p (a t) -> p a t", a=8)
        QPE = view(RB, 2048, 2048, BF16, "p (a t) -> p a t", a=8)
        KS = view(RB, 4096, 512, BF16, "p (a t) -> p a t", a=2)
        KPS = view(RB, 4608, 512, BF16, "p (a t) -> p a t", a=2)
        VS = view(RB, 5120, 512, BF16, "p (a b e) -> p a b e", a=2, b=4)
        PT = view(RB, 5632, 1024, BF16, "p (a t) -> p a t", a=4)
        KSL = [KS[:, 0, :], KS[:, 1, :]]
        KPSL = [KPS[:, 0, :], KPS[:, 1, :]]
        VSL = [VS[:, 0, :, :], VS[:, 1, :, :]]
        for _k in range(2):
            _b = 6656 + _k * 768
            KSL.append(view(RB, _b, 256, BF16))
            KPSL.append(view(RB, _b + 256, 256, BF16))
            VSL.append(view(RB, _b + 512, 256, BF16, "p (b e) -> p b e", b=4))
        KTO = view(RB, 6656, 512, BF16, "p (a t) -> p a t", a=2)
        VTO = view(RB, 7168, 1024, BF16, "p (a h e) -> p a h e", a=2, h=8)
        OMLA = view(RB, 8192, 4096, F32, "p (a t) -> p a t", a=8)
        FT = view(RB, 0, 8192, F32, "p (a t) -> p a t", a=16)
        XB = view(RB, 0, 8192, F32, "p (a t) -> p a t", a=16)

        PA = [es.enter_context(nc.psum_tensor("PA%d" % i, [128, TT], F32)) for i in range(6)]
        PSTAT = es.enter_context(nc.psum_tensor("PSTAT", [128, TT], F32))
        PTR = es.enter_context(nc.psum_tensor("PTR", [128, 8, 128], BF16))

        def R(n):
            return Res(n)

        def RL(n, k):
            return [Res("%s%d" % (n, i)) for i in range(k)]

        r_PA = RL("PA", 6)
        r_PSTAT = R("PSTAT")
        r_PTR = RL("PTR", 8)
        r_WB = RL("WB", 4)
        r_WSM = R("WSM")
        r_XS = RL("XS", 4)
        r_SQ = RL("SQ", 3)
        r_RS = RL("RS", 3)
        r_SG = RL("SG", 6)
        r_SBs = RL("SBs", 6)
        r_CS = R("CS")
        r_const = R("const")
        r_PV = R("PV")
        r_GLOW = R("GLOW")
        r_CH = R("CH")
        r_h = RL("h", 16)
        r_y = RL("y", 16)
        r_u = RL("u", 16)
        r_QF, r_KF, r_BF = RL("QF", 6), RL("KF", 6), RL("BF", 6)
        r_VG, r_VH = RL("VG", 4), RL("VH", 4)
        r_QN, r_CN = RL("QN", 4), RL("CN", 4)
        r_GS = RL("GS", 8)
        r_MT = RL("MT", 16)
        r_HID = RL("HID", 44)
        r_QT, r_KT = RL("QT", 6), RL("KT", 6)
        r_KTT = [RL("KTT%d_" % a, 4) for a in range(6)]
        r_ET = RL("ET", 3)
        r_OT = RL("OT", 8)
        r_ATM = RL("ATM", 8)
        r_K2 = RL("K2", 6)
        r_QNOPE, r_QPE = RL("QNOPE", 8), RL("QPE", 8)
        r_KS, r_KPS, r_VS = RL("KS", 4), RL("KPS", 4), RL("VS", 4)
        r_PT = RL("PT", 4)
        r_KTO, r_VTO = RL("KTO", 2), RL("VTO", 2)
        r_OMLA = RL("OMLA", 8)
        r_FT = RL("FT", 16)
        r_XB = R("XB")
        RA1 = r_QF + r_KF + r_BF + r_VG + r_VH + r_QN + r_CN + r_GS
        RA4 = r_MT
        RA5 = r_HID
        RB2 = r_QT + r_KT + sum(r_KTT, []) + r_ET + r_OT + r_ATM + r_K2
        RB3 = r_QNOPE + r_QPE + r_KS + r_KPS + r_VS + r_PT + r_KTO + r_VTO + r_OMLA
        RB5 = r_FT
        r_xmid = RL("xmid", NT)
        r_xres = RL("xres", NT)
        r_kv = RL("kv", NT)
        r_ccss = RL("ccss", NT)
        r_cache = RL("wcache", 80)

        rot = {"pa": 0, "ptr": 0, "xs": 0, "sq": 0, "rs": 0, "wb": 0, "et": 0, "pt": 0, "atm": 0, "ks": 0, "pa4": 0, "pa3": 0}

        def nxt(name, n):
            i = rot[name]
            rot[name] = (i + 1) % n
            return i

        def next_pa():
            i = nxt("pa", 6)
            return PA[i], r_PA[i]

        def next_wb():
            i = nxt("wb", 4)
            return WB[i], r_WB[i]

        def next_xs():
            i = nxt("xs", 4)
            return XS[:, i, :], r_XS[i]

        def next_sq():
            i = nxt("sq", 3)
            return SQ[:, i, :], r_SQ[i]

        def next_rs():
            i = nxt("rs", 3)
            return RS[:, i, :], r_RS[i]

        def mm(out, lhsT, rhs, start, stop, reads, writes):
            S.op("pe", lambda e: e.matmul(out, lhsT=lhsT, rhs=rhs, start=start, stop=stop), reads, writes)

        def act(out, in_, func, reads, writes, scale=None, bias=None):
            kw = {}
            if scale is not None:
                kw["scale"] = scale
            if bias is not None:
                kw["bias"] = bias
            S.op("act", lambda e: e.activation(out=out, in_=in_, func=func, **kw), reads, writes)

        def tt(out, in0, in1, op, reads, writes, eng="dve"):
            S.op(eng, lambda e: e.tensor_tensor(out=out, in0=in0, in1=in1, op=op), reads, writes)

        def ts(out, in0, s1, s2, op0, op1, reads, writes, eng="dve"):
            if op1 is None:
                S.op(eng, lambda e: e.tensor_scalar(out=out, in0=in0, scalar1=s1, scalar2=None, op0=op0), reads, writes)
            else:
                S.op(eng, lambda e: e.tensor_scalar(out=out, in0=in0, scalar1=s1, scalar2=s2, op0=op0, op1=op1),
                     reads, writes)

        def stt(out, in0, scalar, in1, op0, op1, reads, writes):
            S.op("dve", lambda e: e.scalar_tensor_tensor(out=out, in0=in0, scalar=scalar, in1=in1, op0=op0, op1=op1),
                 reads, writes)

        def cpy(out, in_, reads, writes, eng="dve"):
            if eng == "act":
                S.op("act", lambda e: e.copy(out=out, in_=in_), reads, writes)
            else:
                S.op(eng, lambda e: e.tensor_copy(out=out, in_=in_), reads, writes)

        def rstd_from_psum(ps, r_ps, dim):
            t1, r1 = next_rs()
            act(t1, ps[:, :], AF.Ln, [r_ps, r_const], [r1], scale=1.0 / dim, bias=EPSC[:, 0:1])
            t2, r2 = next_rs()
            act(t2, t1, AF.Exp, [r1], [r2], scale=-0.5)
            return t2, r2

        S.dma("sp", CONF[:], consts_in[:, :], [], [r_const])
        S.dma("sp", PV[:], pvec_in[:, :], [], [r_PV])
        for l in range(L):
            S.dma("pool", W2[:, l, :], w2_in[l], [], [r_const])
        cpy(IDENT[:], CONF[:, 0:128], [r_const], [r_const])
        cpy(MASKC[:], CONF[:, 256:384], [r_const], [r_const])
        cpy(PERM[:], CONF[:, 384:448], [r_const], [r_const])
        MASKB = CONF[:, 128:256]
        EPSC = CONF[:, 512:640]
        S.op("dve", lambda e: e.memset(ONESB[:], 1.0), [], [r_const])
        S.op("dve", lambda e: e.memset(ONEF[:], 1.0), [], [r_const])
        S.op("dve", lambda e: e.memset(SG[:], 0.0), [], r_SG)
        pvl = PV[:, 0:L * NPL].rearrange("p (l c) -> p l c", c=NPL)
        ts(NEGB[:], pvl[:, :, 82:84], -1.0, None, ALU.mult, None, [r_PV], [r_PV])
        lg = pvl[:, :, 84:88].rearrange("p l t -> p t l")
        S.op("dve", lambda e: e.tensor_reduce(out=SMT[:, :, 0:1], in_=lg, axis=mybir.AxisListType.X, op=ALU.max),
             [r_PV], [r_CH])
        tt(LBt[:], lg, SMT[:, :, 0:1].broadcast_to([128, 4, L]), ALU.subtract, [r_PV, r_CH], [r_PV])
        act(LBt[:], LBt[:], AF.Exp, [r_PV], [r_PV])
        S.op("dve", lambda e: e.tensor_reduce(out=SMT[:, :, 1:2], in_=LBt[:], axis=mybir.AxisListType.X, op=ALU.add),
             [r_PV], [r_CH])
        S.op("dve", lambda e: e.reciprocal(out=SMT[:, :, 2:3], in_=SMT[:, :, 1:2]), [r_CH], [r_CH])
        tt(LBt[:], LBt[:], SMT[:, :, 2:3].broadcast_to([128, 4, L]), ALU.mult, [r_PV, r_CH], [r_PV])
        cpy(SMT[:, :, 3:4], LBt[:, :, 0:1], [r_PV], [r_CH])
        for l in range(1, L):
            tt(LBt[:, :, l:l + 1], LBt[:, :, l:l + 1], LBt[:, :, l - 1:l], ALU.add, [r_PV], [r_PV])
        tt(LBt[:], LBt[:], SMT[:, :, 3:4].broadcast_to([128, 4, L]), ALU.subtract, [r_PV, r_CH], [r_PV])
        ts(OMLt[:], LBt[:], -1.0, 1.0, ALU.mult, ALU.add, [r_PV], [r_PV])

        INVF = PV[0:64, L * NPL:L * NPL + 1]
        SGN = PV[0:64, L * NPL + 1:L * NPL + 2]
        for ti in range(NT):
            t0 = ti * TT
            xi, rxi = next_xs()
            S.dma("sp", xi[0:64, :].bitcast(I32), pos_in[:, t0:t0 + TT].partition_broadcast(64), [], [rxi])
            rr, rrr = next_xs()
            cpy(rr[0:64, :], xi[0:64, :].bitcast(I32), [rxi], [rrr])
            ts(rr[0:64, :], rr[0:64, :], INVF, 1.0 / (2.0 * np.pi), ALU.mult, ALU.mult, [rrr, r_PV], [rrr])
            for which in range(2):
                a, ra = next_rs()
                b, rb = next_rs()
                if which == 0:
                    ts(a[0:64, :], rr[0:64, :], 0.25, None, ALU.add, None, [rrr], [ra])
                    src = a
                    rsrc = ra
                else:
                    src = rr
                    rsrc = rrr
                ts(b[0:64, :], src[0:64, :], MAGIC, None, ALU.add, None, [rsrc], [rb])
                ts(b[0:64, :], b[0:64, :], MAGIC, None, ALU.subtract, None, [rb], [rb])
                tt(b[0:64, :], src[0:64, :], b[0:64, :], ALU.subtract, [rsrc, rb], [rb])
                o, ro = next_xs()
                act(o[0:64, :], b[0:64, :], AF.Sin, [rb], [ro], scale=6.283185)
                if which == 1:
                    ts(o[0:64, :], o[0:64, :], SGN, None, ALU.mult, None, [ro, r_PV], [ro])
                    S.dma("sp", ss_d[:, t0:t0 + TT], o[0:64, :], [ro], [r_ccss[ti]])
                else:
                    S.dma("sp", cc_d[:, t0:t0 + TT], o[0:64, :], [ro], [r_ccss[ti]])

        zt, rzt = next_sq()
        S.op("dve", lambda e: e.memset(zt, 0.0), [], [rzt])
        for ti in range(NT):
            S.dma("sp", kpe_d[64:128, ti * TT:(ti + 1) * TT], zt[0:64, :], [rzt], [r_kv[ti]])

        ucnt = [0]
        cur_ti = [0]

        def load_w(dst, rdst, src, wb):
            u = ucnt[0]
            ucnt[0] += 1
            n = 1
            for dd in dst.shape[1:]:
                n *= dd
            flat = wb[:, 0:n]
            if cur_ti[0] == 0 or NT == 1:
                S.dma("pool", dst, src, [], [rdst])
                if NT > 1:
                    S.dma("sp", wcache[u, :, 0:n], flat, [rdst], [r_cache[u]])
            else:
                S.dma("pool", flat, wcache[u, :, 0:n], [r_cache[u]], [rdst])

        def rms_stats_from_dram(src_d, r_src, t0):
            for kc in range(16):
                xb, rx = next_xs()
                S.dma("sp", xb, src_d[kc * 128:(kc + 1) * 128, t0:t0 + TT], [r_src], [rx])
                sq, rsq = next_sq()
                act(sq, xb, AF.Square, [rx], [rsq])
                mm(PSTAT[:, :], ONESB[:, :], sq, kc == 0, kc == 15, [rsq, r_const], [r_PSTAT])
            return rstd_from_psum(PSTAT, r_PSTAT, float(D))

        def normed_from_dram(src_d, r_src, t0, gcol, dst_res, rstd, r_rstd):
            for kc in range(16):
                xb, rx = next_xs()
                S.dma("sp", xb, src_d[kc * 128:(kc + 1) * 128, t0:t0 + TT], [r_src], [rx])
                stt(HY[:, kc, :], xb, PV[:, gcol + kc:gcol + kc + 1], rstd, ALU.mult, ALU.mult,
                    [rx, r_PV, r_rstd], [dst_res[kc]])

        def proj_fm(wsrc_cols, kdim_chunks, rhs_fn, rhs_res, nchunk, evac):
            for c in range(nchunk):
                ps, rps = next_pa()
                for kc in range(kdim_chunks):
                    lhsT, rw, m = wsrc_cols(kc, c)
                    mm(ps[0:m, :], lhsT, rhs_fn(kc), kc == 0, kc == kdim_chunks - 1,
                       [rw, rhs_res[kc]], [rps])
                evac(c, ps, rps)

        def prologue(pl, pti):
            psrc = xT_in if pl == 0 else xres_d
            pres = Res("xin") if pl == 0 else r_xres[pti]
            pt0 = pti * TT
            S.fence(r_u + r_y, r_h)
            prstd, pr_rstd = rms_stats_from_dram(psrc, pres, pt0)
            normed_from_dram(psrc, pres, pt0, pl * NPL + 0, r_h, prstd, pr_rstd)
            S.dma("sp", CS[0:64, 0, :], cc_d[:, pt0:pt0 + TT], [r_ccss[pti]], [r_CS])
            S.dma("sp", CS[0:64, 1, :], ss_d[:, pt0:pt0 + TT], [r_ccss[pti]], [r_CS])

        for l in range(L):
            pb = l * NPL
            src_d = xT_in if l == 0 else xres_d
            r_src_t = None if l == 0 else r_xres
            dst_d = yT_out if l == L - 1 else xres_d
            winv = w_in[l].rearrange("(kc p) n -> p kc n", p=128)
            wqv = wq_in[l].rearrange("(kc p) n -> p kc n", p=128)
            wkvv = wkv_in[l].rearrange("(kc p) n -> p kc n", p=128)
            wov = wo_in[l].rearrange("(kc p) n -> p kc n", p=128)
            wgv = wg_in[l].rearrange("(kc p) n -> p kc n", p=128)
            wuv = wu_in[l].rearrange("(kc p) n -> p kc n", p=128)
            wdv = wd_in[l].rearrange("(kc p) n -> p kc n", p=128)
            S.op("dve", lambda e: e.memset(SG[:], 0.0), [], r_SG)

            for ti in range(NT):
                t0 = ti * TT
                ucnt[0] = 0
                cur_ti[0] = ti
                r_src = [] if r_src_t is None else [r_src_t[ti]]
                rsrc1 = r_src[0] if r_src else Res("xin")

                if l == 0 and ti == 0:
                    prologue(0, 0)
                S.fence(RA5 + RA4, RA1)

                def hrhs(kc):
                    return HY[:, kc, :]

                wsm_c = wcache[79, :, 0:1280].rearrange("p (k n) -> p k n", k=16)
                if ti == 0 or NT == 1:
                    S.dma("pool", WSM[:, :, 0:16], winv[:, :, O_GLOW:O_GLOW + 16], [], [r_WSM])
                    S.dma("pool", WSM[:, :, 16:80], winv[:, :, O_KPE:O_KPE + 64], [], [r_WSM])
                    if NT > 1:
                        S.dma("sp", wsm_c, WSM[:, :, :], [r_WSM], [r_cache[79]])
                else:
                    S.dma("pool", WSM[:, :, :], wsm_c, [r_cache[79]], [r_WSM])

                def load_group(col0, ncols):
                    wb, rwb = next_wb()
                    wv = wb[:, 0:16 * ncols].rearrange("p (k n) -> p k n", k=16)
                    load_w(wv, rwb, winv[:, :, col0:col0 + ncols], wb)
                    return wv, rwb

                def fm_group(col0, nchunk, evac):
                    c = 0
                    while c < nchunk:
                        g = min(3, nchunk - c)
                        wv, rwb = load_group(col0 + c * 128, g * 128)
                        base = c
                        proj_fm(lambda kc, cc, wv=wv, rwb=rwb: (wv[:, kc, cc * 128:(cc + 1) * 128], rwb, 128),
                                16, hrhs, r_h, g, lambda cc, ps, rps, base=base: evac(base + cc, ps, rps))
                        c += g

                def tm_group(col0, dstv, dres):
                    for half in range(2):
                        wv, rwb = load_group(col0 + half * 256, 256)
                        for sub in range(4):
                            ps, rps = next_pa()
                            for kc in range(16):
                                mm(ps[:, 0:256], HY[:, kc, sub * 128:(sub + 1) * 128], wv[:, kc, :],
                                   kc == 0, kc == 15, [rwb, r_h[kc]], [rps])
                            cpy(dstv[:, sub, half * 256:(half + 1) * 256], ps[:, 0:256], [rps], [dres[sub]],
                                eng="act")

                def ev_gqk(c, ps, rps):
                    if c < 2:
                        act(QF[:, c, :], ps[:, :], AF.Copy, [rps], [r_QF[c]], scale=0.125)
                    else:
                        cpy(KF[:, c - 2, :], ps[:, :], [rps], [r_KF[c - 2]], eng="act")
                fm_group(O_GQ, 4, ev_gqk)
                tm_group(O_GV, VG, r_VG)
                ps, rps = next_pa()
                for kc in range(16):
                    mm(ps[0:16, :], WSM[:, kc, 0:16], HY[:, kc, :], kc == 0, kc == 15, [r_WSM, r_h[kc]], [rps])
                cpy(GLOW[:, :], ps[0:16, :], [rps], [r_GLOW], eng="act")
                for c in range(2):
                    ps, rps = next_pa()
                    mm(ps[:, :], W2[:, l, c * 128:(c + 1) * 128], GLOW[:, :], True, True, [r_const, r_GLOW], [rps])
                    e1, re1 = next_rs()
                    act(e1, ps[:, :], AF.Exp, [rps, r_PV], [re1], scale=-1.0, bias=NEGB[:, l, c:c + 1])
                    e2, re2 = next_rs()
                    act(e2, e1, AF.Ln, [re1, r_const], [re2], bias=EPSC[:, 1:2])
                    S.op("dve", lambda e, e2=e2, c=c: e.tensor_tensor_scan(
                        out=BFv[:, c, :], data0=ONEF[:, :], data1=e2, initial=0.0, op0=ALU.mult, op1=ALU.subtract),
                        [re2, r_const], [r_BF[c]])
                fm_group(O_GOUT, 4, lambda c, ps, rps: act(GS[:, c, :], ps[:, :], AF.Silu, [rps], [r_GS[c]]))
                fm_group(O_HQ, 4, lambda c, ps, rps: act(QF[:, 2 + c, :], ps[:, :], AF.Silu, [rps], [r_QF[2 + c]]))

                def ev_hf(c, ps, rps):
                    sg, rsg = next_rs()
                    act(sg, ps[:, :], AF.Sigmoid, [rps], [rsg])
                    f, rf = next_rs()
                    ts(f, sg, OMLt[:, c, l:l + 1], LBt[:, c, l:l + 1], ALU.mult, ALU.add, [rsg, r_PV], [rf])
                    ts(KF[:, 2 + c, :], f, -1.0, 1.0, ALU.mult, ALU.add, [rf], [r_KF[2 + c]])
                    lf, rlf = next_rs()
                    act(lf, f, AF.Ln, [rf], [rlf])
                    S.op("dve", lambda e, lf=lf, c=c: e.tensor_tensor_scan(
                        out=BFv[:, 2 + c, :], data0=ONEF[:, :], data1=lf, initial=0.0, op0=ALU.mult, op1=ALU.add),
                        [rlf, r_const], [r_BF[2 + c]])
                fm_group(O_HF, 4, ev_hf)
                tm_group(O_HI, VH, r_VH)
                fm_group(O_HOUT, 4, lambda c, ps, rps: act(GS[:, 4 + c, :], ps[:, :], AF.Silu, [rps], [r_GS[4 + c]]))

                def latent(col0, gcol, dstv, dres):
                    tmp = []

                    def ev(c, ps, rps):
                        xb, rx = next_xs()
                        cpy(xb, ps[:, :], [rps], [rx], eng="act")
                        sq, rsq = next_sq()
                        act(sq, ps[:, :], AF.Square, [rps], [rsq])
                        mm(PSTAT[:, :], ONESB[:, :], sq, c == 0, c == 3, [rsq, r_const], [r_PSTAT])
                        tmp.append((xb, rx))
                    c = 0
                    wv, rwb = load_group(col0, 384)
                    proj_fm(lambda kc, cc: (wv[:, kc, cc * 128:(cc + 1) * 128], rwb, 128), 16, hrhs, r_h, 3, ev)
                    wv2, rwb2 = load_group(col0 + 384, 128)
                    proj_fm(lambda kc, cc: (wv2[:, kc, 0:128], rwb2, 128), 16, hrhs, r_h, 1,
                            lambda cc, ps, rps: ev(3, ps, rps))
                    rstd2, r_rstd2 = rstd_from_psum(PSTAT, r_PSTAT, 512.0)
                    for c in range(4):
                        xb, rx = tmp[c]
                        stt(dstv[:, c, :], xb, PV[:, gcol + c:gcol + c + 1], rstd2, ALU.mult, ALU.mult,
                            [rx, r_PV, r_rstd2], [dres[c]])
                latent(O_QC, pb + 64, QN, r_QN)
                latent(O_KVC, pb + 68, CN, r_CN)

                S.fence(RB5 + RB3 + [r_XB], RB2)
                ps, rps = next_pa()
                for kc in range(16):
                    mm(ps[0:64, :], WSM[:, kc, 16:80], HY[:, kc, :], kc == 0, kc == 15, [r_WSM, r_h[kc]], [rps])

                def rope(ps, rps, dst, rdst, scale):
                    qb, rqb = next_sq()
                    act(qb[0:64, :], ps[0:64, :], AF.Copy, [rps], [rqb], scale=scale)
                    ps2, rps2 = next_pa()
                    mm(ps2[0:64, :], PERM[0:64, :], qb[0:64, :], True, True, [r_const, rqb], [rps2])
                    a, ra = next_xs()
                    tt(a[0:64, :], qb[0:64, :], CS[0:64, 0, :], ALU.mult, [rqb, r_CS], [ra])
                    b, rb = next_xs()
                    tt(b[0:64, :], ps2[0:64, :], CS[0:64, 1, :], ALU.mult, [rps2, r_CS], [rb])
                    tt(dst, a[0:64, :], b[0:64, :], ALU.add, [ra, rb], [rdst])
                kpo, rkpo = next_sq()
                rope(ps, rps, kpo[0:64, :], rkpo, 1.0)
                S.dma("sp", kpe_d[0:64, t0:t0 + TT], kpo[0:64, :], [rkpo], [r_kv[ti]])

                for (a0, a1, sc) in ((0, 2, 1.0 / 16.0), (2, 6, 1.0)):
                    BL = BFv[:, a0:a1, 63:TT:64]
                    BM = BFv[:, a0:a1, 31:TT:64]
                    rb_ = r_BF[a0:a1]
                    S.op("dve", lambda e, a0=a0, a1=a1: e.memset(CT[:, a0:a1, 0:1], 0.0), [], [r_CH])
                    cpy(CT[:, a0:a1, 1:8], BFv[:, a0:a1, 63:TT - 64:64], rb_, [r_CH])
                    tt(DEC[:, a0:a1, :], BL, CT[:, a0:a1, :], ALU.subtract, rb_ + [r_CH], [r_CH])
                    act(DEC[:, a0:a1, :], DEC[:, a0:a1, :], AF.Exp, [r_CH], [r_CH], scale=sc)
                    tt(WG[:, a0:a1, :], BL, BM, ALU.subtract, rb_, [r_CH])
                    act(WG[:, a0:a1, :], WG[:, a0:a1, :], AF.Exp, [r_CH], [r_CH], scale=sc)
                    tt(SI[:, a0:a1, :], BM, CT[:, a0:a1, :], ALU.subtract, rb_ + [r_CH], [r_CH])
                    act(SI[:, a0:a1, :], SI[:, a0:a1, :], AF.Exp, [r_CH], [r_CH], scale=sc)
                for a in range(6):
                    sc = 1.0 / 16.0 if a < 2 else 1.0
                    i = nxt("et", 3)
                    arg, rarg = ET[:, i, :], r_ET[i]
                    tt(arg.rearrange("p (c t) -> p c t", c=8), BFv[:, a, :].rearrange("p (c t) -> p c t", c=8),
                       BFv[:, a, 31:TT:64].unsqueeze(2).broadcast_to([128, 8, 64]), ALU.subtract, [r_BF[a]], [rarg])
                    i = nxt("et", 3)
                    e1, re1 = ET[:, i, :], r_ET[i]
                    act(e1, arg, AF.Exp, [rarg], [re1], scale=sc)
                    tt(QT[:, a, :], QF[:, a, :], e1, ALU.mult, [r_QF[a], re1], [r_QT[a]])
                    i = nxt("et", 3)
                    e2, re2 = ET[:, i, :], r_ET[i]
                    act(e2, arg, AF.Exp, [rarg], [re2], scale=-sc)
                    tt(KT[:, a, :], KF[:, a, :], e2, ALU.mult, [r_KF[a], re2], [r_KT[a]])
                    tt(K2[:, a, :].rearrange("p (c t) -> p c t", c=8), KT[:, a, :].rearrange("p (c t) -> p c t", c=8),
                       WG[:, a, :].unsqueeze(2).broadcast_to([128, 8, 64]), ALU.mult, [r_KT[a], r_CH], [r_K2[a]])
                    for sub in range(4):
                        S.op("pe", lambda e, a=a, sub=sub: e.transpose(
                            PTR[:, sub, :], K2[:, a, sub * 128:(sub + 1) * 128], IDENT[:, :]),
                            [r_K2[a], r_const], [r_PTR[0]])
                    cpy(KTT[:, a, :, :], PTR[:, 0:4, :], [r_PTR[0]], r_KTT[a], eng="act")

                def rot3():
                    k = nxt("pa3", 3)
                    return ((PA[4], r_PA[4]), (PA[5], r_PA[5]), (PSTAT, r_PSTAT))[k]

                for sub in range(4):
                    tcs = slice(sub * 128, (sub + 1) * 128)
                    for wave in range(2):
                        hinfo = []
                        if wave == 0:
                            for a in range(2):
                                for k in range(2):
                                    hh = 2 * a + k
                                    hinfo.append((a, hh, 64 * k, 64, VG, r_VG, slice(hh * 128, (hh + 1) * 128), hh))
                            alist = [0, 1]
                        else:
                            for hh in range(4):
                                hinfo.append((2 + hh, 4 + hh, 0, 128, VH, r_VH, slice(hh * 128, (hh + 1) * 128), hh))
                            alist = [2, 3, 4, 5]
                        for (a, oh, p0, dk, Vt, rV, vcol, bk) in hinfo:
                            psA, rpsA = rot3()
                            mm(psA[:, 0:128], KT[p0:p0 + dk, a, tcs], QT[p0:p0 + dk, a, tcs], True, True,
                               [r_KT[a], r_QT[a]], [rpsA])
                            tt(ATMv[:, oh, :], psA[:, 0:128], MASKB, ALU.mult, [rpsA, r_const], [r_ATM[oh]])
                        for (a, oh, p0, dk, Vt, rV, vcol, bk) in hinfo:
                            mm(PA[bk][:, 0:128], Vt[:, sub, vcol], ATMv[:, oh, :], True, False,
                               [rV[sub], r_ATM[oh]], [r_PA[bk]])
                        for half in range(2):
                            c = sub * 2 + half
                            pr = slice(half * 64, half * 64 + 64)
                            qcs = slice(c * 64, (c + 1) * 64)
                            for a in alist:
                                S.op("act", lambda e, a=a, c=c: e.activation(
                                    out=SBs[:, a, :], in_=SG[:, a, :], func=AF.Copy, scale=SI[:, a, c:c + 1]),
                                    [r_SG[a], r_CH], [r_SBs[a]])
                            for (a, oh, p0, dk, Vt, rV, vcol, bk) in hinfo:
                                mm(PA[bk][:, half * 64:half * 64 + 64], SBs[p0:p0 + dk, a, :], QT[p0:p0 + dk, a, qcs],
                                   False, half == 1, [r_SBs[a], r_QT[a]], [r_PA[bk]])
                            for a in alist:
                                psU, rpsU = rot3()
                                if a < 2:
                                    mm(psU[:, 0:256], KTT[pr, a, sub, :], VG[pr, sub, a * 256:(a + 1) * 256], True, True,
                                       [r_KTT[a][sub], r_VG[sub]], [rpsU])
                                else:
                                    mm(psU[:, 0:128], KTT[pr, a, sub, :], VH[pr, sub, (a - 2) * 128:(a - 1) * 128],
                                       True, True, [r_KTT[a][sub], r_VH[sub]], [rpsU])
                                if a < 2:
                                    stt(SG[0:64, a, :], SG[0:64, a, :], DEC[0:64, a, c:c + 1], psU[0:64, 0:128],
                                        ALU.mult, ALU.add, [r_SG[a], r_CH, rpsU], [r_SG[a]])
                                    stt(SG[64:128, a, :], SG[64:128, a, :], DEC[64:128, a, c:c + 1], psU[64:128, 128:256],
                                        ALU.mult, ALU.add, [r_SG[a], r_CH, rpsU], [r_SG[a]])
                                else:
                                    stt(SG[:, a, :], SG[:, a, :], DEC[:, a, c:c + 1], psU[:, 0:128], ALU.mult, ALU.add,
                                        [r_SG[a], r_CH, rpsU], [r_SG[a]])
                        for (a, oh, p0, dk, Vt, rV, vcol, bk) in hinfo:
                            cpy(OT[:, oh, tcs], PA[bk][:, 0:128], [r_PA[bk]], [r_OT[oh]], eng="act")

                S.fence(r_h + r_u, r_y)
                for oh in range(8):
                    sq, rsq = next_sq()
                    act(sq, OT[:, oh, :], AF.Square, [r_OT[oh]], [rsq])
                    mm(PSTAT[:, :], ONESB[:, :], sq, True, True, [rsq, r_const], [r_PSTAT])
                    rstd2, r_rstd2 = rstd_from_psum(PSTAT, r_PSTAT, 128.0)
                    gcol = pb + (80 if oh < 4 else 81)
                    t1, rt1 = next_xs()
                    stt(t1, OT[:, oh, :], PV[:, gcol:gcol + 1], rstd2, ALU.mult, ALU.mult,
                        [r_OT[oh], r_PV, r_rstd2], [rt1])
                    tt(HY[:, oh, :], t1, GS[:, oh, :], ALU.mult, [rt1, r_GS[oh]], [r_y[oh]])

                S.fence(RB2 + RB5 + [r_XB], RB3)
                wb, rwb = next_wb()
                wq = wb[:, 0:6144].rearrange("p (k n) -> p k n", k=4)
                load_w(wq, rwb, wqv[:, :, :], wb)
                qscale = 192.0 ** -0.5
                S.op("dve", lambda e: e.memset(QPE[64:128, :, :], 0.0), [], r_QPE)
                for h in range(8):
                    ps, rps = next_pa()
                    for kc in range(4):
                        mm(ps[:, :], wq[:, kc, h * 192:h * 192 + 128], QN[:, kc, :], kc == 0, kc == 3,
                           [rwb, r_QN[kc]], [rps])
                    act(QNOPE[:, h, :], ps[:, :], AF.Copy, [rps], [r_QNOPE[h]], scale=qscale)
                    ps, rps = next_pa()
                    for kc in range(4):
                        mm(ps[0:64, :], wq[:, kc, h * 192 + 128:h * 192 + 192], QN[:, kc, :], kc == 0, kc == 3,
                           [rwb, r_QN[kc]], [rps])
                    rope(ps, rps, QPE[0:64, h, :], r_QPE[h], qscale)
                S.fence_merge(r_KS[2:4] + r_KPS[2:4] + r_VS[2:4], r_KTO + r_VTO)
                wkvs = []
                for half in range(2):
                    wb, rwb = next_wb()
                    wk = wb[:, 0:4096].rearrange("p (k n) -> p k n", k=4)
                    load_w(wk, rwb, wkvv[:, :, half * 1024:(half + 1) * 1024], wb)
                    wkvs.append((wk, rwb))
                for h in range(8):
                    wk, rwb = wkvs[h // 4]
                    hc = (h % 4) * 256
                    ps, rps = next_pa()
                    for kc in range(4):
                        mm(ps[:, :], wk[:, kc, hc:hc + 128], CN[:, kc, :], kc == 0, kc == 3, [rwb, r_CN[kc]], [rps])
                    i = h % 2
                    cpy(KTO[:, i, :], ps[:, :], [rps], [r_KTO[i]], eng="act")
                    S.dma("sp", kn_d[h, :, t0:t0 + TT], KTO[:, i, :], [r_KTO[i]], [r_kv[ti]])
                for sub in range(4):
                    i = sub % 2
                    for half in range(2):
                        wk, rwb = wkvs[half]
                        ps, rps = next_pa()
                        rhsv = wk.rearrange("p k (h c) -> p k h c", c=256)
                        for kc in range(4):
                            mm(ps[:, :].rearrange("p (h e) -> p h e", h=4), CN[:, kc, sub * 128:(sub + 1) * 128],
                               rhsv[:, kc, :, 128:256], kc == 0, kc == 3, [rwb, r_CN[kc]], [rps])
                        cpy(VTO[:, i, half * 4:(half + 1) * 4, :], ps[:, :].rearrange("p (h e) -> p h e", h=4),
                            [rps], [r_VTO[i]], eng="act")
                    S.dma("sp", vd_d[:, ti, :, sub, :].rearrange("h p e -> p h e"), VTO[:, i, :, :],
                          [r_VTO[i]], [r_kv[ti]])

                LOOK = int(os.environ.get('K_LOOK', '2'))
                S.fence_merge(r_KTO + r_VTO, r_KS[2:4] + r_KPS[2:4] + r_VS[2:4])
                for h in range(8):
                    psO, rpsO = PA[4], r_PA[4]
                    psD, rpsD = PA[5], r_PA[5]
                    blocks = [(kt, kb) for kt in range(ti + 1) for kb in range(4)]
                    loaded = {}

                    def ensure_loaded(kt, h=h, loaded=loaded):
                        if kt not in loaded:
                            i = nxt("ks", 4)
                            loaded[kt] = i
                            S.dma("sp", KSL[i], kn_d[h, :, kt * TT:(kt + 1) * TT], [r_kv[kt]], [r_KS[i]])
                            S.dma("sp", KPSL[i], kpe_d[:, kt * TT:(kt + 1) * TT], [r_kv[kt]], [r_KPS[i]])
                            S.dma("sp", VSL[i], vd_d[h, kt], [r_kv[kt]], [r_VS[i]])
                        return loaded[kt]

                    def emit_qk(bi, h=h):
                        kt, kb = blocks[bi]
                        i = ensure_loaded(kt)
                        q0 = kb * 128 if kt == ti else 0
                        qs = slice(q0, TT)
                        k4 = nxt("pa4", 4)
                        psS, rpsS = PA[k4], r_PA[k4]
                        mm(psS[:, qs], KSL[i][:, kb * 128:(kb + 1) * 128], QNOPE[:, h, qs], True, False,
                           [r_KS[i], r_QNOPE[h]], [rpsS])
                        mm(psS[:, qs], KPSL[i][:, kb * 128:(kb + 1) * 128], QPE[:, h, qs], False, True,
                           [r_KPS[i], r_QPE[h]], [rpsS])
                        j = nxt("pt", 4)
                        act(PT[:, j, qs], psS[:, qs], AF.Exp, [rpsS], [r_PT[j]])
                        if kt == ti:
                            tt(PT[:, j, q0:q0 + 128], PT[:, j, q0:q0 + 128], MASKC[:, :], ALU.mult,
                               [r_PT[j], r_const], [r_PT[j]])
                        return (i, j, qs, kb)

                    def emit_pv(bi, st):
                        i, j, qs, kb = st
                        first = bi == 0
                        last = bi == len(blocks) - 1
                        mm(psO[:, qs], VSL[i][:, kb, :], PT[:, j, qs], first, last, [r_VS[i], r_PT[j]], [rpsO])
                        mm(psD[:, qs], ONESB[:, :], PT[:, j, qs], first, last, [r_const, r_PT[j]], [rpsD])

                    pend = []
                    for bi in range(len(blocks) + LOOK):
                        if bi < len(blocks):
                            pend.append(emit_qk(bi))
                        if bi >= LOOK:
                            emit_pv(bi - LOOK, pend[bi - LOOK])
                    rd, rrd = next_rs()
                    S.op("dve", lambda e, rd=rd, psD=psD: e.reciprocal(out=rd, in_=psD[:, :]), [rpsD], [rrd])
                    tt(OMLA[:, h, :], psO[:, :], rd, ALU.mult, [rpsO, rrd], [r_OMLA[h]])
                for h in range(8):
                    sq, rsq = next_sq()
                    act(sq, OMLA[:, h, :], AF.Square, [r_OMLA[h]], [rsq])
                    mm(PSTAT[:, :], ONESB[:, :], sq, h == 0, h == 7, [rsq, r_const], [r_PSTAT])
                rstd3, r_rstd3 = rstd_from_psum(PSTAT, r_PSTAT, 1024.0)
                for h in range(8):
                    stt(HY[:, 8 + h, :], OMLA[:, h, :], PV[:, pb + 72 + h:pb + 73 + h], rstd3, ALU.mult, ALU.mult,
                        [r_OMLA[h], r_PV, r_rstd3], [r_y[8 + h]])

                S.fence(RA1 + RA5, RA4)
                S.fence(RB2 + RB3 + RB5, [r_XB])
                for q4 in range(4):
                    S.dma("sp", XB[:, 4 * q4:4 * q4 + 4, :],
                          src_d[q4 * 512:(q4 + 1) * 512, t0:t0 + TT].rearrange("(g p) t -> p g t", p=128),
                          [rsrc1], [r_XB])
                for g in range(16):
                    if g % 3 == 0:
                        gn = min(3, 16 - g)
                        wb, rwb = next_wb()
                        wv = wb[:, 0:16 * gn * 128].rearrange("p (k n) -> p k n", k=16)
                        load_w(wv, rwb, wov[:, :, g * 128:(g + gn) * 128], wb)
                        gbase = g
                    ps, rps = next_pa()
                    cc = g - gbase
                    for kc in range(16):
                        mm(ps[:, :], wv[:, kc, cc * 128:(cc + 1) * 128], HY[:, kc, :], kc == 0, kc == 15,
                           [rwb, r_y[kc]], [rps])
                    if g > 0:
                        mm(PSTAT[:, :], ONESB[:, :], pend_sq[0], g - 1 == 0, False, [pend_sq[1], r_const], [r_PSTAT])
                    cpy(MT[:, g, :], ps[:, :], [rps], [r_MT[g]], eng="act")
                    sq, rsq = next_sq()
                    act(sq, ps[:, :], AF.Square, [rps], [rsq])
                    pend_sq = (sq, rsq)
                mm(PSTAT[:, :], ONESB[:, :], pend_sq[0], False, True, [pend_sq[1], r_const], [r_PSTAT])
                rstd4, r_rstd4 = rstd_from_psum(PSTAT, r_PSTAT, float(D))
                for g in range(16):
                    stt(MT[:, g, :], MT[:, g, :], PV[:, pb + 16 + g:pb + 17 + g], rstd4, ALU.mult, ALU.mult,
                        [r_MT[g], r_PV, r_rstd4], [r_MT[g]])
                    tt(MT[:, g, :], MT[:, g, :], XB[:, g, :], ALU.add, [r_MT[g], r_XB], [r_MT[g]])
                    S.dma("sp", xmid_d[g * 128:(g + 1) * 128, t0:t0 + TT], MT[:, g, :], [r_MT[g]], [r_xmid[ti]])

                S.fence(r_y + r_h, r_u)
                for g in range(16):
                    sq, rsq = next_sq()
                    act(sq, MT[:, g, :], AF.Square, [r_MT[g]], [rsq])
                    mm(PSTAT[:, :], ONESB[:, :], sq, g == 0, g == 15, [rsq, r_const], [r_PSTAT])
                rstd5, r_rstd5 = rstd_from_psum(PSTAT, r_PSTAT, float(D))
                for g in range(16):
                    stt(HY[:, g, :], MT[:, g, :], PV[:, pb + 32 + g:pb + 33 + g], rstd5, ALU.mult, ALU.mult,
                        [r_MT[g], r_PV, r_rstd5], [r_u[g]])
                S.fence(RA1 + RA4, RA5)
                S.fence(RB2 + RB3 + [r_XB], RB5)
                for c0 in range(0, 44, 3):
                    gn = min(3, 44 - c0)
                    wbg, rwbg = next_wb()
                    wvg = wbg[:, 0:16 * gn * 128].rearrange("p (k n) -> p k n", k=16)
                    load_w(wvg, rwbg, wgv[:, :, c0 * 128:(c0 + gn) * 128], wbg)
                    wbu, rwbu = next_wb()
                    wvu = wbu[:, 0:16 * gn * 128].rearrange("p (k n) -> p k n", k=16)
                    load_w(wvu, rwbu, wuv[:, :, c0 * 128:(c0 + gn) * 128], wbu)
                    for cc in range(gn):
                        c = c0 + cc
                        psg, rpsg = next_pa()
                        for kc in range(16):
                            mm(psg[:, :], wvg[:, kc, cc * 128:(cc + 1) * 128], HY[:, kc, :], kc == 0, kc == 15,
                               [rwbg, r_u[kc]], [rpsg])
                        psu, rpsu = next_pa()
                        for kc in range(16):
                            mm(psu[:, :], wvu[:, kc, cc * 128:(cc + 1) * 128], HY[:, kc, :], kc == 0, kc == 15,
                               [rwbu, r_u[kc]], [rpsu])
                        sg, rsg = next_rs()
                        act(sg, psg[:, :], AF.Silu, [rpsg], [rsg])
                        tt(HID[:, c, :], sg, psu[:, :], ALU.mult, [rsg, rpsu], [r_HID[c]])
                nl, nti = (l, ti + 1) if ti + 1 < NT else (l + 1, 0)
                hoist = nl < L and not (NT == 1) and os.environ.get('K_HOIST', '1') == '1'
                if hoist:
                    prologue(nl, nti)
                for g in range(16):
                    wb, rwb = next_wb()
                    wv = wb[:, 0:44 * 128].rearrange("p (k n) -> p k n", k=44)
                    load_w(wv, rwb, wdv[:, :, g * 128:(g + 1) * 128], wb)
                    ps, rps = next_pa()
                    for kc in range(44):
                        mm(ps[:, :], wv[:, kc, :], HID[:, kc, :], kc == 0, kc == 43, [rwb, r_HID[kc]], [rps])
                    if g > 0:
                        mm(PSTAT[:, :], ONESB[:, :], pend_sq[0], g - 1 == 0, False, [pend_sq[1], r_const], [r_PSTAT])
                    cpy(FT[:, g, :], ps[:, :], [rps], [r_FT[g]], eng="act")
                    sq, rsq = next_sq()
                    act(sq, ps[:, :], AF.Square, [rps], [rsq])
                    pend_sq = (sq, rsq)
                mm(PSTAT[:, :], ONESB[:, :], pend_sq[0], False, True, [pend_sq[1], r_const], [r_PSTAT])
                rstd6, r_rstd6 = rstd_from_psum(PSTAT, r_PSTAT, float(D))
                rdst = [r_xres[ti]] if l < L - 1 else [Res("yout")]
                def ld_xmid(g):
                    S.dma("sp", XS[:, g % 4, :], xmid_d[g * 128:(g + 1) * 128, t0:t0 + TT], [r_xmid[ti]], [r_XS[g % 4]])
                for g in range(4):
                    ld_xmid(g)
                for g in range(16):
                    stt(FT[:, g, :], FT[:, g, :], PV[:, pb + 48 + g:pb + 49 + g], rstd6, ALU.mult, ALU.mult,
                        [r_FT[g], r_PV, r_rstd6], [r_FT[g]])
                    tt(FT[:, g, :], FT[:, g, :], XS[:, g % 4, :], ALU.add, [r_FT[g], r_XS[g % 4]], [r_FT[g]])
                    if g + 4 < 16:
                        ld_xmid(g + 4)
                    S.dma("sp", dst_d[g * 128:(g + 1) * 128, t0:t0 + TT], FT[:, g, :], [r_FT[g]], rdst)
                if nl < L and not hoist:
                    prologue(nl, nti)

        S.drain("sp")
        build_program.stats = (S.nins, S.nwaits)
    return nc


def host_consts(L):
    c = np.zeros((128, 5 * 128), np.float32)
    c[:, 0:128] = np.eye(128, dtype=np.float32)
    j = np.arange(128)[:, None]
    i = np.arange(128)[None, :]
    c[:, 128:256] = ((j // 64 == i // 64) & (i >= j)).astype(np.float32)
    c[:, 256:384] = (i >= j).astype(np.float32)
    jj = np.arange(64)[:, None]
    ii = np.arange(64)[None, :]
    c[0:64, 384:448] = (jj == (ii + 32) % 64).astype(np.float32)
    c[:, 512] = EPS
    c[:, 513] = 1.0
    return c


def host_pvec(L, p):
    pv = np.zeros((128, L * NPL + 2), np.float32)

    def cols(v):
        return np.ascontiguousarray(np.asarray(v, np.float32).reshape(-1, 128).T)
    for l in range(L):
        b = l * NPL
        pv[:, b + 0:b + 16] = cols(p["attn_pre_norm"][l])
        pv[:, b + 16:b + 32] = cols(p["attn_post_norm"][l])
        pv[:, b + 32:b + 48] = cols(p["ffn_pre_norm"][l])
        pv[:, b + 48:b + 64] = cols(p["ffn_post_norm"][l])
        pv[:, b + 64:b + 68] = cols(p["mla_q_norm"][l])
        pv[:, b + 68:b + 72] = cols(p["mla_kv_norm"][l])
        pv[:, b + 72:b + 80] = cols(p["mla_out_norm"][l])
        pv[:, b + 80:b + 81] = cols(p["gla_out_norm"][l])
        pv[:, b + 81:b + 82] = cols(p["hgrn_out_norm"][l])
        pv[:, b + 82:b + 84] = cols(p["gla_gate_b"][l])
        pv[:, b + 84:b + 88] = cols(p["hgrn_lb_logits"][l])
    inv_freq = (10000.0 ** (-np.arange(0, 64, 2, dtype=np.float32) / 64.0)).astype(np.float32)
    pv[0:64, L * NPL] = np.concatenate([inv_freq, inv_freq])
    pv[0:32, L * NPL + 1] = -1.0
    pv[32:64, L * NPL + 1] = 1.0
    return pv


_PROG_CACHE = {}


def run(inputs, T, L, B):
    key = (T, L)
    if key not in _PROG_CACHE:
        _PROG_CACHE[key] = build_program(T, L)
    nc = _PROG_CACHE[key]
    x = np.asarray(inputs["x"], np.float32)
    pos = np.asarray(inputs["positions"], np.int32)
    pv = host_pvec(L, inputs)
    cs = host_consts(L)
    shared = {
        "pvec": pv, "consts": cs,
        "w_in": np.ascontiguousarray(np.asarray(inputs["w_in"], np.float32)),
        "gla_gate_w2": np.ascontiguousarray(np.asarray(inputs["gla_gate_w2"], np.float32)),
        "mla_wq_b": np.ascontiguousarray(np.asarray(inputs["mla_wq_b"], np.float32)),
        "mla_wkv_b": np.ascontiguousarray(np.asarray(inputs["mla_wkv_b"], np.float32)),
        "w_out": np.ascontiguousarray(np.asarray(inputs["w_out"], np.float32)),
        "w_gate": np.ascontiguousarray(np.asarray(inputs["w_gate"], np.float32)),
        "w_up": np.ascontiguousarray(np.asarray(inputs["w_up"], np.float32)),
        "w_down": np.ascontiguousarray(np.asarray(inputs["w_down"], np.float32)),
    }
    work = {0: 0, 4: 1} if B == 2 else {c: c for c in range(B)}
    zeros = {k: np.zeros_like(v) for k, v in shared.items()}
    zx = np.zeros((D, T), np.float32)
    zp = np.zeros((1, T), np.int32)
    in_maps = []
    for c in range(NCORES):
        if c in work:
            b = work[c]
            m = dict(shared)
            m["xT"] = np.ascontiguousarray(x[b].T)
            m["pos"] = np.ascontiguousarray(pos[b].reshape(1, T))
        else:
            m = dict(zeros)
            m["xT"] = zx
            m["pos"] = zp
        in_maps.append(m)
    res = run_bass_kernel_spmd(nc, in_maps, core_ids=list(range(NCORES)))
    inv = {b: c for c, b in work.items()}
    out = np.stack([np.ascontiguousarray(res.results[inv[b]]["yT"].T) for b in range(B)], axis=0)
    return out.astype(np.float32)


def kernel(**inputs):
    x = inputs["x"]
    B, T, _ = x.shape
    L = inputs["w_in"].shape[0]
    return run(inputs, T, L, B)
```
